# Optimizing a Trainium2 kernel written in Bass

```python
import jax, jax.numpy as jnp
from jax import lax
import numpy as np

D_MODEL = 2048
BATCH = 4
SEQ = 2048
DEPTH = 4
DEC_BATCH = 8
DEC_SEQ = 4
PAST_LEN = 16384
PAGE_SIZE = 128

N_MIXERS = 2
N_LAYERS_A = (DEPTH + 1) // 2
N_LAYERS_B = DEPTH // 2
DIL_PATTERNS = ((128, 1), (512, 4), (2048, 16))
N_GROUPS_A = len(DIL_PATTERNS)
HEADS_A = 8
HEAD_DIM_A = 128
BLOCK_Q = 128
ROPE_THETA = 10000.0
HEADS_B = 8
DQK_B = D_MODEL // 2 // HEADS_B
DV_B = D_MODEL // HEADS_B
CHUNK_B = 64
GATE_CAP = 15.0
D_FF = 5632
CONV_W = 3
EPS = 1e-6

kernel_name = "dilated_attn_mlstm_convffn_step"


def rms_norm(x, g):
    xf = x.astype(jnp.float32)
    y = xf * lax.rsqrt(jnp.mean(xf * xf, axis=-1, keepdims=True) + EPS)
    return (y * g.astype(jnp.float32)).astype(x.dtype)


def rotary(x, pos):
    half = HEAD_DIM_A // 2
    inv_freq = ROPE_THETA ** (-jnp.arange(half, dtype=jnp.float32) / half)
    ang = pos.astype(jnp.float32)[:, None] * inv_freq[None, :]
    shape = (pos.shape[0],) + (1,) * (x.ndim - 3) + (half,)
    cos, sin = jnp.cos(ang).reshape(shape), jnp.sin(ang).reshape(shape)
    xf = x.astype(jnp.float32)
    x1, x2 = xf[..., :half], xf[..., half:]
    return jnp.concatenate([x1 * cos - x2 * sin, x2 * cos + x1 * sin], axis=-1).astype(x.dtype)


def attn_project(h, w_qkv, q_gain, k_gain, pos):
    B, S, _ = h.shape
    qkv = jnp.einsum('bsd,de->bse', h, w_qkv).reshape(B, S, N_GROUPS_A, 3, HEADS_A, HEAD_DIM_A)
    q = rotary(rms_norm(qkv[:, :, :, 0], q_gain), pos)
    k = rotary(rms_norm(qkv[:, :, :, 1], k_gain), pos)
    return q, k, qkv[:, :, :, 2]


def dilated_attn_prompt(q, k, v, dil, band):
    B, S, H, HD = q.shape
    L = S // dil
    n_blk = -(-L // BLOCK_Q)
    n_prev = -(-band // BLOCK_Q)
    Lp = n_blk * BLOCK_Q

    def classes(x):
        return x.reshape(B, L, dil, H, HD).transpose(0, 2, 1, 3, 4)

    qb = jnp.pad(classes(q), ((0, 0), (0, 0), (0, Lp - L), (0, 0), (0, 0))).reshape(B, dil, n_blk, BLOCK_Q, H, HD)

    def key_blocks(x):
        xp = jnp.pad(classes(x), ((0, 0), (0, 0), (n_prev * BLOCK_Q, Lp - L), (0, 0), (0, 0)))
        xp = xp.reshape(B, dil, n_blk + n_prev, BLOCK_Q, H, HD)
        return jnp.concatenate([xp[:, :, j:j + n_blk] for j in range(n_prev + 1)], axis=3)

    kb, vb = key_blocks(k), key_blocks(v)
    u_q = np.arange(Lp).reshape(n_blk, BLOCK_Q)
    u_k = (np.arange(n_blk)[:, None] - n_prev) * BLOCK_Q + np.arange((n_prev + 1) * BLOCK_Q)[None, :]
    delta = u_q[:, :, None] - u_k[:, None, :]
    mask = (delta >= 0) & (delta <= band) & (u_k[:, None, :] >= 0)
    s = jnp.einsum('brnqhd,brnkhd->brnhqk', qb, kb, preferred_element_type=jnp.float32) * (HEAD_DIM_A ** -0.5)
    s = jnp.where(mask[:, None], s, -jnp.inf)
    m = jnp.max(s, axis=-1, keepdims=True)
    p = jnp.exp(s - m)
    den = jnp.sum(p, axis=-1)
    o = jnp.einsum('brnhqk,brnkhd->brnqhd', p, vb.astype(jnp.float32)) / jnp.swapaxes(den, 3, 4)[..., None]
    lse = jnp.swapaxes(m[..., 0] + jnp.log(den), 3, 4)

    def positions(x):
        x = x.reshape((B, dil, Lp) + x.shape[4:])[:, :, :L]
        x = jnp.swapaxes(x, 1, 2)
        return x.reshape((B, S) + x.shape[3:])

    return positions(o), positions(lse)


def dilated_attn_sample(q, k_all, v_all, dil, band):
    T = q.shape[1]
    n_buf = k_all.shape[1] - T
    idx = n_buf + np.arange(T)[:, None] - dil * np.arange(band + 1)[None, :]
    valid = idx >= 0
    idx = np.maximum(idx, 0)
    kg, vg = k_all[:, idx], v_all[:, idx]
    s = jnp.einsum('bthd,btjhd->bthj', q, kg, preferred_element_type=jnp.float32) * (HEAD_DIM_A ** -0.5)
    s = jnp.where(valid[:, None, :], s, -jnp.inf)
    m = jnp.max(s, axis=-1, keepdims=True)
    p = jnp.exp(s - m)
    den = jnp.sum(p, axis=-1)
    o = jnp.einsum('bthj,btjhd->bthd', p, vg.astype(jnp.float32)) / den[..., None]
    return o, m[..., 0] + jnp.log(den)


def combine_groups(outs, lses, w_o, dtype):
    alpha = jax.nn.softmax(jnp.stack(lses, axis=-1), axis=-1)
    o = jnp.einsum('bshg,gbshd->bshd', alpha, jnp.stack(outs))
    B, S = o.shape[:2]
    return jnp.einsum('bse,ed->bsd', o.reshape(B, S, HEADS_A * HEAD_DIM_A).astype(dtype), w_o)


def attn_mixer_prompt(h, w_qkv, q_gain, k_gain, w_o):
    S = h.shape[1]
    q, k, v = attn_project(h, w_qkv, q_gain, k_gain, jnp.arange(S))
    outs, lses, bufs = [], [], []
    for g, (win, dil) in enumerate(DIL_PATTERNS):
        o_g, lse_g = dilated_attn_prompt(q[:, :, g], k[:, :, g], v[:, :, g], dil, win // dil)
        outs.append(o_g)
        lses.append(lse_g)
        n_keep = min(win, S)
        bufs.append(jnp.stack([k[:, S - n_keep:, g], v[:, S - n_keep:, g]], axis=2))
    return combine_groups(outs, lses, w_o, h.dtype), bufs


def attn_mixer_sample(h, bufs, w_qkv, q_gain, k_gain, w_o):
    T = h.shape[1]
    q, k, v = attn_project(h, w_qkv, q_gain, k_gain, PAST_LEN + jnp.arange(T))
    outs, lses, new_bufs = [], [], []
    for g, (win, dil) in enumerate(DIL_PATTERNS):
        n_buf = bufs[g].shape[1]
        kv_all = jnp.concatenate([bufs[g].astype(k.dtype), jnp.stack([k[:, :, g], v[:, :, g]], axis=2)], axis=1)
        o_g, lse_g = dilated_attn_sample(q[:, :, g], kv_all[:, :, 0], kv_all[:, :, 1], dil, win // dil)
        outs.append(o_g)
        lses.append(lse_g)
        new_bufs.append(kv_all[:, -n_buf:])
    return combine_groups(outs, lses, w_o, h.dtype), new_bufs


def mlstm_project(h, w_in, b_gates):
    B, S, _ = h.shape
    proj = jnp.einsum('bsd,de->bse', h, w_in)
    qk_w, v_w = HEADS_B * DQK_B, HEADS_B * DV_B
    q = proj[..., :qk_w].reshape(B, S, HEADS_B, DQK_B)
    k = proj[..., qk_w:2 * qk_w].reshape(B, S, HEADS_B, DQK_B) * (DQK_B ** -0.5)
    v = proj[..., 2 * qk_w:2 * qk_w + v_w].reshape(B, S, HEADS_B, DV_B)
    og = proj[..., 2 * qk_w + v_w:2 * qk_w + 2 * v_w]
    gates = proj[..., 2 * qk_w + 2 * v_w:].astype(jnp.float32) + b_gates.astype(jnp.float32)
    gates = GATE_CAP * jnp.tanh(gates / GATE_CAP)
    ig = jnp.swapaxes(gates[..., :HEADS_B], 1, 2)
    lf = jnp.swapaxes(jax.nn.log_sigmoid(gates[..., HEADS_B:]), 1, 2)

    def heads_first(x):
        return jnp.swapaxes(x, 1, 2).astype(jnp.float32)

    return heads_first(q), heads_first(k), heads_first(v), ig, lf, og


def mlstm_chunk(carry, inp):
    C, n, m = carry
    q, k, v, ig, lf = inp
    L = q.shape[2]
    b = jnp.cumsum(lf, axis=-1)
    causal = np.tril(np.ones((L, L), dtype=bool))
    Dm = jnp.where(causal, b[..., :, None] - b[..., None, :] + ig[..., None, :], -jnp.inf)
    inter = b + m[..., None]
    m_t = jnp.maximum(inter, jnp.max(Dm, axis=-1))
    w_inter = jnp.exp(inter - m_t)
    P = jnp.exp(Dm - m_t[..., None]) * jnp.einsum('bhtd,bhsd->bhts', q, k)
    num = w_inter[..., None] * jnp.einsum('bhed,bhtd->bhte', C, q) + jnp.einsum('bhts,bhse->bhte', P, v)
    den = w_inter * jnp.einsum('bhd,bhtd->bht', n, q) + jnp.sum(P, axis=-1)
    h = num / jnp.maximum(jnp.abs(den), jnp.exp(-m_t))[..., None]
    bL = b[..., -1]
    dec = bL[..., None] - b + ig
    m_new = jnp.maximum(bL + m, jnp.max(dec, axis=-1))
    wC = jnp.exp(bL + m - m_new)
    ws = jnp.exp(dec - m_new[..., None])
    C_new = wC[..., None, None] * C + jnp.einsum('bhs,bhse,bhsd->bhed', ws, v, k)
    n_new = wC[..., None] * n + jnp.einsum('bhs,bhsd->bhd', ws, k)
    return (C_new, n_new, m_new), h


def mlstm_output(hh, og, g_h, w_out, dtype):
    B, H, S, DV = hh.shape
    hh = jnp.swapaxes(hh, 1, 2)
    hn = hh * lax.rsqrt(jnp.mean(hh * hh, axis=-1, keepdims=True) + EPS)
    hn = hn.reshape(B, S, H * DV) * g_h.astype(jnp.float32) * jax.nn.sigmoid(og.astype(jnp.float32))
    return jnp.einsum('bse,ed->bsd', hn.astype(dtype), w_out)


def mlstm_mixer_prompt(h, w_in, b_gates, g_h, w_out):
    B, S, _ = h.shape
    q, k, v, ig, lf, og = mlstm_project(h, w_in, b_gates)
    n_chunks = S // CHUNK_B

    def chunks(x):
        return jnp.moveaxis(x.reshape(x.shape[:2] + (n_chunks, CHUNK_B) + x.shape[3:]), 2, 0)

    carry0 = (jnp.zeros((B, HEADS_B, DV_B, DQK_B), jnp.float32),
              jnp.zeros((B, HEADS_B, DQK_B), jnp.float32),
              jnp.zeros((B, HEADS_B), jnp.float32))
    (C, n, m), hs = lax.scan(mlstm_chunk, carry0, (chunks(q), chunks(k), chunks(v), chunks(ig), chunks(lf)))
    hh = jnp.moveaxis(hs, 0, 2).reshape(B, HEADS_B, S, DV_B)
    return mlstm_output(hh, og, g_h, w_out, h.dtype), C, n, m


def mlstm_mixer_sample(h, C0, n0, m0, w_in, b_gates, g_h, w_out):
    q, k, v, ig, lf, og = mlstm_project(h, w_in, b_gates)
    carry0 = (C0.astype(jnp.float32), n0.astype(jnp.float32), m0.astype(jnp.float32))
    (C, n, m), hh = mlstm_chunk(carry0, (q, k, v, ig, lf))
    return mlstm_output(hh, og, g_h, w_out, h.dtype), C, n, m


def conv_ffn(h, buf, w_up, conv_w, conv_b, w_down):
    T = h.shape[1]
    u = jnp.einsum('bsd,de->bse', h, w_up)
    ext = jnp.concatenate([buf.astype(u.dtype), u], axis=1)
    c = conv_b + conv_w[0] * ext[:, 0:T]
    for j in range(1, CONV_W):
        c = c + conv_w[j] * ext[:, j:j + T]
    a, b = jnp.split(c, 2, axis=-1)
    z = jax.nn.silu(a) * b
    return jnp.einsum('bsf,fd->bsd', z, w_down), ext[:, T:]


def setup_inputs(seed: int = 0) -> dict:
    key = jax.random.key(seed)
    ks = jax.random.split(key, 32)
    f32 = jnp.float32

    def normal(k, shape, scale=1.0):
        return scale * jax.random.normal(k, shape, f32)

    qkv_cols = N_GROUPS_A * 3 * HEADS_A * HEAD_DIM_A
    in_cols = 2 * HEADS_B * DQK_B + 2 * HEADS_B * DV_B + 2 * HEADS_B
    return {
        "x_prompt": normal(ks[0], (BATCH, SEQ, D_MODEL)),
        "x_sample": normal(ks[1], (DEC_BATCH, DEC_SEQ, D_MODEL)),
        "cache_kv_w128": normal(ks[2], (N_LAYERS_A, DEC_BATCH, min(DIL_PATTERNS[0][0], PAST_LEN), 2, HEADS_A, HEAD_DIM_A)),
        "cache_kv_w512": normal(ks[3], (N_LAYERS_A, DEC_BATCH, min(DIL_PATTERNS[1][0], PAST_LEN), 2, HEADS_A, HEAD_DIM_A)),
        "cache_kv_w2048": normal(ks[4], (N_LAYERS_A, DEC_BATCH, min(DIL_PATTERNS[2][0], PAST_LEN), 2, HEADS_A, HEAD_DIM_A)),
        "state_mlstm_C": normal(ks[5], (N_LAYERS_B, DEC_BATCH, HEADS_B, DV_B, DQK_B), 0.5),
        "state_mlstm_n": normal(ks[6], (N_LAYERS_B, DEC_BATCH, HEADS_B, DQK_B), 0.5),
        "state_mlstm_m": normal(ks[7], (N_LAYERS_B, DEC_BATCH, HEADS_B)),
        "state_ffn_conv": normal(ks[8], (DEPTH, DEC_BATCH, CONV_W - 1, 2 * D_FF)),
        "norm_mix": 1.0 + normal(ks[9], (DEPTH, D_MODEL), 0.02),
        "norm_ffn": 1.0 + normal(ks[10], (DEPTH, D_MODEL), 0.02),
        "attn_w_qkv": normal(ks[11], (N_LAYERS_A, D_MODEL, qkv_cols), D_MODEL ** -0.5),
        "attn_q_norm": 1.0 + normal(ks[12], (N_LAYERS_A, HEAD_DIM_A), 0.02),
        "attn_k_norm": 1.0 + normal(ks[13], (N_LAYERS_A, HEAD_DIM_A), 0.02),
        "attn_w_o": normal(ks[14], (N_LAYERS_A, HEADS_A * HEAD_DIM_A, D_MODEL), (HEADS_A * HEAD_DIM_A) ** -0.5),
        "mlstm_w_in": normal(ks[15], (N_LAYERS_B, D_MODEL, in_cols), D_MODEL ** -0.5),
        "mlstm_b_gates": jnp.concatenate([normal(ks[16], (N_LAYERS_B, HEADS_B), 0.1),
                                          3.0 + normal(ks[17], (N_LAYERS_B, HEADS_B), 0.5)], axis=-1),
        "mlstm_norm_h": 1.0 + normal(ks[18], (N_LAYERS_B, HEADS_B * DV_B), 0.02),
        "mlstm_w_out": normal(ks[19], (N_LAYERS_B, HEADS_B * DV_B, D_MODEL), (HEADS_B * DV_B) ** -0.5),
        "ffn_w_up": normal(ks[20], (DEPTH, D_MODEL, 2 * D_FF), D_MODEL ** -0.5),
        "ffn_conv_w": normal(ks[21], (DEPTH, CONV_W, 2 * D_FF), CONV_W ** -0.5),
        "ffn_conv_b": normal(ks[22], (DEPTH, 2 * D_FF), 0.02),
        "ffn_w_down": normal(ks[23], (DEPTH, D_FF, D_MODEL), D_FF ** -0.5),
    }


def reference(x_prompt, x_sample, cache_kv_w128, cache_kv_w512, cache_kv_w2048,
              state_mlstm_C, state_mlstm_n, state_mlstm_m, state_ffn_conv,
              norm_mix, norm_ffn, attn_w_qkv, attn_q_norm, attn_k_norm, attn_w_o,
              mlstm_w_in, mlstm_b_gates, mlstm_norm_h, mlstm_w_out,
              ffn_w_up, ffn_conv_w, ffn_conv_b, ffn_w_down):
    cache_kv = (cache_kv_w128, cache_kv_w512, cache_kv_w2048)
    xp, xs = x_prompt, x_sample
    kv_p = [[] for _ in DIL_PATTERNS]
    kv_s = [[] for _ in DIL_PATTERNS]
    C_p, n_p, m_p, C_s, n_s, m_s = [], [], [], [], [], []
    conv_p, conv_s = [], []
    for layer in range(DEPTH):
        hp = rms_norm(xp, norm_mix[layer])
        hs = rms_norm(xs, norm_mix[layer])
        if layer % N_MIXERS == 0:
            ia = layer // N_MIXERS
            yp, bufs_p = attn_mixer_prompt(hp, attn_w_qkv[ia], attn_q_norm[ia], attn_k_norm[ia], attn_w_o[ia])
            ys, bufs_s = attn_mixer_sample(hs, [c[ia] for c in cache_kv], attn_w_qkv[ia],
                                           attn_q_norm[ia], attn_k_norm[ia], attn_w_o[ia])
            for g in range(N_GROUPS_A):
                kv_p[g].append(bufs_p[g])
                kv_s[g].append(bufs_s[g])
        else:
            ib = layer // N_MIXERS
            yp, Cp, np_, mp = mlstm_mixer_prompt(hp, mlstm_w_in[ib], mlstm_b_gates[ib], mlstm_norm_h[ib], mlstm_w_out[ib])
            ys, Cs, ns, ms = mlstm_mixer_sample(hs, state_mlstm_C[ib], state_mlstm_n[ib], state_mlstm_m[ib],
                                                mlstm_w_in[ib], mlstm_b_gates[ib], mlstm_norm_h[ib], mlstm_w_out[ib])
            C_p.append(Cp); n_p.append(np_); m_p.append(mp)
            C_s.append(Cs); n_s.append(ns); m_s.append(ms)
        xp = xp + yp
        xs = xs + ys
        hp = rms_norm(xp, norm_ffn[layer])
        hs = rms_norm(xs, norm_ffn[layer])
        zero_buf = jnp.zeros((xp.shape[0], CONV_W - 1, 2 * D_FF), xp.dtype)
        yp, cp = conv_ffn(hp, zero_buf, ffn_w_up[layer], ffn_conv_w[layer], ffn_conv_b[layer], ffn_w_down[layer])
        ys, cs = conv_ffn(hs, state_ffn_conv[layer], ffn_w_up[layer], ffn_conv_w[layer], ffn_conv_b[layer], ffn_w_down[layer])
        conv_p.append(cp)
        conv_s.append(cs)
        xp = xp + yp
        xs = xs + ys
    dt = x_prompt.dtype
    kv_w128_prompt, kv_w512_prompt, kv_w2048_prompt = [jnp.stack(b) for b in kv_p]
    kv_w128_sample, kv_w512_sample, kv_w2048_sample = [jnp.stack(b) for b in kv_s]
    mlstm_C_prompt = jnp.stack(C_p).astype(dt)
    mlstm_n_prompt = jnp.stack(n_p).astype(dt)
    mlstm_m_prompt = jnp.stack(m_p).astype(dt)
    mlstm_C_sample = jnp.stack(C_s).astype(dt)
    mlstm_n_sample = jnp.stack(n_s).astype(dt)
    mlstm_m_sample = jnp.stack(m_s).astype(dt)
    ffn_conv_prompt = jnp.stack(conv_p)
    ffn_conv_sample = jnp.stack(conv_s)
    return (xp, xs, kv_w128_prompt, kv_w128_sample, kv_w512_prompt, kv_w512_sample,
            kv_w2048_prompt, kv_w2048_sample, mlstm_C_prompt, mlstm_C_sample,
            mlstm_n_prompt, mlstm_n_sample, mlstm_m_prompt, mlstm_m_sample,
            ffn_conv_prompt, ffn_conv_sample)
```

```python
import numpy as np
import ml_dtypes
from contextlib import ExitStack
import concourse.bass as bass
import concourse.mybir as mybir
from concourse.bass_utils import run_bass_kernel_spmd

F32 = mybir.dt.float32
BF16 = mybir.dt.bfloat16
AF = mybir.ActivationFunctionType
ALU = mybir.AluOpType
AX = mybir.AxisListType

ENGS = ("pe", "act", "dve", "pool", "sp")
NSLOT = 12


class Sched:
    def __init__(self, nc):
        self.nc = nc
        self.ops = []

    def op(self, eng, fn, r=(), w=(), dma=False, drain=False):
        self.ops.append((eng, fn, tuple(r), tuple(w), dma, drain))

    def pe(self, fn, r=(), w=()):
        self.op("pe", fn, r, w)

    def act(self, fn, r=(), w=()):
        self.op("act", fn, r, w)

    def dve(self, fn, r=(), w=()):
        self.op("dve", fn, r, w)

    def pool(self, fn, r=(), w=()):
        self.op("pool", fn, r, w)

    def dma(self, eng, fn, r=(), w=()):
        self.op(eng, fn, r, w, True)

    def emit(self, stack):
        nc = self.nc
        ops = self.ops
        n = len(ops)
        pos = [0] * n
        streams = {e: [] for e in ENGS}
        for i, o in enumerate(ops):
            pos[i] = len(streams[o[0]])
            streams[o[0]].append(i)
        last_w = {}
        readers = {}
        waits = [[] for _ in range(n)]
        signal = [False] * n
        waited = {e: {} for e in ENGS}
        dwaited = {e: set() for e in ENGS}
        dslot = {}
        dval = {}
        dpre = {}
        dcount = {e: 0 for e in ENGS}
        duses = {e: [0] * NSLOT for e in ENGS}
        drainvals = {}
        for i, o in enumerate(ops):
            eng, fn, R, W, dma, drain = o
            drainvals[i] = list(duses[eng]) if drain else None
            deps = set()
            for b in R:
                if b in last_w:
                    deps.add(last_w[b])
            for b in W:
                if b in last_w:
                    deps.add(last_w[b])
                for r_ in readers.get(b, ()):
                    deps.add(r_)
            deps.discard(i)
            for j in sorted(deps):
                oj = ops[j]
                if oj[4]:
                    if j in dwaited[eng]:
                        continue
                    dwaited[eng].add(j)
                    waits[i].append(j)
                else:
                    k = oj[0]
                    if k == eng and eng == "pe":
                        continue
                    if waited[eng].get(k, -1) >= pos[j]:
                        continue
                    waited[eng][k] = pos[j]
                    waits[i].append(j)
                    signal[j] = True
            for b in R:
                readers.setdefault(b, []).append(i)
            for b in W:
                last_w[b] = i
                readers[b] = []
            if dma:
                s = dcount[eng] % NSLOT
                dcount[eng] += 1
                dslot[i] = (eng, s)
                dpre[i] = 16 * duses[eng][s]
                duses[eng][s] += 1
                dval[i] = 16 * duses[eng][s]
                if dpre[i] > 0:
                    pass
        rank = {}
        for e in ENGS:
            c = 0
            for i in streams[e]:
                if signal[i]:
                    c += 1
                    rank[i] = c
        esem = {e: stack.enter_context(nc.semaphore("es_" + e)) for e in ENGS}
        dsem = {}
        for e in ENGS:
            for s in range(min(NSLOT, dcount[e])):
                dsem[(e, s)] = stack.enter_context(nc.semaphore("ds_%s_%d" % (e, s)))
        self.n_wait = sum(len(w) for w in waits)
        block = stack.enter_context(nc.Block())

        def run_stream(engname, engobj):
            for i in streams[engname]:
                eng, fn, R, W, dma, drain = ops[i]
                if drain:
                    for s_ in range(NSLOT):
                        if drainvals[i][s_] > 0:
                            engobj.wait_ge(dsem[(engname, s_)], 16 * drainvals[i][s_])
                for j in waits[i]:
                    if ops[j][4]:
                        engobj.wait_ge(dsem[dslot[j]], dval[j])
                    else:
                        engobj.wait_ge(esem[ops[j][0]], rank[j])
                if dma:
                    if dpre[i] > 0:
                        engobj.wait_ge(dsem[dslot[i]], dpre[i])
                    ins = fn(engobj)
                    ins.then_inc(dsem[dslot[i]], 16)
                else:
                    ins = fn(engobj)
                    if signal[i]:
                        ins.then_inc(esem[eng], 1)
            for s in range(NSLOT):
                if duses[engname][s] > 0:
                    engobj.wait_ge(dsem[(engname, s)], 16 * duses[engname][s])

        @block.tensor
        def _(e):
            run_stream("pe", e)

        @block.scalar
        def _(e):
            run_stream("act", e)

        @block.vector
        def _(e):
            run_stream("dve", e)

        @block.gpsimd
        def _(e):
            run_stream("pool", e)

        @block.sync
        def _(e):
            run_stream("sp", e)


DM = 2048
KC = 16
FF = 5632
FC = 44
NL = 4
NLA = 2
NLB = 2
TS = 4
EPS = 1e-6
GRP = 8
NWB = 2
PAST_LEN = 16384
DILS = (1, 4, 16)
NBUF = (128, 512, 2048)
SCALE_A = 128 ** -0.5
OVF = 10240
OVB = 28672
NCM = 256 + 128 + 4 + 4 + 4 + 16


class Prog:
    def __init__(self, T=2048, plan=None, dbg=None):
        self.T = T
        self.TA = T + TS
        self.tiles = [(i * 512, 512) for i in range(T // 512)] + [(T, TS)]
        self.ntile = [(i * 128, 128) for i in range(T // 128)] + [(T, TS)]
        self.plan = plan if plan is not None else [("attn", 0), ("ffn", 0), ("mlstm", 0), ("ffn", 1),
                                                   ("attn", 1), ("ffn", 2), ("mlstm", 1), ("ffn", 3)]
        self.dbg = dbg or {}
        self.nc = bass.Bass("TRN2", target_bir_lowering=False)
        self.ins = {}
        self.outs = {}

    def din(self, name, shape, dt=F32):
        t = self.nc.dram_tensor(name, list(shape), dt, kind="ExternalInput").ap()
        self.ins[name] = t
        return t

    def dout(self, name, shape, dt=F32):
        t = self.nc.dram_tensor(name, list(shape), dt, kind="ExternalOutput").ap()
        self.outs[name] = t
        return t

    def sb(self, name, shape, dt):
        return self.st.enter_context(self.nc.sbuf_tensor(name, list(shape), dt))

    def ov_reset(self):
        self.barrier()
        self.ovf_off = 0
        self.ovb_off = 0
        self.ov_gen += 1
        self.ov_cnt = 0

    def af(self, n):
        v = self.ovf[:, self.ovf_off:self.ovf_off + n]
        self.ovf_off += n
        assert self.ovf_off <= OVF, self.ovf_off
        self.ov_cnt += 1
        return v, ("o", self.ov_gen, self.ov_cnt)

    def ab(self, n):
        n = (n + 1) // 2 * 2
        v = self.ovb[:, self.ovb_off:self.ovb_off + n]
        self.ovb_off += n
        assert self.ovb_off <= OVB, self.ovb_off
        self.ov_cnt += 1
        return v, ("o", self.ov_gen, self.ov_cnt)

    def barrier(self):
        S = self.S
        self.bar_id += 1
        b = self.bar_id
        sc = self.bscr
        S.op("pe", lambda e: e.matmul(self.ps[7][0:1, 0:1], lhsT=self.ones1[0:1, 0:1], rhs=self.ones1[0:1, 0:1], start=True, stop=True),
             r=["ones1"], w=[("bar", b, "pe"), ("ps", 7)])
        S.op("act", lambda e: e.copy(out=sc[0:1, 0:1], in_=self.epsc[0:1, 0:1]), r=["epsc"], w=[("bar", b, "act"), ("bscr", 0)])
        S.op("dve", lambda e: e.memset(sc[0:1, 1:2], 0.0), w=[("bar", b, "dve"), ("bscr", 1)])
        S.op("pool", lambda e: e.memset(sc[0:1, 2:3], 0.0), w=[("bar", b, "pool"), ("bscr", 2)], drain=True)
        S.op("sp", lambda e: e.dma_start(out=sc[0:1, 4:5], in_=self.epsc[0:1, 0:1]), r=["epsc"], w=[("bar", b, "sp"), ("bscr", 4)], dma=True, drain=True)
        allk = [("bar", b, e_) for e_ in ENGS]
        S.op("pe", lambda e: e.matmul(self.ps[7][0:1, 0:1], lhsT=self.ones1[0:1, 0:1], rhs=self.ones1[0:1, 0:1], start=True, stop=True),
             r=allk + ["ones1"], w=[("ps", 7)])
        S.op("act", lambda e: e.copy(out=sc[0:1, 0:1], in_=self.epsc[0:1, 0:1]), r=allk + ["epsc"], w=[("bscr", 0)])
        S.op("dve", lambda e: e.memset(sc[0:1, 1:2], 0.0), r=allk, w=[("bscr", 1)])
        S.op("pool", lambda e: e.memset(sc[0:1, 2:3], 0.0), r=allk, w=[("bscr", 2)])
        S.op("sp", lambda e: e.dma_start(out=sc[0:1, 5:6], in_=self.epsc[0:1, 0:1]), r=allk + ["epsc"], w=[("bscr", 5)], dma=True)

    def build(self):
        nc = self.nc
        T, TA = self.T, self.TA
        with ExitStack() as st:
            self.st = st
            self.S = Sched(nc)
            S = self.S
            self.bar_id = 0
            self.ov_gen = 0
            kinds = set(k for k, _ in self.plan)
            self.xT = self.din("xT", [DM, TA])
            self.yT = self.dout("yT", [DM, TA])
            self.rs = nc.dram_tensor("rs", [DM, TA], F32, kind="Internal").ap()
            self.res_dst = self.rs
            self.nrm = self.din("nrm", [128, 2 * NL, KC])
            self.cst = self.din("cst", [128, 4, 128])
            self.ident_d = self.din("ident", [128, 128])
            if "ffn" in kinds:
                self.w_up = self.din("w_up", [NL, DM, 2 * FF])
                self.w_down = self.din("w_down", [NL, FF, DM])
                self.convw = self.din("convw", [NL, 128, 2 * FC, 3])
                self.convb = self.din("convb", [NL, 128, 2 * FC])
                self.sconv = self.din("sconv", [NL, 128, 2 * FC, 2])
                self.convp_o = self.dout("convp_o", [NL, 128, 2 * FC, 2])
                self.convs_o = self.dout("convs_o", [NL, 128, 2 * FC, 2])
            if "attn" in kinds:
                self.w_qkv = self.din("w_qkv", [NLA, DM, 9216])
                self.w_o = self.din("w_o", [NLA, 1024, DM])
                self.qkg = self.din("qkg", [128, NLA, 2])
                self.rope = self.din("rope", [2, 128, TA])
                self.rmat_d = self.din("rmat", [128, 128])
                self.cmask_d = self.din("cmask", [128, NCM])
                self.cache = [self.din("cache%d" % g, [NLA, NBUF[g], 2, 8, 128]) for g in range(3)]
                self.kT_o = [self.dout("kT_o%d" % g, [NLA, 8, 128, min(NBUF[g], T)]) for g in range(3)]
                self.v_o = [self.dout("v_o%d" % g, [NLA, 8, DILS[g], 128, 128]) for g in range(3)]
                self.kvs_o = [self.dout("kvs_o%d" % g, [NLA, NBUF[g], 2, 8, 128]) for g in range(3)]
            if "mlstm" in kinds:
                self.w_in = self.din("w_in", [NLB, DM, 6160])
                self.w_out = self.din("w_out", [NLB, DM, DM])
                self.bg = self.din("bg", [32, NLB, 2])
                self.gh = self.din("gh", [128, NLB, 16])
                self.sel_d = self.din("sel", [32, 8, 128])
                self.C0T = self.din("C0T", [NLB, 8, 128, 256])
                self.n0 = self.din("n0", [NLB, 8, 128, 1])
                self.m0 = self.din("m0", [32, NLB])
                self.cmask2_d = self.din("cmask2", [128, 128])
                self.CT_o = self.dout("CT_o", [NLB, 2, 8, 128, 256])
                self.n_o = self.dout("n_o", [NLB, 2, 8, 128, 1])
                self.m_o = self.dout("m_o", [NLB, 2, 8, 1])
            self.xn = self.sb("xn", [128, KC, TA + 28], BF16)
            self.ovf = self.sb("ovf", [128, OVF], F32)
            self.ovb = self.sb("ovb", [128, OVB], BF16)
            self.wb = [self.sb("wb%d" % i, [128, 8192], BF16) for i in range(NWB)]
            self.wbi = 0
            self.rt = [self.sb("rt%d" % i, [128, 512], F32) for i in range(2)]
            self.rti = 0
            self.nrm_sb = self.sb("nrm_sb", [128, 2 * NL, KC], F32)
            self.ones_f = self.sb("ones_f", [128, 128], F32)
            self.cst_b = self.sb("cst_b", [128, 4, 128], BF16)
            self.ones_b = self.cst_b[:, 0, :]
            self.onesh = self.cst_b[:, 1, :]
            self.ones1 = self.cst_b[:, 2, :]
            self.identf = self.sb("identf", [128, 128], F32)
            self.identb = self.sb("identb", [128, 128], BF16)
            self.epsc = self.sb("epsc", [128, 1], F32)
            self.bscr = self.sb("bscr", [128, 8], F32)
            if "attn" in kinds:
                self.qkg_sb = self.sb("qkg_sb", [128, NLA, 2], F32)
                self.rmat = self.sb("rmat_s", [128, 128], BF16)
                self.cmask = self.sb("cmask_s", [128, NCM], F32)
            if "mlstm" in kinds:
                self.bg_sb = self.sb("bg_sb", [32, NLB, 2], F32)
                self.gh_sb = self.sb("gh_sb", [128, NLB, 16], F32)
                self.sel = self.sb("sel_s", [32, 8, 128], F32)
                self.m0_sb = self.sb("m0_sb", [32, NLB], F32)
                self.cmask2 = self.sb("cmask2_s", [128, 128], F32)
            self.ps = [st.enter_context(nc.psum_tensor("ps%d" % i, [128, 512], F32)) for i in range(8)]
            self.psi = 0
            S.dma("sp", lambda e: e.dma_start(out=self.ones_f[:], in_=self.cst[:, 2, :]), w=["cst_f"])
            S.dma("pool", lambda e: e.dma_start(out=self.cst_b[:], in_=self.cst), w=["ones_b", "ones1", "onesh"])
            S.dma("sp", lambda e: e.dma_start(out=self.nrm_sb[:], in_=self.nrm), w=["nrm_sb"])
            S.dma("sp", lambda e: e.dma_start(out=self.identf[:], in_=self.ident_d), w=["identf"])
            S.act(lambda e: e.copy(out=self.identb[:], in_=self.identf[:]), r=["identf"], w=["identb"])
            S.dve(lambda e: e.memset(self.epsc[:], EPS), w=["epsc"])
            S.dve(lambda e: e.memset(self.xn[:, :, TA:TA + 28], 0.0), w=["xnpad"])
            if "attn" in kinds:
                S.dma("sp", lambda e: e.dma_start(out=self.qkg_sb[:], in_=self.qkg), w=["qkg"])
                S.dma("pool", lambda e: e.dma_start(out=self.rmat[:], in_=self.rmat_d), w=["rmat"])
                S.dma("sp", lambda e: e.dma_start(out=self.cmask[:], in_=self.cmask_d), w=["cmask"])
            if "mlstm" in kinds:
                S.dma("sp", lambda e: e.dma_start(out=self.bg_sb[:], in_=self.bg), w=["bg"])
                S.dma("sp", lambda e: e.dma_start(out=self.gh_sb[:], in_=self.gh), w=["gh"])
                S.dma("sp", lambda e: e.dma_start(out=self.sel[:], in_=self.sel_d), w=["sel"])
                S.dma("sp", lambda e: e.dma_start(out=self.m0_sb[:], in_=self.m0), w=["m0"])
                S.dma("sp", lambda e: e.dma_start(out=self.cmask2[:], in_=self.cmask2_d), w=["cmask2"])
                S.dve(lambda e: e.tensor_scalar(out=self.bg_sb[:], in0=self.bg_sb[:], scalar1=1.0 / 15.0, scalar2=None, op0=ALU.mult),
                      r=["bg"], w=["bg"])
            self.res_src = self.xT
            for pi, (kind, l) in enumerate(self.plan):
                self.last_phase = (pi == len(self.plan) - 1)
                if kind == "ffn":
                    self.norm_phase(NL + l)
                    self.ffn_phase(l)
                elif kind == "attn":
                    self.norm_phase(2 * l)
                    self.attn_phase(l)
                elif kind == "mlstm":
                    self.norm_phase(2 * l + 1)
                    self.mlstm_phase(l)
            S.emit(st)
        return nc

    def next_wb(self):
        i = self.wbi % NWB
        self.wbi += 1
        return self.wb[i], [("wb", i, 0), ("wb", i, 1), ("wb", i, 2)]

    def next_ps(self, n=6):
        i = self.psi % n
        self.psi += 1
        return self.ps[i], ("ps", i)

    def res_view(self, src):
        return src.rearrange("(k p) t -> p k t", p=128)

    def reskeys(self, t0, n):
        return [("res", k, t0 // 512) for k in range(KC)]

    def xnkeys(self, ti):
        return [("xn", k, ti) for k in range(KC)]

    def allxn(self):
        return [("xn", k, ti) for k in range(KC) for ti in range(len(self.tiles))]

    def tix(self, t0):
        return t0 // 512

    def norm_phase(self, gi):
        S = self.S
        self.ov_reset()
        src = self.res_view(self.res_src)
        xts = [self.af(KC * 128) for _ in range(2)]
        sqs = [self.ab(KC * 128) for _ in range(2)]
        rstds = [self.af(128) for _ in range(2)]
        for i, (t0, n) in enumerate(self.ntile):
            xt, kx = xts[i % 2]
            sq, ks = sqs[i % 2]
            rstd, kr = rstds[i % 2]
            xt = xt.rearrange("p (k t) -> p k t", k=KC)
            sq = sq.rearrange("p (k t) -> p k t", k=KC)
            S.dma("sp", lambda e, xt=xt, t0=t0, n=n: e.dma_start(out=xt[:, :, :n], in_=src[:, :, t0:t0 + n]),
                  r=self.reskeys(t0, n), w=[kx])
            S.act(lambda e, xt=xt, sq=sq, n=n: e.activation(out=sq[:, :, :n], in_=xt[:, :, :n], func=AF.Square),
                  r=[kx], w=[ks])
            ps, kp = self.ps[6], ("ps", 6)
            for k in range(KC):
                S.pe(lambda e, ps=ps, sq=sq, k=k, n=n: e.matmul(ps[:, :n], lhsT=self.ones_b, rhs=sq[:, k, :n],
                                                                 start=(k == 0), stop=(k == KC - 1)),
                     r=[ks, "ones_b"], w=[kp])
            S.act(lambda e, ps=ps, rstd=rstd, n=n: e.activation(out=rstd[:, :n], in_=ps[:, :n], func=AF.Sqrt,
                                                                bias=self.epsc[:, 0:1], scale=1.0),
                  r=[kp, "epsc"], w=[kr])
            S.dve(lambda e, rstd=rstd, n=n: e.reciprocal(out=rstd[:, :n], in_=rstd[:, :n]), r=[kr], w=[kr])
            for k in range(KC):
                S.dve(lambda e, xt=xt, rstd=rstd, k=k, t0=t0, n=n: e.scalar_tensor_tensor(
                    out=self.xn[:, k, t0:t0 + n], in0=xt[:, k, :n], scalar=self.nrm_sb[:, gi, k:k + 1], in1=rstd[:, :n],
                    op0=ALU.mult, op1=ALU.mult), r=[kx, kr, "nrm_sb"], w=[("xn", k, t0 // 512)])

    def residual_add(self, ps, kp, dc, t0, n):
        S = self.S
        rt = self.rt[self.rti % 2]
        kt = ("rt", self.rti % 2)
        self.rti += 1
        src = self.res_src
        dst = self.res_dst
        key = ("res", dc, t0 // 512)
        S.dma("sp", lambda e: e.dma_start(out=rt[:, :n], in_=src[dc * 128:(dc + 1) * 128, t0:t0 + n]), r=[key], w=[kt])
        S.dve(lambda e: e.tensor_tensor(out=rt[:, :n], in0=ps[:, :n], in1=rt[:, :n], op=ALU.add), r=[kp, kt], w=[kt])
        S.dma("sp", lambda e: e.dma_start(out=dst[dc * 128:(dc + 1) * 128, t0:t0 + n], in_=rt[:, :n]), r=[kt], w=[key])

    def proj_rmw(self, wsrc, c0, G, rhs_fn, final=False):
        S = self.S
        if final:
            self.res_dst = self.yT
        for dq in range(4):
            wt, kw = self.next_wb()
            wv = wt[:, 0:G * 512].rearrange("p (c d) -> p c d", c=G)
            S.dma("pool", lambda e, wv=wv, dq=dq: e.dma_start(out=wv, in_=wsrc[:, c0:c0 + G, dq * 512:(dq + 1) * 512]), w=kw)
            for d4 in range(4):
                dc = dq * 4 + d4
                for ti, (t0, n) in enumerate(self.tiles):
                    ps, kp = self.next_ps()
                    for c in range(G):
                        rhs, krhs = rhs_fn(c, t0, n, ti)
                        S.pe(lambda e, ps=ps, wv=wv, c=c, d4=d4, n=n, rhs=rhs: e.matmul(
                            ps[:, :n], lhsT=wv[:, c, d4 * 128:(d4 + 1) * 128], rhs=rhs,
                            start=(c == 0), stop=(c == G - 1)), r=[kw[0], krhs], w=[kp])
                    self.residual_add(ps, kp, dc, t0, n)
        self.res_src = self.rs

    def ffn_phase(self, l):
        S = self.S
        tiles = self.tiles
        TA = self.TA
        self.ov_reset()
        zt, _ = self.ab(GRP * TA)
        z = zt.rearrange("p (g t) -> p g t", g=GRP)
        Ub = [[self.af(516) for j in range(2)] for i in range(2)]
        ccb = [[self.af(512) for j in range(2)] for i in range(2)]
        sab = [self.af(512) for i in range(2)]
        cw_t, _ = self.af(2 * FC * 3)
        cb_t, _ = self.af(2 * FC)
        sc_t, _ = self.af(2 * FC * 2)
        cpo_t, _ = self.af(2 * FC * 2)
        cso_t, _ = self.af(2 * FC * 2)
        self.cw = cw_t.rearrange("p (c j) -> p c j", j=3)
        self.cb = cb_t
        self.sc = sc_t.rearrange("p (c j) -> p c j", j=2)
        self.cpo = cpo_t.rearrange("p (c j) -> p c j", j=2)
        self.cso = cso_t.rearrange("p (c j) -> p c j", j=2)
        S.dma("sp", lambda e: e.dma_start(out=self.cw, in_=self.convw[l]), w=["cw"])
        S.dma("sp", lambda e: e.dma_start(out=self.cb, in_=self.convb[l]), w=["cb"])
        S.dma("sp", lambda e: e.dma_start(out=self.sc, in_=self.sconv[l]), w=["sc"])
        wup = self.w_up[l].rearrange("(k p) c -> p k c", p=128)
        wdn = self.w_down[l].rearrange("(c p) d -> p c d", p=128)
        ucount = 0
        f0 = 0
        gen = self.ov_gen
        while f0 < FC:
            G = min(GRP, FC - f0)
            for pr in range(G // 2):
                fa = f0 + 2 * pr
                wt, kw = self.next_wb()
                wv = wt[:].rearrange("p (k c) -> p k c", k=KC)
                S.dma("pool", lambda e, wv=wv, fa=fa: e.dma_start(out=wv[:, :, 0:256], in_=wup[:, :, fa * 128:fa * 128 + 256]), w=kw)
                S.dma("pool", lambda e, wv=wv, fa=fa: e.dma_start(out=wv[:, :, 256:512], in_=wup[:, :, FF + fa * 128:FF + fa * 128 + 256]), w=[kw[1]])
                for fi in range(2):
                    f = fa + fi
                    zi = f - f0
                    prevU = None
                    for ti, (t0, n) in enumerate(tiles):
                        pa, ka = self.next_ps()
                        pb, kb = self.next_ps()
                        for (pp, kp, co, kwx) in ((pa, ka, fi * 128, kw[0]), (pb, kb, 256 + fi * 128, kw[1])):
                            for k in range(KC):
                                S.pe(lambda e, pp=pp, k=k, co=co, wv=wv, t0=t0, n=n: e.matmul(
                                    pp[:, :n], lhsT=wv[:, k, co:co + 128], rhs=self.xn[:, k, t0:t0 + n],
                                    start=(k == 0), stop=(k == KC - 1)), r=[kwx, ("xn", k, ti)], w=[kp])
                        ub = ucount % 2
                        ucount += 1
                        cs = []
                        for ab, (pp, kp, ch) in enumerate(((pa, ka, f), (pb, kb, FC + f))):
                            U, kU = Ub[ub][ab]
                            c, kc = ccb[ub][ab]
                            is_sample = (t0 == self.T)
                            if is_sample:
                                S.act(lambda e, U=U, ch=ch: e.copy(out=U[:, 0:2], in_=self.sc[:, ch, :]), r=["sc"], w=[kU])
                            elif ti == 0:
                                S.dve(lambda e, U=U: e.memset(U[:, 0:2], 0.0), w=[kU])
                            else:
                                pU, pk = prevU[ab]
                                S.act(lambda e, U=U, pU=pU: e.copy(out=U[:, 0:2], in_=pU[:, 512:514]), r=[pk], w=[kU])
                            S.act(lambda e, U=U, pp=pp, n=n: e.copy(out=U[:, 2:2 + n], in_=pp[:, :n]), r=[kp], w=[kU])
                            S.act(lambda e, c=c, pp=pp, n=n, ch=ch: e.activation(out=c[:, :n], in_=pp[:, :n], func=AF.Identity,
                                                                                 scale=self.cw[:, ch, 2:3], bias=self.cb[:, ch:ch + 1]),
                                  r=[kp, "cw", "cb"], w=[kc])
                            S.dve(lambda e, c=c, U=U, n=n, ch=ch: e.scalar_tensor_tensor(
                                out=c[:, :n], in0=U[:, 1:1 + n], scalar=self.cw[:, ch, 1:2], in1=c[:, :n], op0=ALU.mult, op1=ALU.add),
                                r=[kU, kc, "cw"], w=[kc])
                            S.dve(lambda e, c=c, U=U, n=n, ch=ch: e.scalar_tensor_tensor(
                                out=c[:, :n], in0=U[:, 0:n], scalar=self.cw[:, ch, 0:1], in1=c[:, :n], op0=ALU.mult, op1=ALU.add),
                                r=[kU, kc, "cw"], w=[kc])
                            if ti == len(tiles) - 2:
                                S.act(lambda e, U=U, ch=ch, n=n: e.copy(out=self.cpo[:, ch, :], in_=U[:, n:n + 2]), r=[kU], w=["cpo"])
                            if is_sample:
                                S.act(lambda e, U=U, ch=ch, n=n: e.copy(out=self.cso[:, ch, :], in_=U[:, n:n + 2]), r=[kU], w=["cso"])
                            cs.append((c, kc))
                        prevU = [Ub[ub][0], Ub[ub][1]]
                        sa, ksa = sab[ub]
                        S.act(lambda e, sa=sa, c=cs[0][0], n=n: e.activation(out=sa[:, :n], in_=c[:, :n], func=AF.Silu),
                              r=[cs[0][1]], w=[ksa])
                        S.dve(lambda e, sa=sa, c=cs[1][0], n=n, zi=zi, t0=t0: e.tensor_tensor(
                            out=z[:, zi, t0:t0 + n], in0=sa[:, :n], in1=c[:, :n], op=ALU.mult),
                            r=[ksa, cs[1][1]], w=[("z", gen, zi, ti)])
            self.proj_rmw(wdn, f0, G, lambda c, t0, n, ti: (z[:, c, t0:t0 + n], ("z", gen, c, ti)),
                          final=(self.last_phase and f0 + G >= FC))
            f0 += G
        S.dma("sp", lambda e: e.dma_start(out=self.convp_o[l], in_=self.cpo), r=["cpo"])
        S.dma("sp", lambda e: e.dma_start(out=self.convs_o[l], in_=self.cso), r=["cso"])

    def attn_phase(self, l):
        S = self.S
        T, TA = self.T, self.TA
        tiles = self.tiles
        self.ov_reset()
        gen = self.ov_gen
        oallt, _ = self.ab(8 * TA)
        oav = oallt.rearrange("p (h t) -> p h t", h=8)
        cosb, kcos = self.ab(TA)
        sinb, ksin = self.ab(TA)
        qr, kqr = self.ab(TA + 28)
        kr, kkr = self.ab(TA + 28)
        S.dve(lambda e: e.memset(qr[:, TA:TA + 28], 0.0), w=[(kqr, "pad")])
        S.dve(lambda e: e.memset(kr[:, TA:TA + 28], 0.0), w=[(kkr, "pad")])
        NUM, kN = self.af(TA)
        DEN, kD = self.af(TA)
        sqb = [self.ab(512) for _ in range(2)]
        xgb = [self.ab(512) for _ in range(2)]
        rsb = [self.af(512) for _ in range(2)]
        t1b = [self.af(512) for _ in range(2)]
        t2b = [self.af(512) for _ in range(2)]
        kof = [self.af(512) for _ in range(2)]
        vbb = [self.ab(128) for _ in range(3)]
        vfb = [self.af(128) for _ in range(2)]
        peb = [self.af(256) for _ in range(2)]
        ptb = [self.ab(256) for _ in range(3)]
        smf = [self.af(128) for _ in range(4)]
        smb = [self.ab(128) for _ in range(5)]
        kcf = [self.af(128) for _ in range(2)]
        vcf = [self.af(128) for _ in range(2)]
        S.dma("pool", lambda e: e.dma_start(out=cosb, in_=self.rope[0]), w=[kcos])
        S.dma("pool", lambda e: e.dma_start(out=sinb, in_=self.rope[1]), w=[ksin])
        wq = self.w_qkv[l].rearrange("(k p) c -> p k c", p=128)
        wo = self.w_o[l].rearrange("(c p) d -> p c d", p=128)
        cm = self.cmask
        maskA = cm[:, 0:256]
        maskc0 = cm[:, 384:388]
        masknew0 = cm[0:4, 388:392]
        masknewI = cm[0:4, 392:396]
        cnt = 0
        for h in range(self.dbg.get("heads", 8)):
            for g in self.dbg.get("groups", (0, 1, 2)):
                dil = DILS[g]
                nbuf = NBUF[g]
                nkeep = min(nbuf, T)
                cq = g * 3072 + h * 128
                wt, kw = self.next_wb()
                wv = wt[:, 0:KC * 384].rearrange("p (k c) -> p k c", k=KC)
                S.dma("pool", lambda e, wv=wv, cq=cq: e.dma_start(out=wv[:, :, 0:128], in_=wq[:, :, cq:cq + 128]), w=kw)
                S.dma("pool", lambda e, wv=wv, cq=cq: e.dma_start(out=wv[:, :, 128:256], in_=wq[:, :, cq + 1024:cq + 1152]), w=[kw[1]])
                S.dma("pool", lambda e, wv=wv, cq=cq: e.dma_start(out=wv[:, :, 256:384], in_=wq[:, :, cq + 2048:cq + 2176]), w=[kw[2]])
                for which, (dst, kdst, co) in enumerate(((qr, kqr, 0), (kr, kkr, 128))):
                    for ti, (t0, n) in enumerate(tiles):
                        ps, kp = self.next_ps()
                        for k in range(KC):
                            S.pe(lambda e, ps=ps, k=k, co=co, wv=wv, t0=t0, n=n: e.matmul(
                                ps[:, :n], lhsT=wv[:, k, co:co + 128], rhs=self.xn[:, k, t0:t0 + n],
                                start=(k == 0), stop=(k == KC - 1)), r=[kw[which], ("xn", k, ti)], w=[kp])
                        i2 = cnt % 2
                        cnt += 1
                        sq, ksq = sqb[i2]
                        xg, kxg = xgb[i2]
                        rs, krs = rsb[i2]
                        t1, kt1 = t1b[i2]
                        t2, kt2 = t2b[i2]
                        S.act(lambda e, sq=sq, ps=ps, n=n: e.activation(out=sq[:, :n], in_=ps[:, :n], func=AF.Square), r=[kp], w=[ksq])
                        S.act(lambda e, xg=xg, ps=ps, n=n, which=which: e.activation(out=xg[:, :n], in_=ps[:, :n], func=AF.Identity,
                                                                                      scale=self.qkg_sb[:, l, which:which + 1]),
                              r=[kp, "qkg"], w=[kxg])
                        ps2, kp2 = self.next_ps()
                        S.pe(lambda e, ps2=ps2, sq=sq, n=n: e.matmul(ps2[:, :n], lhsT=self.onesh, rhs=sq[:, :n], start=True, stop=True),
                             r=[ksq, "onesh"], w=[kp2])
                        ps3, kp3 = self.next_ps()
                        S.pe(lambda e, ps3=ps3, xg=xg, n=n: e.matmul(ps3[:, :n], lhsT=self.rmat[:], rhs=xg[:, :n], start=True, stop=True),
                             r=[kxg, "rmat"], w=[kp3])
                        S.act(lambda e, rs=rs, ps2=ps2, n=n: e.activation(out=rs[:, :n], in_=ps2[:, :n], func=AF.Sqrt,
                                                                          bias=self.epsc[:, 0:1], scale=1.0), r=[kp2, "epsc"], w=[krs])
                        S.dve(lambda e, rs=rs, n=n: e.reciprocal(out=rs[:, :n], in_=rs[:, :n]), r=[krs], w=[krs])
                        S.dve(lambda e, t1=t1, xg=xg, t0=t0, n=n: e.tensor_tensor(out=t1[:, :n], in0=xg[:, :n], in1=cosb[:, t0:t0 + n], op=ALU.mult),
                              r=[kxg, kcos], w=[kt1])
                        S.dve(lambda e, t2=t2, ps3=ps3, t0=t0, n=n: e.tensor_tensor(out=t2[:, :n], in0=ps3[:, :n], in1=sinb[:, t0:t0 + n], op=ALU.mult),
                              r=[kp3, ksin], w=[kt2])
                        S.pool(lambda e, t1=t1, t2=t2, n=n: e.tensor_tensor(out=t1[:, :n], in0=t1[:, :n], in1=t2[:, :n], op=ALU.add),
                               r=[kt1, kt2], w=[kt1])
                        S.dve(lambda e, dst=dst, t1=t1, rs=rs, t0=t0, n=n: e.tensor_tensor(out=dst[:, t0:t0 + n], in0=t1[:, :n], in1=rs[:, :n], op=ALU.mult),
                              r=[kt1, krs], w=[(kdst, ti)])
                krk = [(kkr, ti) for ti in range(len(tiles))]
                kqk = [(kqr, ti) for ti in range(len(tiles))]
                for pc in range(nkeep // 512 if nkeep >= 512 else 1):
                    w_ = min(512, nkeep)
                    c0 = T - nkeep + pc * w_
                    ko, kko = kof[pc % 2]
                    S.act(lambda e, ko=ko, c0=c0, w_=w_: e.copy(out=ko[:, :w_], in_=kr[:, c0:c0 + w_]), r=krk, w=[kko])
                    S.dma("sp", lambda e, ko=ko, pc=pc, w_=w_, g=g, h=h: e.dma_start(out=self.kT_o[g][l, h, :, pc * w_:(pc + 1) * w_], in_=ko[:, :w_]), r=[kko])
                if self.dbg.get("noattn"):
                    continue
                nb = (T // dil) // 128
                qv = qr[:, 0:T].rearrange("p (a b) -> p a b", b=dil)
                kv_ = kr[:, 0:T].rearrange("p (a b) -> p a b", b=dil)
                Nv = NUM[:, 0:T].rearrange("p (a b) -> p a b", b=dil)
                Dv = DEN[:, 0:T].rearrange("p (a b) -> p a b", b=dil)
                for r in range(dil):
                    prev = None
                    for m in range(nb):
                        psv, kpv = self.next_ps()
                        for k in range(KC if not self.dbg.get("nov") else 0):
                            xv = self.xn[:, k, 0:T].rearrange("p (a b) -> p a b", b=dil)
                            S.pe(lambda e, psv=psv, k=k, xv=xv, m=m, r=r, wv=wv: e.matmul(
                                psv[:, 0:128], lhsT=xv[:, 128 * m:128 * m + 128, r], rhs=wv[:, k, 256:384],
                                start=(k == 0), stop=(k == KC - 1)), r=[kw[2]] + [("xn", k, ti) for ti in range(len(tiles) - 1)], w=[kpv])
                        vb, kvb = vbb[cnt % 3]
                        if m == nb - 1:
                            vf, kvf = vfb[cnt % 2]
                            S.dve(lambda e, vf=vf, psv=psv: e.tensor_copy(out=vf, in_=psv[:, 0:128]), r=[kpv], w=[kvf])
                            S.act(lambda e, vb=vb, vf=vf: e.copy(out=vb, in_=vf), r=[kvf], w=[kvb])
                            S.dma("sp", lambda e, vf=vf, g=g, h=h, r=r: e.dma_start(out=self.v_o[g][l, h, r], in_=vf), r=[kvf])
                        else:
                            S.act(lambda e, vb=vb, psv=psv: e.copy(out=vb, in_=psv[:, 0:128]), r=[kpv], w=[kvb])
                        nq = 256 if m + 1 < nb else 128
                        pss, kps = self.next_ps()
                        S.pe(lambda e, pss=pss, m=m, r=r, nq=nq, kv_=kv_, qv=qv: e.matmul(
                            pss[:, 0:nq], lhsT=kv_[:, 128 * m:128 * m + 128, r], rhs=qv[:, 128 * m:128 * m + nq, r], start=True, stop=True),
                            r=krk + kqk, w=[kps])
                        pe_, kpe = peb[cnt % 2]
                        pt, kpt = ptb[cnt % 3]
                        S.act(lambda e, pe_=pe_, pss=pss, nq=nq: e.activation(out=pe_[:, :nq], in_=pss[:, :nq], func=AF.Exp, scale=SCALE_A), r=[kps], w=[kpe])
                        S.dve(lambda e, pt=pt, pe_=pe_, nq=nq: e.tensor_tensor(out=pt[:, :nq], in0=pe_[:, :nq], in1=maskA[:, :nq], op=ALU.mult),
                              r=[kpe, "cmask"], w=[kpt])
                        if self.dbg.get("nopv"):
                            prev = (vb, kvb, pt, kpt)
                            cnt += 1
                            continue
                        pso, kpo = self.next_ps()
                        for (lo, which) in ((0, "v"), (128, "1")):
                            if prev is not None:
                                pvb, pkvb, ppt, pkpt = prev
                                S.pe(lambda e, pso=pso, lo=lo, which=which, pvb=pvb, ppt=ppt: e.matmul(
                                    pso[:, lo:lo + 128], lhsT=(pvb if which == "v" else self.ones1), rhs=ppt[:, 128:256], start=True, stop=False),
                                    r=[pkvb, pkpt, "ones1"], w=[kpo])
                            S.pe(lambda e, pso=pso, lo=lo, which=which, vb=vb, pt=pt, first=(prev is None): e.matmul(
                                pso[:, lo:lo + 128], lhsT=(vb if which == "v" else self.ones1), rhs=pt[:, 0:128], start=first, stop=True),
                                r=[kvb, kpt, "ones1"], w=[kpo])
                        nvv = Nv[:, 128 * m:128 * m + 128, r]
                        dvv = Dv[:, 128 * m:128 * m + 128, r]
                        if g == 0:
                            S.act(lambda e, nvv=nvv, pso=pso: e.copy(out=nvv, in_=pso[:, 0:128]), r=[kpo], w=[kN])
                            S.act(lambda e, dvv=dvv, pso=pso: e.copy(out=dvv, in_=pso[:, 128:256]), r=[kpo], w=[kD])
                        else:
                            S.dve(lambda e, nvv=nvv, pso=pso: e.tensor_tensor(out=nvv, in0=nvv, in1=pso[:, 0:128], op=ALU.add), r=[kpo, kN], w=[kN])
                            S.dve(lambda e, dvv=dvv, pso=pso: e.tensor_tensor(out=dvv, in0=dvv, in1=pso[:, 128:256], op=ALU.add), r=[kpo, kD], w=[kD])
                        prev = (vb, kvb, pt, kpt)
                        cnt += 1
                if self.dbg.get("nosample"):
                    continue
                qs = qr[:, T:T + 4]
                ks_ = kr[:, T:T + 4]
                psv, kpv = self.next_ps()
                for k in range(KC):
                    S.pe(lambda e, psv=psv, k=k, wv=wv: e.matmul(psv[0:32, 0:128], lhsT=self.xn[:, k, T:T + 32], rhs=wv[:, k, 256:384],
                                                                  start=(k == 0), stop=(k == KC - 1)), r=[kw[2], ("xn", k, len(tiles) - 1), "xnpad"], w=[kpv])
                vnb, kvnb = smb[0]
                vnf, kvnf = smf[0]
                S.dve(lambda e, psv=psv: e.tensor_copy(out=vnf[0:32, :], in_=psv[0:32, 0:128]), r=[kpv], w=[kvnf])
                S.act(lambda e: e.copy(out=vnb[0:32, :], in_=vnf[0:32, :]), r=[kvnf], w=[kvnb])
                S.dma("sp", lambda e, g=g, h=h, nbuf=nbuf: e.dma_start(out=self.kvs_o[g][l, nbuf - 4:nbuf, 1, h, :], in_=vnf[0:4, :]), r=[kvnf])
                psk, kpk = self.next_ps()
                S.pe(lambda e, psk=psk: e.matmul(psk[0:32, 0:128], lhsT=kr[:, T:T + 32], rhs=self.identb[:], start=True, stop=True), r=krk + ["identb", (kkr, "pad")], w=[kpk])
                knf, kknf = smf[1]
                S.act(lambda e, psk=psk: e.copy(out=knf[0:4, :], in_=psk[0:4, 0:128]), r=[kpk], w=[kknf])
                S.dma("sp", lambda e, g=g, h=h, nbuf=nbuf: e.dma_start(out=self.kvs_o[g][l, nbuf - 4:nbuf, 0, h, :], in_=knf[0:4, :]), r=[kknf])
                if h == 0 and not self.dbg.get("noshift"):
                    S.dma("sp", lambda e, g=g, nbuf=nbuf: e.dma_start(out=self.kvs_o[g][l, 0:nbuf - 4], in_=self.cache[g][l, 4:nbuf]))
                if self.dbg.get("s_stage", 9) < 2:
                    continue
                psn, kpn = self.next_ps()
                S.pe(lambda e, psn=psn: e.matmul(psn[0:32, 0:4], lhsT=kr[:, T:T + 32], rhs=qs, start=True, stop=True), r=krk + kqk + [(kkr, "pad")], w=[kpn])
                pnf, kpnf = smf[2]
                pnb, kpnb = smb[1]
                S.act(lambda e, psn=psn: e.activation(out=pnf[0:32, 0:4], in_=psn[0:32, 0:4], func=AF.Exp, scale=SCALE_A), r=[kpn], w=[kpnf])
                mk = cm[0:32, 388:392] if g == 0 else cm[0:32, 392:396]
                S.dve(lambda e, mk=mk: e.tensor_tensor(out=pnb[0:32, 0:4], in0=pnf[0:32, 0:4], in1=mk, op=ALU.mult), r=[kpnf, "cmask"], w=[kpnb])
                pso, kpo = self.next_ps()
                S.pe(lambda e, pso=pso: e.matmul(pso[:, 0:4], lhsT=vnb[0:32, :], rhs=pnb[0:32, 0:4], start=True, stop=True), r=[kvnb, kpnb], w=[kpo])
                S.pe(lambda e, pso=pso: e.matmul(pso[:, 4:8], lhsT=self.ones1[0:32, :], rhs=pnb[0:32, 0:4], start=True, stop=True), r=[kpnb, "ones1"], w=[kpo])
                if g == 0:
                    S.act(lambda e, pso=pso: e.copy(out=NUM[:, T:T + 4], in_=pso[:, 0:4]), r=[kpo], w=[kN])
                    S.act(lambda e, pso=pso: e.copy(out=DEN[:, T:T + 4], in_=pso[:, 4:8]), r=[kpo], w=[kD])
                else:
                    S.dve(lambda e, pso=pso: e.tensor_tensor(out=NUM[:, T:T + 4], in0=NUM[:, T:T + 4], in1=pso[:, 0:4], op=ALU.add), r=[kpo, kN], w=[kN])
                    S.dve(lambda e, pso=pso: e.tensor_tensor(out=DEN[:, T:T + 4], in0=DEN[:, T:T + 4], in1=pso[:, 4:8], op=ALU.add), r=[kpo, kD], w=[kD])
                if self.dbg.get("s_stage", 9) < 3:
                    continue
                cg = self.cache[g][l]
                nblk = 1 if g == 0 else 4
                for bi in range(nblk):
                    rows = cg[0:128] if g == 0 else cg.rearrange("(u s) a h d -> s u a h d", s=dil)[bi]
                    kc, kkc = kcf[bi % 2]
                    vc, kvc = vcf[bi % 2]
                    S.dma("sp", lambda e, kc=kc, rows=rows, h=h: e.dma_start(out=kc, in_=rows[:, 0, h, :]), w=[kkc])
                    S.dma("sp", lambda e, vc=vc, rows=rows, h=h: e.dma_start(out=vc, in_=rows[:, 1, h, :]), w=[kvc])
                    pst, kpt_ = self.next_ps()
                    S.pe(lambda e, pst=pst, kc=kc: e.matmul(pst[:, 0:128], lhsT=kc, rhs=self.identf[:], start=True, stop=True), r=[kkc, "identf"], w=[kpt_])
                    kcT, kkcT = smb[2]
                    vcb, kvcb = smb[3]
                    S.act(lambda e, pst=pst: e.copy(out=kcT, in_=pst[:, 0:128]), r=[kpt_], w=[kkcT])
                    S.dve(lambda e, vc=vc: e.tensor_copy(out=vcb, in_=vc), r=[kvc], w=[kvcb])
                    q0, nqc = 0, 4
                    mkc = maskc0 if g == 0 else cm[:, 396 + 4 * bi:400 + 4 * bi]
                    pss, kps = self.next_ps()
                    S.pe(lambda e, pss=pss, q0=q0, nqc=nqc: e.matmul(pss[:, 0:nqc], lhsT=kcT, rhs=qr[:, T + q0:T + q0 + nqc], start=True, stop=True),
                         r=[kkcT] + kqk, w=[kps])
                    pcf, kpcf = smf[3]
                    pcb, kpcb = smb[4]
                    S.act(lambda e, pss=pss, nqc=nqc: e.activation(out=pcf[:, 0:nqc], in_=pss[:, 0:nqc], func=AF.Exp, scale=SCALE_A), r=[kps], w=[kpcf])
                    S.dve(lambda e, mkc=mkc: e.tensor_tensor(out=pcb[:, 0:4], in0=pcf[:, 0:4], in1=mkc, op=ALU.mult), r=[kpcf, "cmask"], w=[kpcb])
                    pso, kpo = self.next_ps()
                    S.pe(lambda e, pso=pso, nqc=nqc: e.matmul(pso[:, 0:nqc], lhsT=vcb, rhs=pcb[:, 0:nqc], start=True, stop=True), r=[kvcb, kpcb], w=[kpo])
                    S.pe(lambda e, pso=pso, nqc=nqc: e.matmul(pso[:, 4:4 + nqc], lhsT=self.ones1, rhs=pcb[:, 0:nqc], start=True, stop=True), r=[kpcb, "ones1"], w=[kpo])
                    S.dve(lambda e, pso=pso, q0=q0, nqc=nqc: e.tensor_tensor(out=NUM[:, T + q0:T + q0 + nqc], in0=NUM[:, T + q0:T + q0 + nqc], in1=pso[:, 0:nqc], op=ALU.add),
                          r=[kpo, kN], w=[kN])
                    S.dve(lambda e, pso=pso, q0=q0, nqc=nqc: e.tensor_tensor(out=DEN[:, T + q0:T + q0 + nqc], in0=DEN[:, T + q0:T + q0 + nqc], in1=pso[:, 4:4 + nqc], op=ALU.add),
                          r=[kpo, kD], w=[kD])
            S.dve(lambda e: e.reciprocal(out=DEN, in_=DEN), r=[kD], w=[kD])
            S.dve(lambda e, h=h: e.tensor_tensor(out=oav[:, h, :], in0=NUM, in1=DEN, op=ALU.mult), r=[kN, kD], w=[("oall", gen, h)])
        if self.dbg.get("noproj"):
            return
        self.proj_rmw(wo, 0, 8, lambda c, t0, n, ti: (oav[:, c, t0:t0 + n], ("oall", gen, c)), final=self.last_phase)

    def mlstm_phase(self, l):
        S = self.S
        T, TA = self.T, self.TA
        TP = TA + 28
        tiles = self.tiles
        NCH = T // 128
        self.ov_reset()
        gen = self.ov_gen
        win = self.w_in[l].rearrange("(k p) c -> p k c", p=128)
        wout = self.w_out[l].rearrange("(c p) d -> p c d", p=128)
        A, kA = self.af(TP)
        Bf, kBf = self.af(TP)
        Bc, kBc = self.af(TP)
        lb, klb = Bf, kBf
        Mloc, kMl = self.af(32)
        Mc, kMc = self.af(32)
        Mprev, kMp = self.af(32)
        negM, kNm = self.af(32)
        dec, kdec = self.af(32)
        mout, kmo = self.af(2)
        etok_t, ket = self.af(17 * 8)
        etok = etok_t.rearrange("p (c h) -> p c h", h=8)
        decbc_t, kdb = self.af(8 * 17)
        decbc = decbc_t.rearrange("p (h c) -> p h c", c=17)
        Sf, kSf = self.af(384)
        n0t, kn0 = self.af(1)
        rden = [self.af(128) for _ in range(2)]
        rsb, krsb = self.af(512)
        sgb, ksgb = self.af(512)
        tmpb, ktmp = self.af(512)
        qT, kqT = self.ab(TP)
        kT, kkT = self.ab(TP)
        hT_t, khT = self.ab(2 * TA)
        hT = hT_t.rearrange("p (j t) -> p j t", j=2)
        hn_t, khn = self.ab(2 * TA)
        hn = hn_t.rearrange("p (j t) -> p j t", j=2)
        ktokb = [self.ab(128) for _ in range(2)]
        vxb = [self.ab(384) for _ in range(2)]
        Gmb = [self.ab(128) for _ in range(2)]
        Sb, kSb = self.ab(384)
        sqh_t, ksqh = self.ab(2 * 512)
        sqh = sqh_t.rearrange("p (j t) -> p j t", j=2)
        ones_f1 = self.ones_f[:]
        ones256 = self.cst_b[:, 3, :]
        for buf, kb in ((A, kA), (Bf, kBf), (Bc, kBc)):
            S.dve(lambda e, buf=buf: e.memset(buf[:, TA:TP], 0.0), w=[kb])
        S.dve(lambda e: e.memset(qT[:, TA:TP], 0.0), w=[kqT])
        S.dve(lambda e: e.memset(kT[:, TA:TP], 0.0), w=[kkT])
        wt, kw = self.next_wb()
        wg = wt[:, 0:KC * 64].rearrange("p (k c) -> p k c", k=KC)
        S.dve(lambda e: e.memset(wt[:, 0:KC * 64], 0.0), w=kw)
        S.dma("pool", lambda e: e.dma_start(out=wg[:, :, 0:8], in_=win[:, :, 6144:6152]), w=[kw[0]])
        S.dma("pool", lambda e: e.dma_start(out=wg[:, :, 32:40], in_=win[:, :, 6152:6160]), w=[kw[1]])
        for ti, (t0, n) in enumerate(tiles):
            for gi_, (dst, kd) in enumerate(((A, kA), (Bf, kBf))):
                ps, kp = self.next_ps()
                for k in range(KC):
                    S.pe(lambda e, ps=ps, k=k, gi_=gi_, t0=t0, n=n: e.matmul(
                        ps[0:32, :n], lhsT=wg[:, k, gi_ * 32:gi_ * 32 + 32], rhs=self.xn[:, k, t0:t0 + n],
                        start=(k == 0), stop=(k == KC - 1)), r=[kw[gi_], ("xn", k, ti)], w=[kp])
                S.act(lambda e, ps=ps, dst=dst, gi_=gi_, t0=t0, n=n: e.activation(
                    out=dst[0:32, t0:t0 + n], in_=ps[0:32, :n], func=AF.Tanh, scale=1.0 / 15.0, bias=self.bg_sb[0:32, l, gi_:gi_ + 1]),
                    r=[kp, "bg"], w=[kd])
        S.act(lambda e: e.activation(out=Bf[0:32, 0:TA], in_=Bf[0:32, 0:TA], func=AF.Exp, scale=-15.0), r=[kBf], w=[kBf])
        S.act(lambda e: e.activation(out=Bf[0:32, 0:TA], in_=Bf[0:32, 0:TA], func=AF.Ln, bias=self.ones_f[0:32, 0:1], scale=1.0), r=[kBf, "cst_f"], w=[kBf])
        S.dve(lambda e: e.tensor_scalar(out=A[0:32, 0:TA], in0=A[0:32, 0:TA], scalar1=15.0, scalar2=None, op0=ALU.mult), r=[kA], w=[kA])
        S.dve(lambda e: e.tensor_tensor_scan(out=Bc[0:32, 0:T], data0=Bf[0:32, 0:T], data1=Bf[0:32, 0:T], initial=0.0, op0=ALU.add, op1=ALU.bypass),
              r=[kBf], w=[kBc])
        S.dve(lambda e: e.tensor_tensor_scan(out=Bc[0:32, T:TA], data0=Bf[0:32, T:TA], data1=Bf[0:32, T:TA], initial=0.0, op0=ALU.add, op1=ALU.bypass),
              r=[kBf], w=[kBc])
        S.dve(lambda e: e.tensor_tensor(out=A[0:32, 0:TA], in0=A[0:32, 0:TA], in1=Bc[0:32, 0:TA], op=ALU.add), r=[kA, kBc], w=[kA])
        S.dve(lambda e: e.tensor_reduce(out=Mloc[0:32, 0:NCH], in_=A[0:32, 0:T].rearrange("p (c t) -> p c t", t=128), axis=AX.X, op=ALU.max),
              r=[kA], w=[kMl])
        S.dve(lambda e: e.tensor_reduce(out=Mloc[0:32, 16:17], in_=A[0:32, T:TA], axis=AX.X, op=ALU.max), r=[kA], w=[kMl])
        S.dve(lambda e: e.tensor_scalar(out=Mc[0:32, 0:1], in0=Mloc[0:32, 0:1], scalar1=0.0, scalar2=None, op0=ALU.max), r=[kMl], w=[kMc])
        for c in range(1, NCH):
            S.dve(lambda e, c=c: e.tensor_tensor(out=Mc[0:32, c:c + 1], in0=Mc[0:32, c - 1:c], in1=Mloc[0:32, c:c + 1], op=ALU.max), r=[kMl, kMc], w=[kMc])
        S.dve(lambda e: e.tensor_tensor(out=Mc[0:32, 16:17], in0=Mloc[0:32, 16:17], in1=self.m0_sb[0:32, l:l + 1], op=ALU.max), r=[kMl, "m0"], w=[kMc])
        S.dve(lambda e: e.memset(Mprev[0:32, 0:1], 0.0), w=[kMp])
        S.dve(lambda e: e.tensor_copy(out=Mprev[0:32, 1:NCH], in_=Mc[0:32, 0:NCH - 1]), r=[kMc], w=[kMp])
        S.dve(lambda e: e.tensor_copy(out=Mprev[0:32, 16:17], in_=self.m0_sb[0:32, l:l + 1]), r=["m0"], w=[kMp])
        S.dve(lambda e: e.tensor_tensor(out=dec[0:32, 0:17], in0=Mprev[0:32, 0:17], in1=Mc[0:32, 0:17], op=ALU.subtract), r=[kMp, kMc], w=[kdec])
        S.act(lambda e: e.activation(out=dec[0:32, 0:17], in_=dec[0:32, 0:17], func=AF.Exp), r=[kdec], w=[kdec])
        S.dve(lambda e: e.tensor_scalar(out=negM[0:32, 0:17], in0=Mc[0:32, 0:17], scalar1=-1.0, scalar2=None, op0=ALU.mult), r=[kMc], w=[kNm])
        S.dve(lambda e: e.tensor_tensor(out=mout[0:32, 0:1], in0=Mc[0:32, NCH - 1:NCH], in1=Bc[0:32, T - 1:T], op=ALU.subtract), r=[kMc, kBc], w=[kmo])
        S.dve(lambda e: e.tensor_tensor(out=mout[0:32, 1:2], in0=Mc[0:32, 16:17], in1=Bc[0:32, TA - 1:TA], op=ALU.subtract), r=[kMc, kBc], w=[kmo])
        S.dma("sp", lambda e: e.dma_start(out=self.m_o[l, 0], in_=mout[0:8, 0:1]), r=[kmo])
        S.dma("sp", lambda e: e.dma_start(out=self.m_o[l, 1], in_=mout[0:8, 1:2]), r=[kmo])
        if self.dbg.get("mstage", 9) < 2:
            return
        chunks = [(c * 128, 128, c) for c in range(NCH)] + [(T, TS, 16)]
        for (c0, L, c) in chunks:
            S.act(lambda e, c0=c0, L=L, c=c: e.activation(out=A[0:32, c0:c0 + L], in_=A[0:32, c0:c0 + L], func=AF.Exp, bias=negM[0:32, c:c + 1], scale=1.0),
                  r=[kA, kNm], w=[kA])
            S.act(lambda e, c0=c0, L=L, c=c: e.activation(out=Bc[0:32, c0:c0 + L], in_=Bc[0:32, c0:c0 + L], func=AF.Exp, bias=negM[0:32, c:c + 1], scale=1.0),
                  r=[kBc, kNm], w=[kBc])
        pse, kpe = self.next_ps()
        for (c0, L, c) in chunks:
            Lm = 128 if L == 128 else 32
            S.pe(lambda e, c0=c0, Lm=Lm, c=c: e.matmul(pse[0:Lm, c * 8:(c + 1) * 8], lhsT=A[0:32, c0:c0 + Lm], rhs=self.identf[0:32, 0:8], start=True, stop=True),
                 r=[kA, "identf"], w=[kpe])
        S.act(lambda e: e.copy(out=etok_t[:, 0:NCH * 8], in_=pse[:, 0:NCH * 8]), r=[kpe], w=[ket])
        S.act(lambda e: e.copy(out=etok_t[0:32, 128:136], in_=pse[0:32, 128:136]), r=[kpe], w=[ket])
        psd, kpd = self.next_ps()
        for h in range(8):
            S.pe(lambda e, h=h: e.matmul(psd[:, h * 17:(h + 1) * 17], lhsT=self.sel[0:32, h, :], rhs=dec[0:32, 0:17], start=True, stop=True),
                 r=[kdec, "sel"], w=[kpd])
        S.act(lambda e: e.copy(out=decbc_t[:, 0:136], in_=psd[:, 0:136]), r=[kpd], w=[kdb])
        if self.dbg.get("mstage", 9) < 3:
            return
        scale_k = 128 ** -0.5
        cnt = 0
        for h in range(self.dbg.get("mheads", 8)):
            w1t, kw1 = self.next_wb()
            w1 = w1t[:].rearrange("p (k c) -> p k c", k=KC)
            S.dma("pool", lambda e, h=h: e.dma_start(out=w1[:, :, 0:128], in_=win[:, :, h * 128:h * 128 + 128]), w=kw1)
            S.dma("pool", lambda e, h=h: e.dma_start(out=w1[:, :, 128:256], in_=win[:, :, 1024 + h * 128:1024 + h * 128 + 128]), w=[kw1[1]])
            S.dma("pool", lambda e, h=h: e.dma_start(out=w1[:, :, 256:512], in_=win[:, :, 4096 + h * 256:4096 + h * 256 + 256]), w=[kw1[2]])
            w2t, kw2 = self.next_wb()
            w2 = w2t[:, 0:KC * 384].rearrange("p (k c) -> p k c", k=KC)
            S.dma("pool", lambda e, h=h: e.dma_start(out=w2[:, :, 0:128], in_=win[:, :, 1024 + h * 128:1024 + h * 128 + 128]), w=kw2)
            S.dma("pool", lambda e, h=h: e.dma_start(out=w2[:, :, 128:384], in_=win[:, :, 2048 + h * 256:2048 + h * 256 + 256]), w=[kw2[1]])
            for which, (dst, kd) in enumerate(((qT, kqT), (kT, kkT))):
                for ti, (t0, n) in enumerate(tiles):
                    ps, kp = self.next_ps()
                    for k in range(KC):
                        S.pe(lambda e, ps=ps, k=k, which=which, t0=t0, n=n, w1=w1: e.matmul(
                            ps[:, :n], lhsT=w1[:, k, which * 128:which * 128 + 128], rhs=self.xn[:, k, t0:t0 + n],
                            start=(k == 0), stop=(k == KC - 1)), r=[kw1[which], ("xn", k, ti)], w=[kp])
                    if which == 0:
                        S.act(lambda e, ps=ps, t0=t0, n=n: e.copy(out=qT[:, t0:t0 + n], in_=ps[:, :n]), r=[kp], w=[kqT])
                    else:
                        S.act(lambda e, ps=ps, t0=t0, n=n: e.mul(out=kT[:, t0:t0 + n], in_=ps[:, :n], mul=scale_k), r=[kp], w=[kkT])
            for ti, (t0, n) in enumerate(tiles):
                ps, kp = self.next_ps()
                S.pe(lambda e, ps=ps, h=h, t0=t0, n=n: e.matmul(ps[:, :n], lhsT=self.sel[0:32, h, :], rhs=Bc[0:32, t0:t0 + n], start=True, stop=True),
                     r=[kBc, "sel"], w=[kp])
                S.act(lambda e, ps=ps, t0=t0, n=n: e.copy(out=lb[:, t0:t0 + n], in_=ps[:, :n]), r=[kp], w=[klb])
            for (c0, L, c) in chunks:
                sample = (c == 16)
                if c >= self.dbg.get("mchunks", 99) and not (sample and self.dbg.get("msample")):
                    continue
                Lm = 128 if not sample else 32
                has_state = sample or c > 0
                if sample:
                    S.dma("sp", lambda e, h=h: e.dma_start(out=self.CT_o[l, 0, h], in_=Sf[:, 0:256]), r=[kSf])
                    S.dma("sp", lambda e, h=h: e.dma_start(out=self.n_o[l, 0, h], in_=Sf[:, 256:257]), r=[kSf])
                    S.dma("sp", lambda e, h=h: e.dma_start(out=Sf[:, 0:256], in_=self.C0T[l, h]), w=[kSf])
                    S.dma("sp", lambda e, h=h: e.dma_start(out=n0t[:, 0:1], in_=self.n0[l, h]), w=[kn0])
                    S.dve(lambda e: e.tensor_scalar(out=Sf[:, 256:384], in0=ones_f1, scalar1=n0t[:, 0:1], scalar2=None, op0=ALU.mult),
                          r=[kn0, "cst_f", kSf], w=[kSf])
                    S.act(lambda e, h=h: e.activation(out=Sb, in_=Sf, func=AF.Identity, scale=decbc[:, h, 16:17]), r=[kSf, kdb], w=[kSb])
                pkv, kpkv = self.next_ps()
                for k in range(KC):
                    S.pe(lambda e, pkv=pkv, k=k, c0=c0, Lm=Lm, w2=w2: e.matmul(
                        pkv[0:Lm, 0:384], lhsT=self.xn[:, k, c0:c0 + Lm], rhs=w2[:, k, 0:384], start=(k == 0), stop=(k == KC - 1)),
                        r=[kw2[0], kw2[1], ("xn", k, c0 // 512), "xnpad"], w=[kpkv])
                ktok, kkt = ktokb[cnt % 2]
                vx, kvx = vxb[cnt % 2]
                Gm, kGm = Gmb[cnt % 2]
                rd, krd = rden[cnt % 2]
                cnt += 1
                S.dve(lambda e, ktok=ktok, pkv=pkv, Lm=Lm: e.tensor_scalar(out=ktok[0:Lm, :], in0=pkv[0:Lm, 0:128], scalar1=scale_k, scalar2=None, op0=ALU.mult),
                      r=[kpkv], w=[kkt])
                S.dve(lambda e, vx=vx, pkv=pkv, Lm=Lm, c=c, h=h: e.tensor_scalar(out=vx[0:Lm, 0:256], in0=pkv[0:Lm, 128:384], scalar1=etok[0:Lm, c, h:h + 1],
                                                                               scalar2=None, op0=ALU.mult), r=[kpkv, ket], w=[kvx])
                S.dve(lambda e, vx=vx, Lm=Lm, c=c, h=h: e.tensor_scalar(out=vx[0:Lm, 256:384], in0=ones_f1[0:Lm, :], scalar1=etok[0:Lm, c, h:h + 1],
                                                                      scalar2=None, op0=ALU.mult), r=[ket, "cst_f"], w=[kvx])
                if self.dbg.get("cstage", 9) < 2:
                    continue
                pg, kpg = self.next_ps()
                S.pe(lambda e, pg=pg, c0=c0, Lm=Lm, L=L: e.matmul(pg[0:Lm, 0:L], lhsT=kT[:, c0:c0 + Lm], rhs=qT[:, c0:c0 + L], start=True, stop=True),
                     r=[kkT, kqT], w=[kpg])
                S.dve(lambda e, Gm=Gm, pg=pg, Lm=Lm, L=L: e.tensor_tensor(out=Gm[0:Lm, 0:L], in0=pg[0:Lm, 0:L], in1=self.cmask2[0:Lm, 0:L], op=ALU.mult),
                      r=[kpg, "cmask2"], w=[kGm])
                if self.dbg.get("cstage", 9) < 3:
                    continue
                pn, kpn = self.next_ps()
                for j in range(3):
                    S.pe(lambda e, pn=pn, j=j, vx=vx, Gm=Gm, Lm=Lm, L=L, hs=has_state: e.matmul(
                        pn[:, j * 128:j * 128 + L], lhsT=vx[0:Lm, j * 128:(j + 1) * 128], rhs=Gm[0:Lm, 0:L], start=True, stop=(not hs)),
                        r=[kvx, kGm], w=[kpn])
                    if has_state:
                        S.pe(lambda e, pn=pn, j=j, c0=c0, L=L: e.matmul(
                            pn[:, j * 128:j * 128 + L], lhsT=Sb[:, j * 128:(j + 1) * 128], rhs=qT[:, c0:c0 + L], start=False, stop=True),
                            r=[kSb, kqT], w=[kpn])
                S.act(lambda e, rd=rd, pn=pn, L=L: e.activation(out=rd[:, 0:L], in_=pn[:, 256:256 + L], func=AF.Abs), r=[kpn], w=[krd])
                S.dve(lambda e, rd=rd, c0=c0, L=L: e.tensor_tensor(out=rd[:, 0:L], in0=rd[:, 0:L], in1=lb[:, c0:c0 + L], op=ALU.max),
                      r=[krd, klb], w=[krd])
                S.dve(lambda e, rd=rd, L=L: e.reciprocal(out=rd[:, 0:L], in_=rd[:, 0:L]), r=[krd], w=[krd])
                for j in range(2):
                    S.dve(lambda e, rd=rd, pn=pn, j=j, c0=c0, L=L: e.tensor_tensor(out=hT[:, j, c0:c0 + L], in0=pn[:, j * 128:j * 128 + L], in1=rd[:, 0:L], op=ALU.mult),
                          r=[kpn, krd], w=[(khT, c0 // 512)])
                if self.dbg.get("cstage", 9) < 4:
                    continue
                pS, kpS = self.next_ps()
                S.pe(lambda e, pS=pS, ktok=ktok, vx=vx, Lm=Lm: e.matmul(pS[:, 0:384], lhsT=ktok[0:Lm, :], rhs=vx[0:Lm, 0:384], start=True, stop=True),
                     r=[kkt, kvx], w=[kpS])
                if not has_state:
                    S.act(lambda e, pS=pS: e.copy(out=Sf, in_=pS[:, 0:384]), r=[kpS], w=[kSf])
                else:
                    S.dve(lambda e, pS=pS, h=h, c=c: e.scalar_tensor_tensor(out=Sf, in0=Sf, scalar=decbc[:, h, c:c + 1], in1=pS[:, 0:384],
                                                                            op0=ALU.mult, op1=ALU.add), r=[kpS, kSf, kdb], w=[kSf])
                if c < NCH - 1:
                    S.act(lambda e, h=h, c=c: e.activation(out=Sb, in_=Sf, func=AF.Identity, scale=decbc[:, h, c + 1:c + 2]), r=[kSf, kdb], w=[kSb])
                if sample:
                    S.dma("sp", lambda e, h=h: e.dma_start(out=self.CT_o[l, 1, h], in_=Sf[:, 0:256]), r=[kSf])
                    S.dma("sp", lambda e, h=h: e.dma_start(out=self.n_o[l, 1, h], in_=Sf[:, 256:257]), r=[kSf])
            for ti, (t0, n) in enumerate(tiles if not self.dbg.get("nonorm") else []):
                S.act(lambda e, t0=t0, n=n: e.activation(out=sqh[:, :, :n], in_=hT[:, :, t0:t0 + n], func=AF.Square), r=[(khT, ti)], w=[ksqh])
                pss, kps = self.next_ps()
                for j in range(2):
                    S.pe(lambda e, pss=pss, j=j, n=n: e.matmul(pss[:, :n], lhsT=ones256, rhs=sqh[:, j, :n], start=(j == 0), stop=(j == 1)),
                         r=[ksqh, "ones_b"], w=[kps])
                S.act(lambda e, pss=pss, n=n: e.activation(out=rsb[:, :n], in_=pss[:, :n], func=AF.Sqrt, bias=self.epsc[:, 0:1], scale=1.0),
                      r=[kps, "epsc"], w=[krsb])
                S.dve(lambda e, n=n: e.reciprocal(out=rsb[:, :n], in_=rsb[:, :n]), r=[krsb], w=[krsb])
                for j in range(2):
                    pog, kpog = self.next_ps()
                    for k in range(KC):
                        S.pe(lambda e, pog=pog, k=k, j=j, t0=t0, n=n, w1=w1: e.matmul(
                            pog[:, :n], lhsT=w1[:, k, 256 + j * 128:384 + j * 128], rhs=self.xn[:, k, t0:t0 + n],
                            start=(k == 0), stop=(k == KC - 1)), r=[kw1[2], ("xn", k, ti)], w=[kpog])
                    S.act(lambda e, pog=pog, n=n: e.activation(out=sgb[:, :n], in_=pog[:, :n], func=AF.Sigmoid), r=[kpog], w=[ksgb])
                    S.dve(lambda e, j=j, t0=t0, n=n, h=h: e.scalar_tensor_tensor(out=tmpb[:, :n], in0=hT[:, j, t0:t0 + n], scalar=self.gh_sb[:, l, 2 * h + j:2 * h + j + 1],
                                                                                in1=rsb[:, :n], op0=ALU.mult, op1=ALU.mult), r=[(khT, ti), krsb, "gh"], w=[ktmp])
                    S.dve(lambda e, j=j, t0=t0, n=n: e.tensor_tensor(out=hn[:, j, t0:t0 + n], in0=tmpb[:, :n], in1=sgb[:, :n], op=ALU.mult),
                          r=[ktmp, ksgb], w=[(khn, ti)])
            if self.dbg.get("noproj"):
                continue
            self.proj_rmw(wout, 2 * h, 2, lambda c, t0, n, ti: (hn[:, c, t0:t0 + n], (khn, ti)), final=(self.last_phase and h == 7))


def _consts(T):
    TA = T + TS
    half = 64
    inv_freq = (10000.0 ** (-np.arange(half, dtype=np.float32) / half)).astype(np.float32)
    pos = np.concatenate([np.arange(T), PAST_LEN + np.arange(TS)]).astype(np.float32)
    ang = (pos[None, :] * np.tile(inv_freq, 2)[:, None]).astype(np.float32)
    rope = np.stack([np.cos(ang), np.sin(ang)]).astype(np.float32)
    rmat = np.zeros((128, 128), np.float32)
    for p in range(64):
        rmat[p + 64, p] = -1.0
        rmat[p, p + 64] = 1.0
    cm = np.zeros((128, NCM), np.float32)
    i = np.arange(128)[:, None]
    j = np.arange(256)[None, :]
    cm[:, 0:256] = ((j - i >= 0) & (j - i <= 128)).astype(np.float32)
    t = np.arange(4)[None, :]
    cm[:, 384:388] = (i >= t).astype(np.float32)
    tp = np.arange(4)[:, None]
    cm[0:4, 388:392] = (tp <= t).astype(np.float32)
    cm[0:4, 392:396] = (tp == t).astype(np.float32)
    for bi in range(4):
        cm[:, 396 + 4 * bi + bi] = 1.0
    cst = np.stack([np.full((128, 128), 1.0 / 2048), np.full((128, 128), 1.0 / 128), np.ones((128, 128)), np.full((128, 128), 1.0 / 256)], 1).astype(np.float32)
    sel = np.zeros((32, 8, 128), np.float32)
    for h in range(8):
        sel[h, h, :] = 1.0
    cm2 = (np.arange(128)[:, None] <= np.arange(128)[None, :]).astype(np.float32)
    return dict(rope=rope, rmat=rmat, cmask=cm, cst=cst, ident=np.eye(128, dtype=np.float32), sel=sel, cmask2=cm2)


def prep_shared(inp, T=2048):
    f = lambda a: np.ascontiguousarray(np.asarray(a, dtype=np.float32))
    d = _consts(T)
    nm, nf = f(inp["norm_mix"]), f(inp["norm_ffn"])
    d["nrm"] = f(np.concatenate([nm, nf], 0).reshape(8, 16, 128).transpose(2, 0, 1))
    d["w_up"] = f(inp["ffn_w_up"])
    d["w_down"] = f(inp["ffn_w_down"])
    d["convw"] = f(f(inp["ffn_conv_w"]).reshape(4, 3, 88, 128).transpose(0, 3, 2, 1))
    d["convb"] = f(f(inp["ffn_conv_b"]).reshape(4, 88, 128).transpose(0, 2, 1))
    d["w_qkv"] = f(inp["attn_w_qkv"])
    d["w_o"] = f(inp["attn_w_o"])
    d["qkg"] = f(np.stack([f(inp["attn_q_norm"]), f(inp["attn_k_norm"])], -1).transpose(1, 0, 2))
    d["w_in"] = f(inp["mlstm_w_in"])
    d["w_out"] = f(inp["mlstm_w_out"])
    bgs = f(inp["mlstm_b_gates"])
    bgp = np.zeros((32, 2, 2), np.float32)
    bgp[:8] = np.stack([bgs[:, :8], bgs[:, 8:]], -1).transpose(1, 0, 2)
    d["bg"] = bgp
    d["gh"] = f(f(inp["mlstm_norm_h"]).reshape(2, 16, 128).transpose(2, 0, 1))
    return d


def prep_core(inp, c, T=2048):
    f = lambda a: np.ascontiguousarray(np.asarray(a, dtype=np.float32))
    d = {}
    xp = np.asarray(inp["x_prompt"])[c % 4, :T]
    xs = np.asarray(inp["x_sample"])[c]
    d["xT"] = f(np.concatenate([xp.T, xs.T], 1))
    d["sconv"] = f(f(inp["state_ffn_conv"])[:, c].reshape(4, 2, 88, 128).transpose(0, 3, 2, 1))
    for g, nm in enumerate(("cache_kv_w128", "cache_kv_w512", "cache_kv_w2048")):
        d["cache%d" % g] = f(np.asarray(inp[nm])[:, c])
    d["C0T"] = f(np.asarray(inp["state_mlstm_C"])[:, c].transpose(0, 1, 3, 2))
    d["n0"] = f(np.asarray(inp["state_mlstm_n"])[:, c][..., None])
    m0p = np.zeros((32, 2), np.float32)
    m0p[:8] = np.asarray(inp["state_mlstm_m"])[:, c].T
    d["m0"] = m0p
    return d


_CACHE = {}


def kernel(**inputs):
    T = 2048
    if "prog" not in _CACHE:
        P = Prog(T=T)
        P.build()
        _CACHE["prog"] = P
    P = _CACHE["prog"]
    shared = prep_shared(inputs, T)
    in_maps = []
    for c in range(8):
        d = dict(shared)
        d.update(prep_core(inputs, c, T))
        in_maps.append({k: d[k] for k in P.ins})
    res = run_bass_kernel_spmd(P.nc, in_maps, core_ids=list(range(8)))
    R = res.results
    return assemble(R, T)


def assemble(R, T=2048):
    f32 = np.float32
    yp = np.stack([R[b]["yT"][:, :T].T for b in range(4)]).astype(f32)
    ys = np.stack([R[c]["yT"][:, T:].T for c in range(8)]).astype(f32)
    outs = [yp, ys]
    if "kT_o0" not in R[0]:
        z = lambda *sh: np.zeros(sh, f32)
        for g in range(3):
            outs += [z(NLA, 4, min(NBUF[g], T), 2, 8, 128), z(NLA, 8, NBUF[g], 2, 8, 128)]
        outs += [z(NLB, 4, 8, 256, 128), z(NLB, 8, 8, 256, 128), z(NLB, 4, 8, 128), z(NLB, 8, 8, 128), z(NLB, 4, 8), z(NLB, 8, 8)]
        cv = lambda a: a.transpose(0, 3, 2, 1).reshape(NL, 2, 2 * FF)
        outs += [np.stack([cv(R[b]["convp_o"]) for b in range(4)], 1).astype(f32),
                 np.stack([cv(R[c]["convs_o"]) for c in range(8)], 1).astype(f32)]
        return tuple(outs)
    for g in range(3):
        dil, nb = DILS[g], NBUF[g]
        nkeep = min(nb, T)
        kvp = np.zeros((NLA, 4, nkeep, 2, 8, 128), f32)
        for b in range(4):
            kT = R[b]["kT_o%d" % g]
            kvp[:, b, :, 0] = kT.transpose(0, 3, 1, 2)
            v = R[b]["v_o%d" % g]
            kvp[:, b, :, 1] = v.transpose(0, 3, 2, 1, 4).reshape(NLA, nkeep, 8, 128)
        kvs = np.stack([R[c]["kvs_o%d" % g] for c in range(8)], 1).astype(f32)
        outs += [kvp, kvs]
    CT = [np.stack([R[b]["CT_o"][:, 0] for b in range(4)], 1), np.stack([R[c]["CT_o"][:, 1] for c in range(8)], 1)]
    outs += [np.ascontiguousarray(x.transpose(0, 1, 2, 4, 3)).astype(f32) for x in CT]
    outs += [np.stack([R[b]["n_o"][:, 0, :, :, 0] for b in range(4)], 1).astype(f32),
             np.stack([R[c]["n_o"][:, 1, :, :, 0] for c in range(8)], 1).astype(f32)]
    outs += [np.stack([R[b]["m_o"][:, 0, :, 0] for b in range(4)], 1).astype(f32),
             np.stack([R[c]["m_o"][:, 1, :, 0] for c in range(8)], 1).astype(f32)]
    cv = lambda a: a.transpose(0, 3, 2, 1).reshape(NL, 2, 2 * FF)
    outs += [np.stack([cv(R[b]["convp_o"]) for b in range(4)], 1).astype(f32),
             np.stack([cv(R[c]["convs_o"]) for c in range(8)], 1).astype(f32)]
    return tuple(outs)
```

```python
import numpy as np
import ml_dtypes
from contextlib import ExitStack
import concourse.bass as bass
import concourse.mybir as mybir
from concourse.bass_utils import run_bass_kernel_spmd

F32 = mybir.dt.float32
BF16 = mybir.dt.bfloat16
AF = mybir.ActivationFunctionType
ALU = mybir.AluOpType
AX = mybir.AxisListType

ENGS = ("pe", "act", "dve", "pool", "sp")
NSLOT = 12


class Sched:
    def __init__(self, nc):
        self.nc = nc
        self.ops = []

    def op(self, eng, fn, r=(), w=(), dma=False, drain=False):
        self.ops.append((eng, fn, tuple(r), tuple(w), dma, drain))

    def pe(self, fn, r=(), w=()):
        self.op("pe", fn, r, w)

    def act(self, fn, r=(), w=()):
        self.op("act", fn, r, w)

    def dve(self, fn, r=(), w=()):
        self.op("dve", fn, r, w)

    def pool(self, fn, r=(), w=()):
        self.op("pool", fn, r, w)

    def dma(self, eng, fn, r=(), w=()):
        self.op(eng, fn, r, w, True)

    def emit(self, stack):
        nc = self.nc
        ops = self.ops
        n = len(ops)
        pos = [0] * n
        streams = {e: [] for e in ENGS}
        for i, o in enumerate(ops):
            pos[i] = len(streams[o[0]])
            streams[o[0]].append(i)
        last_w = {}
        readers = {}
        waits = [[] for _ in range(n)]
        signal = [False] * n
        waited = {e: {} for e in ENGS}
        dwaited = {e: set() for e in ENGS}
        dslot = {}
        dval = {}
        dpre = {}
        dcount = {e: 0 for e in ENGS}
        duses = {e: [0] * NSLOT for e in ENGS}
        drainvals = {}
        for i, o in enumerate(ops):
            eng, fn, R, W, dma, drain = o
            drainvals[i] = list(duses[eng]) if drain else None
            deps = set()
            for b in R:
                if b in last_w:
                    deps.add(last_w[b])
            for b in W:
                if b in last_w:
                    deps.add(last_w[b])
                for r_ in readers.get(b, ()):
                    deps.add(r_)
            deps.discard(i)
            for j in sorted(deps):
                oj = ops[j]
                if oj[4]:
                    if j in dwaited[eng]:
                        continue
                    dwaited[eng].add(j)
                    waits[i].append(j)
                else:
                    k = oj[0]
                    if k == eng and eng == "pe":
                        continue
                    if waited[eng].get(k, -1) >= pos[j]:
                        continue
                    waited[eng][k] = pos[j]
                    waits[i].append(j)
                    signal[j] = True
            for b in R:
                readers.setdefault(b, []).append(i)
            for b in W:
                last_w[b] = i
                readers[b] = []
            if dma:
                s = dcount[eng] % NSLOT
                dcount[eng] += 1
                dslot[i] = (eng, s)
                dpre[i] = 16 * duses[eng][s]
                duses[eng][s] += 1
                dval[i] = 16 * duses[eng][s]
                if dpre[i] > 0:
                    pass
        rank = {}
        for e in ENGS:
            c = 0
            for i in streams[e]:
                if signal[i]:
                    c += 1
                    rank[i] = c
        esem = {e: stack.enter_context(nc.semaphore("es_" + e)) for e in ENGS}
        dsem = {}
        for e in ENGS:
            for s in range(min(NSLOT, dcount[e])):
                dsem[(e, s)] = stack.enter_context(nc.semaphore("ds_%s_%d" % (e, s)))
        self.n_wait = sum(len(w) for w in waits)
        block = stack.enter_context(nc.Block())

        def run_stream(engname, engobj):
            for i in streams[engname]:
                eng, fn, R, W, dma, drain = ops[i]
                if drain:
                    for s_ in range(NSLOT):
                        if drainvals[i][s_] > 0:
                            engobj.wait_ge(dsem[(engname, s_)], 16 * drainvals[i][s_])
                for j in waits[i]:
                    if ops[j][4]:
                        engobj.wait_ge(dsem[dslot[j]], dval[j])
                    else:
                        engobj.wait_ge(esem[ops[j][0]], rank[j])
                if dma:
                    if dpre[i] > 0:
                        engobj.wait_ge(dsem[dslot[i]], dpre[i])
                    ins = fn(engobj)
                    ins.then_inc(dsem[dslot[i]], 16)
                else:
                    ins = fn(engobj)
                    if signal[i]:
                        ins.then_inc(esem[eng], 1)
            for s in range(NSLOT):
                if duses[engname][s] > 0:
                    engobj.wait_ge(dsem[(engname, s)], 16 * duses[engname][s])

        @block.tensor
        def _(e):
            run_stream("pe", e)

        @block.scalar
        def _(e):
            run_stream("act", e)

        @block.vector
        def _(e):
            run_stream("dve", e)

        @block.gpsimd
        def _(e):
            run_stream("pool", e)

        @block.sync
        def _(e):
            run_stream("sp", e)


DM = 2048
KC = 16
FF = 5632
FC = 44
NL = 4
NLA = 2
NLB = 2
TS = 4
EPS = 1e-6
GRP = 8
NWB = 2
PAST_LEN = 16384
DILS = (1, 4, 16)
NBUF = (128, 512, 2048)
SCALE_A = 128 ** -0.5
OVF = 10240
OVB = 28672
NCM = 256 + 128 + 4 + 4 + 4 + 16


class Prog:
    def __init__(self, T=2048, plan=None, dbg=None):
        self.T = T
        self.TA = T + TS
        self.tiles = [(i * 512, 512) for i in range(T // 512)] + [(T, TS)]
        self.ntile = [(i * 128, 128) for i in range(T // 128)] + [(T, TS)]
        self.plan = plan if plan is not None else [("attn", 0), ("ffn", 0), ("mlstm", 0), ("ffn", 1),
                                                   ("attn", 1), ("ffn", 2), ("mlstm", 1), ("ffn", 3)]
        self.dbg = dbg or {}
        self.nc = bass.Bass("TRN2", target_bir_lowering=False)
        self.ins = {}
        self.outs = {}

    def din(self, name, shape, dt=F32):
        t = self.nc.dram_tensor(name, list(shape), dt, kind="ExternalInput").ap()
        self.ins[name] = t
        return t

    def dout(self, name, shape, dt=F32):
        t = self.nc.dram_tensor(name, list(shape), dt, kind="ExternalOutput").ap()
        self.outs[name] = t
        return t

    def sb(self, name, shape, dt):
        return self.st.enter_context(self.nc.sbuf_tensor(name, list(shape), dt))

    def ov_reset(self):
        self.barrier()
        self.rt_list = [(self.rt[i][:], ("rt", i)) for i in range(len(self.rt))]
        self.ovf_off = 0
        self.ovb_off = 0
        self.ov_gen += 1
        self.ov_cnt = 0

    def af(self, n):
        v = self.ovf[:, self.ovf_off:self.ovf_off + n]
        self.ovf_off += n
        assert self.ovf_off <= OVF, self.ovf_off
        self.ov_cnt += 1
        return v, ("o", self.ov_gen, self.ov_cnt)

    def ab(self, n):
        n = (n + 1) // 2 * 2
        v = self.ovb[:, self.ovb_off:self.ovb_off + n]
        self.ovb_off += n
        assert self.ovb_off <= OVB, self.ovb_off
        self.ov_cnt += 1
        return v, ("o", self.ov_gen, self.ov_cnt)

    def barrier(self):
        S = self.S
        self.bar_id += 1
        b = self.bar_id
        sc = self.bscr
        S.op("pe", lambda e: e.matmul(self.ps[7][0:1, 0:1], lhsT=self.ones1[0:1, 0:1], rhs=self.ones1[0:1, 0:1], start=True, stop=True),
             r=["ones1"], w=[("bar", b, "pe"), ("ps", 7)])
        S.op("act", lambda e: e.copy(out=sc[0:1, 0:1], in_=self.epsc[0:1, 0:1]), r=["epsc"], w=[("bar", b, "act"), ("bscr", 0)], drain=True)
        S.op("dve", lambda e: e.memset(sc[0:1, 1:2], 0.0), w=[("bar", b, "dve"), ("bscr", 1)])
        S.op("pool", lambda e: e.memset(sc[0:1, 2:3], 0.0), w=[("bar", b, "pool"), ("bscr", 2)], drain=True)
        S.op("sp", lambda e: e.dma_start(out=sc[0:1, 4:5], in_=self.epsc[0:1, 0:1]), r=["epsc"], w=[("bar", b, "sp"), ("bscr", 4)], dma=True, drain=True)
        allk = [("bar", b, e_) for e_ in ENGS]
        S.op("pe", lambda e: e.matmul(self.ps[7][0:1, 0:1], lhsT=self.ones1[0:1, 0:1], rhs=self.ones1[0:1, 0:1], start=True, stop=True),
             r=allk + ["ones1"], w=[("ps", 7)])
        S.op("act", lambda e: e.copy(out=sc[0:1, 0:1], in_=self.epsc[0:1, 0:1]), r=allk + ["epsc"], w=[("bscr", 0)])
        S.op("dve", lambda e: e.memset(sc[0:1, 1:2], 0.0), r=allk, w=[("bscr", 1)])
        S.op("pool", lambda e: e.memset(sc[0:1, 2:3], 0.0), r=allk, w=[("bscr", 2)])
        S.op("sp", lambda e: e.dma_start(out=sc[0:1, 5:6], in_=self.epsc[0:1, 0:1]), r=allk + ["epsc"], w=[("bscr", 5)], dma=True)

    def build(self):
        nc = self.nc
        T, TA = self.T, self.TA
        with ExitStack() as st:
            self.st = st
            self.S = Sched(nc)
            S = self.S
            self.bar_id = 0
            self.ov_gen = 0
            kinds = set(k for k, _ in self.plan)
            self.xT = self.din("xT", [DM, TA])
            self.yT = self.dout("yT", [DM, TA])
            self.rs = nc.dram_tensor("rs", [DM, TA], F32, kind="Internal").ap()
            self.res_dst = self.rs
            self.nrm = self.din("nrm", [128, 2 * NL, KC])
            self.cst = self.din("cst", [128, 4, 128])
            self.ident_d = self.din("ident", [128, 128])
            if "ffn" in kinds:
                self.w_up = self.din("w_up", [NL, DM, 2 * FF])
                self.w_down = self.din("w_down", [NL, FF, DM])
                self.convw = self.din("convw", [NL, 128, 2 * FC, 3])
                self.convb = self.din("convb", [NL, 128, 2 * FC])
                self.sconv = self.din("sconv", [NL, 128, 2 * FC, 2])
                self.convp_o = self.dout("convp_o", [NL, 128, 2 * FC, 2])
                self.convs_o = self.dout("convs_o", [NL, 128, 2 * FC, 2])
            if "attn" in kinds:
                self.w_qkv = self.din("w_qkv", [NLA, DM, 9216])
                self.w_o = self.din("w_o", [NLA, 1024, DM])
                self.qkg = self.din("qkg", [128, NLA, 2])
                self.rope = self.din("rope", [2, 128, TA])
                self.rmat_d = self.din("rmat", [128, 128])
                self.cmask_d = self.din("cmask", [128, NCM])
                self.cache = [self.din("cache%d" % g, [NLA, NBUF[g], 2, 8, 128]) for g in range(3)]
                self.kT_o = [self.dout("kT_o%d" % g, [NLA, 8, 128, min(NBUF[g], T)]) for g in range(3)]
                self.v_o = [self.dout("v_o%d" % g, [NLA, 8, DILS[g], 128, 128]) for g in range(3)]
                self.kvs_o = [self.dout("kvs_o%d" % g, [NLA, NBUF[g], 2, 8, 128]) for g in range(3)]
            if "mlstm" in kinds:
                self.w_in = self.din("w_in", [NLB, DM, 6160])
                self.w_out = self.din("w_out", [NLB, DM, DM])
                self.bg = self.din("bg", [32, NLB, 2])
                self.gh = self.din("gh", [128, NLB, 16])
                self.sel_d = self.din("sel", [32, 8, 128])
                self.C0T = self.din("C0T", [NLB, 8, 128, 256])
                self.n0 = self.din("n0", [NLB, 8, 128, 1])
                self.m0 = self.din("m0", [32, NLB])
                self.cmask2_d = self.din("cmask2", [128, 128])
                self.CT_o = self.dout("CT_o", [NLB, 2, 8, 128, 256])
                self.n_o = self.dout("n_o", [NLB, 2, 8, 128, 1])
                self.m_o = self.dout("m_o", [NLB, 2, 8, 1])
            self.xn = self.sb("xn", [128, KC, TA + 28], BF16)
            self.ovf = self.sb("ovf", [128, OVF], F32)
            self.ovb = self.sb("ovb", [128, OVB], BF16)
            self.wb = [self.sb("wb%d" % i, [128, 8192], BF16) for i in range(NWB)]
            self.wbi = 0
            self.rt = [self.sb("rt%d" % i, [128, 512], F32) for i in range(2)]
            self.rti = 0
            self.nrm_sb = self.sb("nrm_sb", [128, 2 * NL, KC], F32)
            self.ones_f = self.sb("ones_f", [128, 128], F32)
            self.cst_b = self.sb("cst_b", [128, 4, 128], BF16)
            self.ones_b = self.cst_b[:, 0, :]
            self.onesh = self.cst_b[:, 1, :]
            self.ones1 = self.cst_b[:, 2, :]
            self.identf = self.sb("identf", [128, 128], F32)
            self.identb = self.sb("identb", [128, 128], BF16)
            self.epsc = self.sb("epsc", [128, 1], F32)
            self.bscr = self.sb("bscr", [128, 8], F32)
            if "attn" in kinds:
                self.qkg_sb = self.sb("qkg_sb", [128, NLA, 2], F32)
                self.rmat = self.sb("rmat_s", [128, 128], BF16)
                self.cmask = self.sb("cmask_s", [128, NCM], F32)
            if "mlstm" in kinds:
                self.bg_sb = self.sb("bg_sb", [32, NLB, 2], F32)
                self.gh_sb = self.sb("gh_sb", [128, NLB, 16], F32)
                self.sel = self.sb("sel_s", [32, 8, 128], F32)
                self.m0_sb = self.sb("m0_sb", [32, NLB], F32)
                self.cmask2 = self.sb("cmask2_s", [128, 128], F32)
            self.ps = [st.enter_context(nc.psum_tensor("ps%d" % i, [128, 512], F32)) for i in range(8)]
            self.psi = 0
            S.dma("sp", lambda e: e.dma_start(out=self.ones_f[:], in_=self.cst[:, 2, :]), w=["cst_f"])
            S.dma("pool", lambda e: e.dma_start(out=self.cst_b[:], in_=self.cst), w=["ones_b", "ones1", "onesh"])
            S.dma("sp", lambda e: e.dma_start(out=self.nrm_sb[:], in_=self.nrm), w=["nrm_sb"])
            S.dma("sp", lambda e: e.dma_start(out=self.identf[:], in_=self.ident_d), w=["identf"])
            S.act(lambda e: e.copy(out=self.identb[:], in_=self.identf[:]), r=["identf"], w=["identb"])
            S.dve(lambda e: e.memset(self.epsc[:], EPS), w=["epsc"])
            S.dve(lambda e: e.memset(self.xn[:, :, TA:TA + 28], 0.0), w=["xnpad"])
            if "attn" in kinds:
                S.dma("sp", lambda e: e.dma_start(out=self.qkg_sb[:], in_=self.qkg), w=["qkg"])
                S.dma("pool", lambda e: e.dma_start(out=self.rmat[:], in_=self.rmat_d), w=["rmat"])
                S.dma("sp", lambda e: e.dma_start(out=self.cmask[:], in_=self.cmask_d), w=["cmask"])
            if "mlstm" in kinds:
                S.dma("sp", lambda e: e.dma_start(out=self.bg_sb[:], in_=self.bg), w=["bg"])
                S.dma("sp", lambda e: e.dma_start(out=self.gh_sb[:], in_=self.gh), w=["gh"])
                S.dma("sp", lambda e: e.dma_start(out=self.sel[:], in_=self.sel_d), w=["sel"])
                S.dma("sp", lambda e: e.dma_start(out=self.m0_sb[:], in_=self.m0), w=["m0"])
                S.dma("sp", lambda e: e.dma_start(out=self.cmask2[:], in_=self.cmask2_d), w=["cmask2"])
                S.dve(lambda e: e.tensor_scalar(out=self.bg_sb[:], in0=self.bg_sb[:], scalar1=1.0 / 15.0, scalar2=None, op0=ALU.mult),
                      r=["bg"], w=["bg"])
            self.res_src = self.xT
            for pi, (kind, l) in enumerate(self.plan):
                self.last_phase = (pi == len(self.plan) - 1)
                if kind == "ffn":
                    self.norm_phase(NL + l)
                    self.ffn_phase(l)
                elif kind == "attn":
                    self.norm_phase(2 * l)
                    self.attn_phase(l)
                elif kind == "mlstm":
                    self.norm_phase(2 * l + 1)
                    self.mlstm_phase(l)
            S.emit(st)
        return nc

    def next_wb(self):
        i = self.wbi % NWB
        self.wbi += 1
        return self.wb[i], [("wb", i, 0), ("wb", i, 1), ("wb", i, 2)]

    def next_ps(self, n=6):
        i = self.psi % n
        self.psi += 1
        return self.ps[i], ("ps", i)

    def res_view(self, src):
        return src.rearrange("(k p) t -> p k t", p=128)

    def reskeys(self, t0, n):
        return [("res", k, t0 // 512) for k in range(KC)]

    def xnkeys(self, ti):
        return [("xn", k, ti) for k in range(KC)]

    def allxn(self):
        return [("xn", k, ti) for k in range(KC) for ti in range(len(self.tiles))]

    def tix(self, t0):
        return t0 // 512

    def norm_phase(self, gi):
        S = self.S
        self.ov_reset()
        src = self.res_view(self.res_src)
        xts = [self.af(KC * 128) for _ in range(2)]
        sqs = [self.ab(KC * 128) for _ in range(2)]
        rstds = [self.af(128) for _ in range(2)]
        for i, (t0, n) in enumerate(self.ntile):
            xt, kx = xts[i % 2]
            sq, ks = sqs[i % 2]
            rstd, kr = rstds[i % 2]
            xt = xt.rearrange("p (k t) -> p k t", k=KC)
            sq = sq.rearrange("p (k t) -> p k t", k=KC)
            S.dma("sp", lambda e, xt=xt, t0=t0, n=n: e.dma_start(out=xt[:, :, :n], in_=src[:, :, t0:t0 + n]),
                  r=self.reskeys(t0, n), w=[kx])
            S.act(lambda e, xt=xt, sq=sq, n=n: e.activation(out=sq[:, :, :n], in_=xt[:, :, :n], func=AF.Square),
                  r=[kx], w=[ks])
            ps, kp = self.ps[6], ("ps", 6)
            for k in range(KC):
                S.pe(lambda e, ps=ps, sq=sq, k=k, n=n: e.matmul(ps[:, :n], lhsT=self.ones_b, rhs=sq[:, k, :n],
                                                                 start=(k == 0), stop=(k == KC - 1)),
                     r=[ks, "ones_b"], w=[kp])
            S.act(lambda e, ps=ps, rstd=rstd, n=n: e.activation(out=rstd[:, :n], in_=ps[:, :n], func=AF.Sqrt,
                                                                bias=self.epsc[:, 0:1], scale=1.0),
                  r=[kp, "epsc"], w=[kr])
            S.dve(lambda e, rstd=rstd, n=n: e.reciprocal(out=rstd[:, :n], in_=rstd[:, :n]), r=[kr], w=[kr])
            for k in range(KC):
                S.dve(lambda e, xt=xt, rstd=rstd, k=k, t0=t0, n=n: e.scalar_tensor_tensor(
                    out=self.xn[:, k, t0:t0 + n], in0=xt[:, k, :n], scalar=self.nrm_sb[:, gi, k:k + 1], in1=rstd[:, :n],
                    op0=ALU.mult, op1=ALU.mult), r=[kx, kr, "nrm_sb"], w=[("xn", k, t0 // 512)])

    def residual_add(self, ps, kp, dc, t0, n):
        S = self.S
        rt, kt = self.rt_list[self.rti % len(self.rt_list)]
        self.rti += 1
        src = self.res_src
        dst = self.res_dst
        key = ("res", dc, t0 // 512)
        S.dma("sp", lambda e: e.dma_start(out=rt[:, :n], in_=src[dc * 128:(dc + 1) * 128, t0:t0 + n]), r=[key], w=[kt])
        S.dve(lambda e: e.tensor_tensor(out=rt[:, :n], in0=ps[:, :n], in1=rt[:, :n], op=ALU.add), r=[kp, kt], w=[kt])
        S.dma("act", lambda e: e.dma_start(out=dst[dc * 128:(dc + 1) * 128, t0:t0 + n], in_=rt[:, :n]), r=[kt], w=[key])

    def proj_rmw(self, wsrc, c0, G, rhs_fn, final=False):
        S = self.S
        if final:
            self.res_dst = self.yT
        for dq in range(4):
            wt, kw = self.next_wb()
            wv = wt[:, 0:G * 512].rearrange("p (c d) -> p c d", c=G)
            S.dma("pool", lambda e, wv=wv, dq=dq: e.dma_start(out=wv, in_=wsrc[:, c0:c0 + G, dq * 512:(dq + 1) * 512]), w=kw)
            for d4 in range(4):
                dc = dq * 4 + d4
                for ti, (t0, n) in enumerate(self.tiles):
                    ps, kp = self.next_ps()
                    for c in range(G):
                        rhs, krhs = rhs_fn(c, t0, n, ti)
                        S.pe(lambda e, ps=ps, wv=wv, c=c, d4=d4, n=n, rhs=rhs: e.matmul(
                            ps[:, :n], lhsT=wv[:, c, d4 * 128:(d4 + 1) * 128], rhs=rhs,
                            start=(c == 0), stop=(c == G - 1)), r=[kw[0], krhs], w=[kp])
                    self.residual_add(ps, kp, dc, t0, n)
        self.res_src = self.rs

    def ffn_phase(self, l):
        S = self.S
        tiles = self.tiles
        TA = self.TA
        self.ov_reset()
        zt, _ = self.ab(GRP * TA)
        z = zt.rearrange("p (g t) -> p g t", g=GRP)
        Ub = [[self.af(516) for j in range(2)] for i in range(2)]
        ccb = [[self.af(512) for j in range(2)] for i in range(2)]
        sab = [self.af(512) for i in range(2)]
        self.rt_list = self.rt_list + [self.af(512) for _ in range(6)]
        cw_t, _ = self.af(2 * FC * 3)
        cb_t, _ = self.af(2 * FC)
        sc_t, _ = self.af(2 * FC * 2)
        cpo_t, _ = self.af(2 * FC * 2)
        cso_t, _ = self.af(2 * FC * 2)
        self.cw = cw_t.rearrange("p (c j) -> p c j", j=3)
        self.cb = cb_t
        self.sc = sc_t.rearrange("p (c j) -> p c j", j=2)
        self.cpo = cpo_t.rearrange("p (c j) -> p c j", j=2)
        self.cso = cso_t.rearrange("p (c j) -> p c j", j=2)
        S.dma("sp", lambda e: e.dma_start(out=self.cw, in_=self.convw[l]), w=["cw"])
        S.dma("sp", lambda e: e.dma_start(out=self.cb, in_=self.convb[l]), w=["cb"])
        S.dma("sp", lambda e: e.dma_start(out=self.sc, in_=self.sconv[l]), w=["sc"])
        wup = self.w_up[l].rearrange("(k p) c -> p k c", p=128)
        wdn = self.w_down[l].rearrange("(c p) d -> p c d", p=128)
        ucount = 0
        f0 = 0
        gen = self.ov_gen
        while f0 < FC:
            G = min(GRP, FC - f0)
            for pr in range(G // 2):
                fa = f0 + 2 * pr
                wt, kw = self.next_wb()
                wv = wt[:].rearrange("p (k c) -> p k c", k=KC)
                S.dma("pool", lambda e, wv=wv, fa=fa: e.dma_start(out=wv[:, :, 0:256], in_=wup[:, :, fa * 128:fa * 128 + 256]), w=kw)
                S.dma("pool", lambda e, wv=wv, fa=fa: e.dma_start(out=wv[:, :, 256:512], in_=wup[:, :, FF + fa * 128:FF + fa * 128 + 256]), w=[kw[1]])
                for fi in range(2):
                    f = fa + fi
                    zi = f - f0
                    prevU = None
                    for ti, (t0, n) in enumerate(tiles):
                        pa, ka = self.next_ps()
                        pb, kb = self.next_ps()
                        for (pp, kp, co, kwx) in ((pa, ka, fi * 128, kw[0]), (pb, kb, 256 + fi * 128, kw[1])):
                            for k in range(KC):
                                S.pe(lambda e, pp=pp, k=k, co=co, wv=wv, t0=t0, n=n: e.matmul(
                                    pp[:, :n], lhsT=wv[:, k, co:co + 128], rhs=self.xn[:, k, t0:t0 + n],
                                    start=(k == 0), stop=(k == KC - 1)), r=[kwx, ("xn", k, ti)], w=[kp])
                        ub = ucount % 2
                        ucount += 1
                        cs = []
                        for ab, (pp, kp, ch) in enumerate(((pa, ka, f), (pb, kb, FC + f))):
                            U, kU = Ub[ub][ab]
                            c, kc = ccb[ub][ab]
                            is_sample = (t0 == self.T)
                            if is_sample:
                                S.act(lambda e, U=U, ch=ch: e.copy(out=U[:, 0:2], in_=self.sc[:, ch, :]), r=["sc"], w=[kU])
                            elif ti == 0:
                                S.dve(lambda e, U=U: e.memset(U[:, 0:2], 0.0), w=[kU])
                            else:
                                pU, pk = prevU[ab]
                                S.act(lambda e, U=U, pU=pU: e.copy(out=U[:, 0:2], in_=pU[:, 512:514]), r=[pk], w=[kU])
                            S.act(lambda e, U=U, pp=pp, n=n: e.copy(out=U[:, 2:2 + n], in_=pp[:, :n]), r=[kp], w=[kU])
                            S.act(lambda e, c=c, pp=pp, n=n, ch=ch: e.activation(out=c[:, :n], in_=pp[:, :n], func=AF.Identity,
                                                                                 scale=self.cw[:, ch, 2:3], bias=self.cb[:, ch:ch + 1]),
                                  r=[kp, "cw", "cb"], w=[kc])
                            S.dve(lambda e, c=c, U=U, n=n, ch=ch: e.scalar_tensor_tensor(
                                out=c[:, :n], in0=U[:, 1:1 + n], scalar=self.cw[:, ch, 1:2], in1=c[:, :n], op0=ALU.mult, op1=ALU.add),
                                r=[kU, kc, "cw"], w=[kc])
                            S.dve(lambda e, c=c, U=U, n=n, ch=ch: e.scalar_tensor_tensor(
                                out=c[:, :n], in0=U[:, 0:n], scalar=self.cw[:, ch, 0:1], in1=c[:, :n], op0=ALU.mult, op1=ALU.add),
                                r=[kU, kc, "cw"], w=[kc])
                            if ti == len(tiles) - 2:
                                S.act(lambda e, U=U, ch=ch, n=n: e.copy(out=self.cpo[:, ch, :], in_=U[:, n:n + 2]), r=[kU], w=["cpo"])
                            if is_sample:
                                S.act(lambda e, U=U, ch=ch, n=n: e.copy(out=self.cso[:, ch, :], in_=U[:, n:n + 2]), r=[kU], w=["cso"])
                            cs.append((c, kc))
                        prevU = [Ub[ub][0], Ub[ub][1]]
                        sa, ksa = sab[ub]
                        S.act(lambda e, sa=sa, c=cs[0][0], n=n: e.activation(out=sa[:, :n], in_=c[:, :n], func=AF.Silu),
                              r=[cs[0][1]], w=[ksa])
                        S.dve(lambda e, sa=sa, c=cs[1][0], n=n, zi=zi, t0=t0: e.tensor_tensor(
                            out=z[:, zi, t0:t0 + n], in0=sa[:, :n], in1=c[:, :n], op=ALU.mult),
                            r=[ksa, cs[1][1]], w=[("z", gen, zi, ti)])
            self.proj_rmw(wdn, f0, G, lambda c, t0, n, ti: (z[:, c, t0:t0 + n], ("z", gen, c, ti)),
                          final=(self.last_phase and f0 + G >= FC))
            f0 += G
        S.dma("sp", lambda e: e.dma_start(out=self.convp_o[l], in_=self.cpo), r=["cpo"])
        S.dma("sp", lambda e: e.dma_start(out=self.convs_o[l], in_=self.cso), r=["cso"])

    def attn_phase(self, l):
        S = self.S
        T, TA = self.T, self.TA
        tiles = self.tiles
        self.ov_reset()
        gen = self.ov_gen
        oallt, _ = self.ab(8 * TA)
        oav = oallt.rearrange("p (h t) -> p h t", h=8)
        cosb, kcos = self.ab(TA)
        sinb, ksin = self.ab(TA)
        qr, kqr = self.ab(TA + 28)
        kr, kkr = self.ab(TA + 28)
        S.dve(lambda e: e.memset(qr[:, TA:TA + 28], 0.0), w=[(kqr, "pad")])
        S.dve(lambda e: e.memset(kr[:, TA:TA + 28], 0.0), w=[(kkr, "pad")])
        NUM, kN = self.af(TA)
        DEN, kD = self.af(TA)
        sqb = [self.ab(512) for _ in range(2)]
        xgb = [self.ab(512) for _ in range(2)]
        rsb = [self.af(512) for _ in range(2)]
        t1b = [self.af(512) for _ in range(2)]
        t2b = [self.af(512) for _ in range(2)]
        kof = [self.af(512) for _ in range(2)]
        vbb = [self.ab(128) for _ in range(3)]
        vfb = [self.af(128) for _ in range(2)]
        peb = [self.af(256) for _ in range(2)]
        ptb = [self.ab(256) for _ in range(3)]
        smf = [self.af(128) for _ in range(4)]
        smb = [self.ab(128) for _ in range(5)]
        kcf = [self.af(128) for _ in range(2)]
        vcf = [self.af(128) for _ in range(2)]
        S.dma("pool", lambda e: e.dma_start(out=cosb, in_=self.rope[0]), w=[kcos])
        S.dma("pool", lambda e: e.dma_start(out=sinb, in_=self.rope[1]), w=[ksin])
        wq = self.w_qkv[l].rearrange("(k p) c -> p k c", p=128)
        wo = self.w_o[l].rearrange("(c p) d -> p c d", p=128)
        cm = self.cmask
        maskA = cm[:, 0:256]
        maskc0 = cm[:, 384:388]
        masknew0 = cm[0:4, 388:392]
        masknewI = cm[0:4, 392:396]
        cnt = 0
        for h in range(self.dbg.get("heads", 8)):
            for g in self.dbg.get("groups", (0, 1, 2)):
                dil = DILS[g]
                nbuf = NBUF[g]
                nkeep = min(nbuf, T)
                cq = g * 3072 + h * 128
                wt, kw = self.next_wb()
                wv = wt[:, 0:KC * 384].rearrange("p (k c) -> p k c", k=KC)
                S.dma("pool", lambda e, wv=wv, cq=cq: e.dma_start(out=wv[:, :, 0:128], in_=wq[:, :, cq:cq + 128]), w=kw)
                S.dma("pool", lambda e, wv=wv, cq=cq: e.dma_start(out=wv[:, :, 128:256], in_=wq[:, :, cq + 1024:cq + 1152]), w=[kw[1]])
                S.dma("pool", lambda e, wv=wv, cq=cq: e.dma_start(out=wv[:, :, 256:384], in_=wq[:, :, cq + 2048:cq + 2176]), w=[kw[2]])
                for which, (dst, kdst, co) in enumerate(((qr, kqr, 0), (kr, kkr, 128))):
                    for ti, (t0, n) in enumerate(tiles):
                        ps, kp = self.next_ps()
                        for k in range(KC):
                            S.pe(lambda e, ps=ps, k=k, co=co, wv=wv, t0=t0, n=n: e.matmul(
                                ps[:, :n], lhsT=wv[:, k, co:co + 128], rhs=self.xn[:, k, t0:t0 + n],
                                start=(k == 0), stop=(k == KC - 1)), r=[kw[which], ("xn", k, ti)], w=[kp])
                        i2 = cnt % 2
                        cnt += 1
                        sq, ksq = sqb[i2]
                        xg, kxg = xgb[i2]
                        rs, krs = rsb[i2]
                        t1, kt1 = t1b[i2]
                        t2, kt2 = t2b[i2]
                        S.act(lambda e, sq=sq, ps=ps, n=n: e.activation(out=sq[:, :n], in_=ps[:, :n], func=AF.Square), r=[kp], w=[ksq])
                        S.act(lambda e, xg=xg, ps=ps, n=n, which=which: e.activation(out=xg[:, :n], in_=ps[:, :n], func=AF.Identity,
                                                                                      scale=self.qkg_sb[:, l, which:which + 1]),
                              r=[kp, "qkg"], w=[kxg])
                        ps2, kp2 = self.next_ps()
                        S.pe(lambda e, ps2=ps2, sq=sq, n=n: e.matmul(ps2[:, :n], lhsT=self.onesh, rhs=sq[:, :n], start=True, stop=True),
                             r=[ksq, "onesh"], w=[kp2])
                        ps3, kp3 = self.next_ps()
                        S.pe(lambda e, ps3=ps3, xg=xg, n=n: e.matmul(ps3[:, :n], lhsT=self.rmat[:], rhs=xg[:, :n], start=True, stop=True),
                             r=[kxg, "rmat"], w=[kp3])
                        S.act(lambda e, rs=rs, ps2=ps2, n=n: e.activation(out=rs[:, :n], in_=ps2[:, :n], func=AF.Sqrt,
                                                                          bias=self.epsc[:, 0:1], scale=1.0), r=[kp2, "epsc"], w=[krs])
                        S.dve(lambda e, rs=rs, n=n: e.reciprocal(out=rs[:, :n], in_=rs[:, :n]), r=[krs], w=[krs])
                        S.dve(lambda e, t1=t1, xg=xg, t0=t0, n=n: e.tensor_tensor(out=t1[:, :n], in0=xg[:, :n], in1=cosb[:, t0:t0 + n], op=ALU.mult),
                              r=[kxg, kcos], w=[kt1])
                        S.dve(lambda e, t2=t2, ps3=ps3, t0=t0, n=n: e.tensor_tensor(out=t2[:, :n], in0=ps3[:, :n], in1=sinb[:, t0:t0 + n], op=ALU.mult),
                              r=[kp3, ksin], w=[kt2])
                        S.pool(lambda e, t1=t1, t2=t2, n=n: e.tensor_tensor(out=t1[:, :n], in0=t1[:, :n], in1=t2[:, :n], op=ALU.add),
                               r=[kt1, kt2], w=[kt1])
                        S.dve(lambda e, dst=dst, t1=t1, rs=rs, t0=t0, n=n: e.tensor_tensor(out=dst[:, t0:t0 + n], in0=t1[:, :n], in1=rs[:, :n], op=ALU.mult),
                              r=[kt1, krs], w=[(kdst, ti)])
                krk = [(kkr, ti) for ti in range(len(tiles))]
                kqk = [(kqr, ti) for ti in range(len(tiles))]
                for pc in range(nkeep // 512 if nkeep >= 512 else 1):
                    w_ = min(512, nkeep)
                    c0 = T - nkeep + pc * w_
                    ko, kko = kof[pc % 2]
                    S.act(lambda e, ko=ko, c0=c0, w_=w_: e.copy(out=ko[:, :w_], in_=kr[:, c0:c0 + w_]), r=krk, w=[kko])
                    S.dma("sp", lambda e, ko=ko, pc=pc, w_=w_, g=g, h=h: e.dma_start(out=self.kT_o[g][l, h, :, pc * w_:(pc + 1) * w_], in_=ko[:, :w_]), r=[kko])
                if self.dbg.get("noattn"):
                    continue
                nb = (T // dil) // 128
                qv = qr[:, 0:T].rearrange("p (a b) -> p a b", b=dil)
                kv_ = kr[:, 0:T].rearrange("p (a b) -> p a b", b=dil)
                Nv = NUM[:, 0:T].rearrange("p (a b) -> p a b", b=dil)
                Dv = DEN[:, 0:T].rearrange("p (a b) -> p a b", b=dil)
                for r in range(dil):
                    prev = None
                    for m in range(nb):
                        psv, kpv = self.next_ps()
                        for k in range(KC if not self.dbg.get("nov") else 0):
                            xv = self.xn[:, k, 0:T].rearrange("p (a b) -> p a b", b=dil)
                            S.pe(lambda e, psv=psv, k=k, xv=xv, m=m, r=r, wv=wv: e.matmul(
                                psv[:, 0:128], lhsT=xv[:, 128 * m:128 * m + 128, r], rhs=wv[:, k, 256:384],
                                start=(k == 0), stop=(k == KC - 1)), r=[kw[2]] + [("xn", k, ti) for ti in range(len(tiles) - 1)], w=[kpv])
                        vb, kvb = vbb[cnt % 3]
                        if m == nb - 1:
                            vf, kvf = vfb[cnt % 2]
                            S.dve(lambda e, vf=vf, psv=psv: e.tensor_copy(out=vf, in_=psv[:, 0:128]), r=[kpv], w=[kvf])
                            S.act(lambda e, vb=vb, vf=vf: e.copy(out=vb, in_=vf), r=[kvf], w=[kvb])
                            S.dma("sp", lambda e, vf=vf, g=g, h=h, r=r: e.dma_start(out=self.v_o[g][l, h, r], in_=vf), r=[kvf])
                        else:
                            S.act(lambda e, vb=vb, psv=psv: e.copy(out=vb, in_=psv[:, 0:128]), r=[kpv], w=[kvb])
                        nq = 256 if m + 1 < nb else 128
                        pss, kps = self.next_ps()
                        S.pe(lambda e, pss=pss, m=m, r=r, nq=nq, kv_=kv_, qv=qv: e.matmul(
                            pss[:, 0:nq], lhsT=kv_[:, 128 * m:128 * m + 128, r], rhs=qv[:, 128 * m:128 * m + nq, r], start=True, stop=True),
                            r=krk + kqk, w=[kps])
                        pe_, kpe = peb[cnt % 2]
                        pt, kpt = ptb[cnt % 3]
                        S.act(lambda e, pe_=pe_, pss=pss, nq=nq: e.activation(out=pe_[:, :nq], in_=pss[:, :nq], func=AF.Exp, scale=SCALE_A), r=[kps], w=[kpe])
                        S.dve(lambda e, pt=pt, pe_=pe_, nq=nq: e.tensor_tensor(out=pt[:, :nq], in0=pe_[:, :nq], in1=maskA[:, :nq], op=ALU.mult),
                              r=[kpe, "cmask"], w=[kpt])
                        if self.dbg.get("nopv"):
                            prev = (vb, kvb, pt, kpt)
                            cnt += 1
                            continue
                        pso, kpo = self.next_ps()
                        for (lo, which) in ((0, "v"), (128, "1")):
                            if prev is not None:
                                pvb, pkvb, ppt, pkpt = prev
                                S.pe(lambda e, pso=pso, lo=lo, which=which, pvb=pvb, ppt=ppt: e.matmul(
                                    pso[:, lo:lo + 128], lhsT=(pvb if which == "v" else self.ones1), rhs=ppt[:, 128:256], start=True, stop=False),
                                    r=[pkvb, pkpt, "ones1"], w=[kpo])
                            S.pe(lambda e, pso=pso, lo=lo, which=which, vb=vb, pt=pt, first=(prev is None): e.matmul(
                                pso[:, lo:lo + 128], lhsT=(vb if which == "v" else self.ones1), rhs=pt[:, 0:128], start=first, stop=True),
                                r=[kvb, kpt, "ones1"], w=[kpo])
                        nvv = Nv[:, 128 * m:128 * m + 128, r]
                        dvv = Dv[:, 128 * m:128 * m + 128, r]
                        if g == 0:
                            S.act(lambda e, nvv=nvv, pso=pso: e.copy(out=nvv, in_=pso[:, 0:128]), r=[kpo], w=[kN])
                            S.act(lambda e, dvv=dvv, pso=pso: e.copy(out=dvv, in_=pso[:, 128:256]), r=[kpo], w=[kD])
                        else:
                            S.dve(lambda e, nvv=nvv, pso=pso: e.tensor_tensor(out=nvv, in0=nvv, in1=pso[:, 0:128], op=ALU.add), r=[kpo, kN], w=[kN])
                            S.dve(lambda e, dvv=dvv, pso=pso: e.tensor_tensor(out=dvv, in0=dvv, in1=pso[:, 128:256], op=ALU.add), r=[kpo, kD], w=[kD])
                        prev = (vb, kvb, pt, kpt)
                        cnt += 1
                if self.dbg.get("nosample"):
                    continue
                qs = qr[:, T:T + 4]
                ks_ = kr[:, T:T + 4]
                psv, kpv = self.next_ps()
                for k in range(KC):
                    S.pe(lambda e, psv=psv, k=k, wv=wv: e.matmul(psv[0:32, 0:128], lhsT=self.xn[:, k, T:T + 32], rhs=wv[:, k, 256:384],
                                                                  start=(k == 0), stop=(k == KC - 1)), r=[kw[2], ("xn", k, len(tiles) - 1), "xnpad"], w=[kpv])
                vnb, kvnb = smb[0]
                vnf, kvnf = smf[0]
                S.dve(lambda e, psv=psv: e.tensor_copy(out=vnf[0:32, :], in_=psv[0:32, 0:128]), r=[kpv], w=[kvnf])
                S.act(lambda e: e.copy(out=vnb[0:32, :], in_=vnf[0:32, :]), r=[kvnf], w=[kvnb])
                S.dma("sp", lambda e, g=g, h=h, nbuf=nbuf: e.dma_start(out=self.kvs_o[g][l, nbuf - 4:nbuf, 1, h, :], in_=vnf[0:4, :]), r=[kvnf])
                psk, kpk = self.next_ps()
                S.pe(lambda e, psk=psk: e.matmul(psk[0:32, 0:128], lhsT=kr[:, T:T + 32], rhs=self.identb[:], start=True, stop=True), r=krk + ["identb", (kkr, "pad")], w=[kpk])
                knf, kknf = smf[1]
                S.act(lambda e, psk=psk: e.copy(out=knf[0:4, :], in_=psk[0:4, 0:128]), r=[kpk], w=[kknf])
                S.dma("sp", lambda e, g=g, h=h, nbuf=nbuf: e.dma_start(out=self.kvs_o[g][l, nbuf - 4:nbuf, 0, h, :], in_=knf[0:4, :]), r=[kknf])
                if h == 0 and not self.dbg.get("noshift"):
                    S.dma("sp", lambda e, g=g, nbuf=nbuf: e.dma_start(out=self.kvs_o[g][l, 0:nbuf - 4], in_=self.cache[g][l, 4:nbuf]))
                if self.dbg.get("s_stage", 9) < 2:
                    continue
                psn, kpn = self.next_ps()
                S.pe(lambda e, psn=psn: e.matmul(psn[0:32, 0:4], lhsT=kr[:, T:T + 32], rhs=qs, start=True, stop=True), r=krk + kqk + [(kkr, "pad")], w=[kpn])
                pnf, kpnf = smf[2]
                pnb, kpnb = smb[1]
                S.act(lambda e, psn=psn: e.activation(out=pnf[0:32, 0:4], in_=psn[0:32, 0:4], func=AF.Exp, scale=SCALE_A), r=[kpn], w=[kpnf])
                mk = cm[0:32, 388:392] if g == 0 else cm[0:32, 392:396]
                S.dve(lambda e, mk=mk: e.tensor_tensor(out=pnb[0:32, 0:4], in0=pnf[0:32, 0:4], in1=mk, op=ALU.mult), r=[kpnf, "cmask"], w=[kpnb])
                pso, kpo = self.next_ps()
                S.pe(lambda e, pso=pso: e.matmul(pso[:, 0:4], lhsT=vnb[0:32, :], rhs=pnb[0:32, 0:4], start=True, stop=True), r=[kvnb, kpnb], w=[kpo])
                S.pe(lambda e, pso=pso: e.matmul(pso[:, 4:8], lhsT=self.ones1[0:32, :], rhs=pnb[0:32, 0:4], start=True, stop=True), r=[kpnb, "ones1"], w=[kpo])
                if g == 0:
                    S.act(lambda e, pso=pso: e.copy(out=NUM[:, T:T + 4], in_=pso[:, 0:4]), r=[kpo], w=[kN])
                    S.act(lambda e, pso=pso: e.copy(out=DEN[:, T:T + 4], in_=pso[:, 4:8]), r=[kpo], w=[kD])
                else:
                    S.dve(lambda e, pso=pso: e.tensor_tensor(out=NUM[:, T:T + 4], in0=NUM[:, T:T + 4], in1=pso[:, 0:4], op=ALU.add), r=[kpo, kN], w=[kN])
                    S.dve(lambda e, pso=pso: e.tensor_tensor(out=DEN[:, T:T + 4], in0=DEN[:, T:T + 4], in1=pso[:, 4:8], op=ALU.add), r=[kpo, kD], w=[kD])
                if self.dbg.get("s_stage", 9) < 3:
                    continue
                cg = self.cache[g][l]
                nblk = 1 if g == 0 else 4
                for bi in range(nblk):
                    rows = cg[0:128] if g == 0 else cg.rearrange("(u s) a h d -> s u a h d", s=dil)[bi]
                    kc, kkc = kcf[bi % 2]
                    vc, kvc = vcf[bi % 2]
                    S.dma("sp", lambda e, kc=kc, rows=rows, h=h: e.dma_start(out=kc, in_=rows[:, 0, h, :]), w=[kkc])
                    S.dma("sp", lambda e, vc=vc, rows=rows, h=h: e.dma_start(out=vc, in_=rows[:, 1, h, :]), w=[kvc])
                    pst, kpt_ = self.next_ps()
                    S.pe(lambda e, pst=pst, kc=kc: e.matmul(pst[:, 0:128], lhsT=kc, rhs=self.identf[:], start=True, stop=True), r=[kkc, "identf"], w=[kpt_])
                    kcT, kkcT = smb[2]
                    vcb, kvcb = smb[3]
                    S.act(lambda e, pst=pst: e.copy(out=kcT, in_=pst[:, 0:128]), r=[kpt_], w=[kkcT])
                    S.dve(lambda e, vc=vc: e.tensor_copy(out=vcb, in_=vc), r=[kvc], w=[kvcb])
                    q0, nqc = 0, 4
                    mkc = maskc0 if g == 0 else cm[:, 396 + 4 * bi:400 + 4 * bi]
                    pss, kps = self.next_ps()
                    S.pe(lambda e, pss=pss, q0=q0, nqc=nqc: e.matmul(pss[:, 0:nqc], lhsT=kcT, rhs=qr[:, T + q0:T + q0 + nqc], start=True, stop=True),
                         r=[kkcT] + kqk, w=[kps])
                    pcf, kpcf = smf[3]
                    pcb, kpcb = smb[4]
                    S.act(lambda e, pss=pss, nqc=nqc: e.activation(out=pcf[:, 0:nqc], in_=pss[:, 0:nqc], func=AF.Exp, scale=SCALE_A), r=[kps], w=[kpcf])
                    S.dve(lambda e, mkc=mkc: e.tensor_tensor(out=pcb[:, 0:4], in0=pcf[:, 0:4], in1=mkc, op=ALU.mult), r=[kpcf, "cmask"], w=[kpcb])
                    pso, kpo = self.next_ps()
                    S.pe(lambda e, pso=pso, nqc=nqc: e.matmul(pso[:, 0:nqc], lhsT=vcb, rhs=pcb[:, 0:nqc], start=True, stop=True), r=[kvcb, kpcb], w=[kpo])
                    S.pe(lambda e, pso=pso, nqc=nqc: e.matmul(pso[:, 4:4 + nqc], lhsT=self.ones1, rhs=pcb[:, 0:nqc], start=True, stop=True), r=[kpcb, "ones1"], w=[kpo])
                    S.dve(lambda e, pso=pso, q0=q0, nqc=nqc: e.tensor_tensor(out=NUM[:, T + q0:T + q0 + nqc], in0=NUM[:, T + q0:T + q0 + nqc], in1=pso[:, 0:nqc], op=ALU.add),
                          r=[kpo, kN], w=[kN])
                    S.dve(lambda e, pso=pso, q0=q0, nqc=nqc: e.tensor_tensor(out=DEN[:, T + q0:T + q0 + nqc], in0=DEN[:, T + q0:T + q0 + nqc], in1=pso[:, 4:4 + nqc], op=ALU.add),
                          r=[kpo, kD], w=[kD])
            S.dve(lambda e: e.reciprocal(out=DEN, in_=DEN), r=[kD], w=[kD])
            S.dve(lambda e, h=h: e.tensor_tensor(out=oav[:, h, :], in0=NUM, in1=DEN, op=ALU.mult), r=[kN, kD], w=[("oall", gen, h)])
        if self.dbg.get("noproj"):
            return
        self.barrier()
        self.ovf_off = 0
        self.rt_list = self.rt_list + [self.af(512) for _ in range(8)]
        self.proj_rmw(wo, 0, 8, lambda c, t0, n, ti: (oav[:, c, t0:t0 + n], ("oall", gen, c)), final=self.last_phase)

    def mlstm_phase(self, l):
        S = self.S
        T, TA = self.T, self.TA
        TP = TA + 28
        tiles = self.tiles
        NCH = T // 128
        self.ov_reset()
        gen = self.ov_gen
        win = self.w_in[l].rearrange("(k p) c -> p k c", p=128)
        wout = self.w_out[l].rearrange("(c p) d -> p c d", p=128)
        A, kA = self.af(TP)
        Bf, kBf = self.af(TP)
        Bc, kBc = self.af(TP)
        lb, klb = Bf, kBf
        Mloc, kMl = self.af(32)
        Mc, kMc = self.af(32)
        Mprev, kMp = self.af(32)
        negM, kNm = self.af(32)
        dec, kdec = self.af(32)
        mout, kmo = self.af(2)
        etok_t, ket = self.af(17 * 8)
        etok = etok_t.rearrange("p (c h) -> p c h", h=8)
        decbc_t, kdb = self.af(8 * 17)
        decbc = decbc_t.rearrange("p (h c) -> p h c", c=17)
        Sf, kSf = self.af(384)
        n0t, kn0 = self.af(1)
        rden = [self.af(128) for _ in range(2)]
        rsb, krsb = self.af(512)
        sgb, ksgb = self.af(512)
        tmpb, ktmp = self.af(512)
        qT, kqT = self.ab(TP)
        kT, kkT = self.ab(TP)
        hT_t, khT = self.ab(2 * TA)
        hT = hT_t.rearrange("p (j t) -> p j t", j=2)
        hn_t, khn = self.ab(2 * TA)
        hn = hn_t.rearrange("p (j t) -> p j t", j=2)
        ktokb = [self.ab(128) for _ in range(2)]
        vxb = [self.ab(384) for _ in range(2)]
        Gmb = [self.ab(128) for _ in range(2)]
        Sb, kSb = self.ab(384)
        sqh_t, ksqh = self.ab(2 * 512)
        sqh = sqh_t.rearrange("p (j t) -> p j t", j=2)
        ones_f1 = self.ones_f[:]
        ones256 = self.cst_b[:, 3, :]
        for buf, kb in ((A, kA), (Bf, kBf), (Bc, kBc)):
            S.dve(lambda e, buf=buf: e.memset(buf[:, TA:TP], 0.0), w=[kb])
        S.dve(lambda e: e.memset(qT[:, TA:TP], 0.0), w=[kqT])
        S.dve(lambda e: e.memset(kT[:, TA:TP], 0.0), w=[kkT])
        wt, kw = self.next_wb()
        wg = wt[:, 0:KC * 64].rearrange("p (k c) -> p k c", k=KC)
        S.dve(lambda e: e.memset(wt[:, 0:KC * 64], 0.0), w=kw)
        S.dma("pool", lambda e: e.dma_start(out=wg[:, :, 0:8], in_=win[:, :, 6144:6152]), w=[kw[0]])
        S.dma("pool", lambda e: e.dma_start(out=wg[:, :, 32:40], in_=win[:, :, 6152:6160]), w=[kw[1]])
        for ti, (t0, n) in enumerate(tiles):
            for gi_, (dst, kd) in enumerate(((A, kA), (Bf, kBf))):
                ps, kp = self.next_ps()
                for k in range(KC):
                    S.pe(lambda e, ps=ps, k=k, gi_=gi_, t0=t0, n=n: e.matmul(
                        ps[0:32, :n], lhsT=wg[:, k, gi_ * 32:gi_ * 32 + 32], rhs=self.xn[:, k, t0:t0 + n],
                        start=(k == 0), stop=(k == KC - 1)), r=[kw[gi_], ("xn", k, ti)], w=[kp])
                S.act(lambda e, ps=ps, dst=dst, gi_=gi_, t0=t0, n=n: e.activation(
                    out=dst[0:32, t0:t0 + n], in_=ps[0:32, :n], func=AF.Tanh, scale=1.0 / 15.0, bias=self.bg_sb[0:32, l, gi_:gi_ + 1]),
                    r=[kp, "bg"], w=[kd])
        S.act(lambda e: e.activation(out=Bf[0:32, 0:TA], in_=Bf[0:32, 0:TA], func=AF.Exp, scale=-15.0), r=[kBf], w=[kBf])
        S.act(lambda e: e.activation(out=Bf[0:32, 0:TA], in_=Bf[0:32, 0:TA], func=AF.Ln, bias=self.ones_f[0:32, 0:1], scale=1.0), r=[kBf, "cst_f"], w=[kBf])
        S.dve(lambda e: e.tensor_scalar(out=A[0:32, 0:TA], in0=A[0:32, 0:TA], scalar1=15.0, scalar2=None, op0=ALU.mult), r=[kA], w=[kA])
        S.dve(lambda e: e.tensor_tensor_scan(out=Bc[0:32, 0:T], data0=Bf[0:32, 0:T], data1=Bf[0:32, 0:T], initial=0.0, op0=ALU.add, op1=ALU.bypass),
              r=[kBf], w=[kBc])
        S.dve(lambda e: e.tensor_tensor_scan(out=Bc[0:32, T:TA], data0=Bf[0:32, T:TA], data1=Bf[0:32, T:TA], initial=0.0, op0=ALU.add, op1=ALU.bypass),
              r=[kBf], w=[kBc])
        S.dve(lambda e: e.tensor_tensor(out=A[0:32, 0:TA], in0=A[0:32, 0:TA], in1=Bc[0:32, 0:TA], op=ALU.add), r=[kA, kBc], w=[kA])
        S.dve(lambda e: e.tensor_reduce(out=Mloc[0:32, 0:NCH], in_=A[0:32, 0:T].rearrange("p (c t) -> p c t", t=128), axis=AX.X, op=ALU.max),
              r=[kA], w=[kMl])
        S.dve(lambda e: e.tensor_reduce(out=Mloc[0:32, 16:17], in_=A[0:32, T:TA], axis=AX.X, op=ALU.max), r=[kA], w=[kMl])
        S.dve(lambda e: e.tensor_scalar(out=Mc[0:32, 0:1], in0=Mloc[0:32, 0:1], scalar1=0.0, scalar2=None, op0=ALU.max), r=[kMl], w=[kMc])
        for c in range(1, NCH):
            S.dve(lambda e, c=c: e.tensor_tensor(out=Mc[0:32, c:c + 1], in0=Mc[0:32, c - 1:c], in1=Mloc[0:32, c:c + 1], op=ALU.max), r=[kMl, kMc], w=[kMc])
        S.dve(lambda e: e.tensor_tensor(out=Mc[0:32, 16:17], in0=Mloc[0:32, 16:17], in1=self.m0_sb[0:32, l:l + 1], op=ALU.max), r=[kMl, "m0"], w=[kMc])
        S.dve(lambda e: e.memset(Mprev[0:32, 0:1], 0.0), w=[kMp])
        S.dve(lambda e: e.tensor_copy(out=Mprev[0:32, 1:NCH], in_=Mc[0:32, 0:NCH - 1]), r=[kMc], w=[kMp])
        S.dve(lambda e: e.tensor_copy(out=Mprev[0:32, 16:17], in_=self.m0_sb[0:32, l:l + 1]), r=["m0"], w=[kMp])
        S.dve(lambda e: e.tensor_tensor(out=dec[0:32, 0:17], in0=Mprev[0:32, 0:17], in1=Mc[0:32, 0:17], op=ALU.subtract), r=[kMp, kMc], w=[kdec])
        S.act(lambda e: e.activation(out=dec[0:32, 0:17], in_=dec[0:32, 0:17], func=AF.Exp), r=[kdec], w=[kdec])
        S.dve(lambda e: e.tensor_scalar(out=negM[0:32, 0:17], in0=Mc[0:32, 0:17], scalar1=-1.0, scalar2=None, op0=ALU.mult), r=[kMc], w=[kNm])
        S.dve(lambda e: e.tensor_tensor(out=mout[0:32, 0:1], in0=Mc[0:32, NCH - 1:NCH], in1=Bc[0:32, T - 1:T], op=ALU.subtract), r=[kMc, kBc], w=[kmo])
        S.dve(lambda e: e.tensor_tensor(out=mout[0:32, 1:2], in0=Mc[0:32, 16:17], in1=Bc[0:32, TA - 1:TA], op=ALU.subtract), r=[kMc, kBc], w=[kmo])
        S.dma("sp", lambda e: e.dma_start(out=self.m_o[l, 0], in_=mout[0:8, 0:1]), r=[kmo])
        S.dma("sp", lambda e: e.dma_start(out=self.m_o[l, 1], in_=mout[0:8, 1:2]), r=[kmo])
        if self.dbg.get("mstage", 9) < 2:
            return
        chunks = [(c * 128, 128, c) for c in range(NCH)] + [(T, TS, 16)]
        for (c0, L, c) in chunks:
            S.act(lambda e, c0=c0, L=L, c=c: e.activation(out=A[0:32, c0:c0 + L], in_=A[0:32, c0:c0 + L], func=AF.Exp, bias=negM[0:32, c:c + 1], scale=1.0),
                  r=[kA, kNm], w=[kA])
            S.act(lambda e, c0=c0, L=L, c=c: e.activation(out=Bc[0:32, c0:c0 + L], in_=Bc[0:32, c0:c0 + L], func=AF.Exp, bias=negM[0:32, c:c + 1], scale=1.0),
                  r=[kBc, kNm], w=[kBc])
        pse, kpe = self.next_ps()
        for (c0, L, c) in chunks:
            Lm = 128 if L == 128 else 32
            S.pe(lambda e, c0=c0, Lm=Lm, c=c: e.matmul(pse[0:Lm, c * 8:(c + 1) * 8], lhsT=A[0:32, c0:c0 + Lm], rhs=self.identf[0:32, 0:8], start=True, stop=True),
                 r=[kA, "identf"], w=[kpe])
        S.act(lambda e: e.copy(out=etok_t[:, 0:NCH * 8], in_=pse[:, 0:NCH * 8]), r=[kpe], w=[ket])
        S.act(lambda e: e.copy(out=etok_t[0:32, 128:136], in_=pse[0:32, 128:136]), r=[kpe], w=[ket])
        psd, kpd = self.next_ps()
        for h in range(8):
            S.pe(lambda e, h=h: e.matmul(psd[:, h * 17:(h + 1) * 17], lhsT=self.sel[0:32, h, :], rhs=dec[0:32, 0:17], start=True, stop=True),
                 r=[kdec, "sel"], w=[kpd])
        S.act(lambda e: e.copy(out=decbc_t[:, 0:136], in_=psd[:, 0:136]), r=[kpd], w=[kdb])
        if self.dbg.get("mstage", 9) < 3:
            return
        scale_k = 128 ** -0.5
        cnt = 0
        for h in range(self.dbg.get("mheads", 8)):
            w1t, kw1 = self.next_wb()
            w1 = w1t[:].rearrange("p (k c) -> p k c", k=KC)
            S.dma("pool", lambda e, h=h: e.dma_start(out=w1[:, :, 0:128], in_=win[:, :, h * 128:h * 128 + 128]), w=kw1)
            S.dma("pool", lambda e, h=h: e.dma_start(out=w1[:, :, 128:256], in_=win[:, :, 1024 + h * 128:1024 + h * 128 + 128]), w=[kw1[1]])
            S.dma("pool", lambda e, h=h: e.dma_start(out=w1[:, :, 256:512], in_=win[:, :, 4096 + h * 256:4096 + h * 256 + 256]), w=[kw1[2]])
            w2t, kw2 = self.next_wb()
            w2 = w2t[:, 0:KC * 384].rearrange("p (k c) -> p k c", k=KC)
            S.dma("pool", lambda e, h=h: e.dma_start(out=w2[:, :, 0:128], in_=win[:, :, 1024 + h * 128:1024 + h * 128 + 128]), w=kw2)
            S.dma("pool", lambda e, h=h: e.dma_start(out=w2[:, :, 128:384], in_=win[:, :, 2048 + h * 256:2048 + h * 256 + 256]), w=[kw2[1]])
            for which, (dst, kd) in enumerate(((qT, kqT), (kT, kkT))):
                for ti, (t0, n) in enumerate(tiles):
                    ps, kp = self.next_ps()
                    for k in range(KC):
                        S.pe(lambda e, ps=ps, k=k, which=which, t0=t0, n=n, w1=w1: e.matmul(
                            ps[:, :n], lhsT=w1[:, k, which * 128:which * 128 + 128], rhs=self.xn[:, k, t0:t0 + n],
                            start=(k == 0), stop=(k == KC - 1)), r=[kw1[which], ("xn", k, ti)], w=[kp])
                    if which == 0:
                        S.act(lambda e, ps=ps, t0=t0, n=n: e.copy(out=qT[:, t0:t0 + n], in_=ps[:, :n]), r=[kp], w=[kqT])
                    else:
                        S.act(lambda e, ps=ps, t0=t0, n=n: e.mul(out=kT[:, t0:t0 + n], in_=ps[:, :n], mul=scale_k), r=[kp], w=[kkT])
            for ti, (t0, n) in enumerate(tiles):
                ps, kp = self.next_ps()
                S.pe(lambda e, ps=ps, h=h, t0=t0, n=n: e.matmul(ps[:, :n], lhsT=self.sel[0:32, h, :], rhs=Bc[0:32, t0:t0 + n], start=True, stop=True),
                     r=[kBc, "sel"], w=[kp])
                S.act(lambda e, ps=ps, t0=t0, n=n: e.copy(out=lb[:, t0:t0 + n], in_=ps[:, :n]), r=[kp], w=[klb])
            def stage1(c0, L, c, h=h, w2=w2, kw2=kw2):
                Lm = 128 if c != 16 else 32
                pkv, kpkv = self.next_ps()
                for k in range(KC):
                    S.pe(lambda e, pkv=pkv, k=k, c0=c0, Lm=Lm, w2=w2: e.matmul(
                        pkv[0:Lm, 0:384], lhsT=self.xn[:, k, c0:c0 + Lm], rhs=w2[:, k, 0:384], start=(k == 0), stop=(k == KC - 1)),
                        r=[kw2[0], kw2[1], ("xn", k, c0 // 512), "xnpad"], w=[kpkv])
                ktok, kkt = ktokb[c % 2]
                vx, kvx = vxb[c % 2]
                Gm, kGm = Gmb[c % 2]
                S.dve(lambda e, ktok=ktok, pkv=pkv, Lm=Lm: e.tensor_scalar(out=ktok[0:Lm, :], in0=pkv[0:Lm, 0:128], scalar1=scale_k, scalar2=None, op0=ALU.mult),
                      r=[kpkv], w=[kkt])
                S.dve(lambda e, vx=vx, pkv=pkv, Lm=Lm, c=c, h=h: e.tensor_scalar(out=vx[0:Lm, 0:256], in0=pkv[0:Lm, 128:384], scalar1=etok[0:Lm, c, h:h + 1],
                                                                               scalar2=None, op0=ALU.mult), r=[kpkv, ket], w=[kvx])
                S.dve(lambda e, vx=vx, Lm=Lm, c=c, h=h: e.tensor_scalar(out=vx[0:Lm, 256:384], in0=ones_f1[0:Lm, :], scalar1=etok[0:Lm, c, h:h + 1],
                                                                      scalar2=None, op0=ALU.mult), r=[ket, "cst_f"], w=[kvx])
                pg, kpg = self.next_ps()
                S.pe(lambda e, pg=pg, c0=c0, Lm=Lm, L=L: e.matmul(pg[0:Lm, 0:L], lhsT=kT[:, c0:c0 + Lm], rhs=qT[:, c0:c0 + L], start=True, stop=True),
                     r=[kkT, kqT], w=[kpg])
                S.dve(lambda e, Gm=Gm, pg=pg, Lm=Lm, L=L: e.tensor_tensor(out=Gm[0:Lm, 0:L], in0=pg[0:Lm, 0:L], in1=self.cmask2[0:Lm, 0:L], op=ALU.mult),
                      r=[kpg, "cmask2"], w=[kGm])

            def stage2(c0, L, c, h=h):
                sample = (c == 16)
                Lm = 128 if not sample else 32
                has_state = sample or c > 0
                ktok, kkt = ktokb[c % 2]
                vx, kvx = vxb[c % 2]
                Gm, kGm = Gmb[c % 2]
                rd, krd = rden[c % 2]
                if sample:
                    S.dma("sp", lambda e, h=h: e.dma_start(out=self.CT_o[l, 0, h], in_=Sf[:, 0:256]), r=[kSf])
                    S.dma("sp", lambda e, h=h: e.dma_start(out=self.n_o[l, 0, h], in_=Sf[:, 256:257]), r=[kSf])
                    S.dma("sp", lambda e, h=h: e.dma_start(out=Sf[:, 0:256], in_=self.C0T[l, h]), w=[kSf])
                    S.dma("sp", lambda e, h=h: e.dma_start(out=n0t[:, 0:1], in_=self.n0[l, h]), w=[kn0])
                    S.dve(lambda e: e.tensor_scalar(out=Sf[:, 256:384], in0=ones_f1, scalar1=n0t[:, 0:1], scalar2=None, op0=ALU.mult),
                          r=[kn0, "cst_f", kSf], w=[kSf])
                    S.act(lambda e, h=h: e.activation(out=Sb, in_=Sf, func=AF.Identity, scale=decbc[:, h, 16:17]), r=[kSf, kdb], w=[kSb])
                pn, kpn = self.next_ps()
                for j in range(3):
                    S.pe(lambda e, pn=pn, j=j, vx=vx, Gm=Gm, Lm=Lm, L=L, hs=has_state: e.matmul(
                        pn[:, j * 128:j * 128 + L], lhsT=vx[0:Lm, j * 128:(j + 1) * 128], rhs=Gm[0:Lm, 0:L], start=True, stop=(not hs)),
                        r=[kvx, kGm], w=[kpn])
                    if has_state:
                        S.pe(lambda e, pn=pn, j=j, c0=c0, L=L: e.matmul(
                            pn[:, j * 128:j * 128 + L], lhsT=Sb[:, j * 128:(j + 1) * 128], rhs=qT[:, c0:c0 + L], start=False, stop=True),
                            r=[kSb, kqT], w=[kpn])
                S.act(lambda e, rd=rd, pn=pn, L=L: e.activation(out=rd[:, 0:L], in_=pn[:, 256:256 + L], func=AF.Abs), r=[kpn], w=[krd])
                S.dve(lambda e, rd=rd, c0=c0, L=L: e.tensor_tensor(out=rd[:, 0:L], in0=rd[:, 0:L], in1=lb[:, c0:c0 + L], op=ALU.max),
                      r=[krd, klb], w=[krd])
                S.dve(lambda e, rd=rd, L=L: e.reciprocal(out=rd[:, 0:L], in_=rd[:, 0:L]), r=[krd], w=[krd])
                for j in range(2):
                    S.dve(lambda e, rd=rd, pn=pn, j=j, c0=c0, L=L: e.tensor_tensor(out=hT[:, j, c0:c0 + L], in0=pn[:, j * 128:j * 128 + L], in1=rd[:, 0:L], op=ALU.mult),
                          r=[kpn, krd], w=[(khT, c0 // 512)])
                pS, kpS = self.next_ps()
                S.pe(lambda e, pS=pS, ktok=ktok, vx=vx, Lm=Lm: e.matmul(pS[:, 0:384], lhsT=ktok[0:Lm, :], rhs=vx[0:Lm, 0:384], start=True, stop=True),
                     r=[kkt, kvx], w=[kpS])
                if not has_state:
                    S.act(lambda e, pS=pS: e.copy(out=Sf, in_=pS[:, 0:384]), r=[kpS], w=[kSf])
                else:
                    S.dve(lambda e, pS=pS, h=h, c=c: e.scalar_tensor_tensor(out=Sf, in0=Sf, scalar=decbc[:, h, c:c + 1], in1=pS[:, 0:384],
                                                                            op0=ALU.mult, op1=ALU.add), r=[kpS, kSf, kdb], w=[kSf])
                if c < NCH - 1:
                    S.act(lambda e, h=h, c=c: e.activation(out=Sb, in_=Sf, func=AF.Identity, scale=decbc[:, h, c + 1:c + 2]), r=[kSf, kdb], w=[kSb])
                if sample:
                    S.dma("sp", lambda e, h=h: e.dma_start(out=self.CT_o[l, 1, h], in_=Sf[:, 0:256]), r=[kSf])
                    S.dma("sp", lambda e, h=h: e.dma_start(out=self.n_o[l, 1, h], in_=Sf[:, 256:257]), r=[kSf])

            stage1(*chunks[0])
            for ci in range(len(chunks)):
                if ci + 1 < len(chunks):
                    stage1(*chunks[ci + 1])
                stage2(*chunks[ci])
            for ti, (t0, n) in enumerate(tiles if not self.dbg.get("nonorm") else []):
                S.act(lambda e, t0=t0, n=n: e.activation(out=sqh[:, :, :n], in_=hT[:, :, t0:t0 + n], func=AF.Square), r=[(khT, ti)], w=[ksqh])
                pss, kps = self.next_ps()
                for j in range(2):
                    S.pe(lambda e, pss=pss, j=j, n=n: e.matmul(pss[:, :n], lhsT=ones256, rhs=sqh[:, j, :n], start=(j == 0), stop=(j == 1)),
                         r=[ksqh, "ones_b"], w=[kps])
                S.act(lambda e, pss=pss, n=n: e.activation(out=rsb[:, :n], in_=pss[:, :n], func=AF.Sqrt, bias=self.epsc[:, 0:1], scale=1.0),
                      r=[kps, "epsc"], w=[krsb])
                S.dve(lambda e, n=n: e.reciprocal(out=rsb[:, :n], in_=rsb[:, :n]), r=[krsb], w=[krsb])
                for j in range(2):
                    pog, kpog = self.next_ps()
                    for k in range(KC):
                        S.pe(lambda e, pog=pog, k=k, j=j, t0=t0, n=n, w1=w1: e.matmul(
                            pog[:, :n], lhsT=w1[:, k, 256 + j * 128:384 + j * 128], rhs=self.xn[:, k, t0:t0 + n],
                            start=(k == 0), stop=(k == KC - 1)), r=[kw1[2], ("xn", k, ti)], w=[kpog])
                    S.act(lambda e, pog=pog, n=n: e.activation(out=sgb[:, :n], in_=pog[:, :n], func=AF.Sigmoid), r=[kpog], w=[ksgb])
                    S.dve(lambda e, j=j, t0=t0, n=n, h=h: e.scalar_tensor_tensor(out=tmpb[:, :n], in0=hT[:, j, t0:t0 + n], scalar=self.gh_sb[:, l, 2 * h + j:2 * h + j + 1],
                                                                                in1=rsb[:, :n], op0=ALU.mult, op1=ALU.mult), r=[(khT, ti), krsb, "gh"], w=[ktmp])
                    S.dve(lambda e, j=j, t0=t0, n=n: e.tensor_tensor(out=hn[:, j, t0:t0 + n], in0=tmpb[:, :n], in1=sgb[:, :n], op=ALU.mult),
                          r=[ktmp, ksgb], w=[(khn, ti)])
            if self.dbg.get("noproj"):
                continue
            self.proj_rmw(wout, 2 * h, 2, lambda c, t0, n, ti: (hn[:, c, t0:t0 + n], (khn, ti)), final=(self.last_phase and h == 7))


def _consts(T):
    TA = T + TS
    half = 64
    inv_freq = (10000.0 ** (-np.arange(half, dtype=np.float32) / half)).astype(np.float32)
    pos = np.concatenate([np.arange(T), PAST_LEN + np.arange(TS)]).astype(np.float32)
    ang = (pos[None, :] * np.tile(inv_freq, 2)[:, None]).astype(np.float32)
    rope = np.stack([np.cos(ang), np.sin(ang)]).astype(np.float32)
    rmat = np.zeros((128, 128), np.float32)
    for p in range(64):
        rmat[p + 64, p] = -1.0
        rmat[p, p + 64] = 1.0
    cm = np.zeros((128, NCM), np.float32)
    i = np.arange(128)[:, None]
    j = np.arange(256)[None, :]
    cm[:, 0:256] = ((j - i >= 0) & (j - i <= 128)).astype(np.float32)
    t = np.arange(4)[None, :]
    cm[:, 384:388] = (i >= t).astype(np.float32)
    tp = np.arange(4)[:, None]
    cm[0:4, 388:392] = (tp <= t).astype(np.float32)
    cm[0:4, 392:396] = (tp == t).astype(np.float32)
    for bi in range(4):
        cm[:, 396 + 4 * bi + bi] = 1.0
    cst = np.stack([np.full((128, 128), 1.0 / 2048), np.full((128, 128), 1.0 / 128), np.ones((128, 128)), np.full((128, 128), 1.0 / 256)], 1).astype(np.float32)
    sel = np.zeros((32, 8, 128), np.float32)
    for h in range(8):
        sel[h, h, :] = 1.0
    cm2 = (np.arange(128)[:, None] <= np.arange(128)[None, :]).astype(np.float32)
    return dict(rope=rope, rmat=rmat, cmask=cm, cst=cst, ident=np.eye(128, dtype=np.float32), sel=sel, cmask2=cm2)


def prep_shared(inp, T=2048):
    f = lambda a: np.ascontiguousarray(np.asarray(a, dtype=np.float32))
    d = _consts(T)
    nm, nf = f(inp["norm_mix"]), f(inp["norm_ffn"])
    d["nrm"] = f(np.concatenate([nm, nf], 0).reshape(8, 16, 128).transpose(2, 0, 1))
    d["w_up"] = f(inp["ffn_w_up"])
    d["w_down"] = f(inp["ffn_w_down"])
    d["convw"] = f(f(inp["ffn_conv_w"]).reshape(4, 3, 88, 128).transpose(0, 3, 2, 1))
    d["convb"] = f(f(inp["ffn_conv_b"]).reshape(4, 88, 128).transpose(0, 2, 1))
    d["w_qkv"] = f(inp["attn_w_qkv"])
    d["w_o"] = f(inp["attn_w_o"])
    d["qkg"] = f(np.stack([f(inp["attn_q_norm"]), f(inp["attn_k_norm"])], -1).transpose(1, 0, 2))
    d["w_in"] = f(inp["mlstm_w_in"])
    d["w_out"] = f(inp["mlstm_w_out"])
    bgs = f(inp["mlstm_b_gates"])
    bgp = np.zeros((32, 2, 2), np.float32)
    bgp[:8] = np.stack([bgs[:, :8], bgs[:, 8:]], -1).transpose(1, 0, 2)
    d["bg"] = bgp
    d["gh"] = f(f(inp["mlstm_norm_h"]).reshape(2, 16, 128).transpose(2, 0, 1))
    return d


def prep_core(inp, c, T=2048):
    f = lambda a: np.ascontiguousarray(np.asarray(a, dtype=np.float32))
    d = {}
    xp = np.asarray(inp["x_prompt"])[c % 4, :T]
    xs = np.asarray(inp["x_sample"])[c]
    d["xT"] = f(np.concatenate([xp.T, xs.T], 1))
    d["sconv"] = f(f(inp["state_ffn_conv"])[:, c].reshape(4, 2, 88, 128).transpose(0, 3, 2, 1))
    for g, nm in enumerate(("cache_kv_w128", "cache_kv_w512", "cache_kv_w2048")):
        d["cache%d" % g] = f(np.asarray(inp[nm])[:, c])
    d["C0T"] = f(np.asarray(inp["state_mlstm_C"])[:, c].transpose(0, 1, 3, 2))
    d["n0"] = f(np.asarray(inp["state_mlstm_n"])[:, c][..., None])
    m0p = np.zeros((32, 2), np.float32)
    m0p[:8] = np.asarray(inp["state_mlstm_m"])[:, c].T
    d["m0"] = m0p
    return d


_CACHE = {}


def kernel(**inputs):
    T = 2048
    if "prog" not in _CACHE:
        P = Prog(T=T)
        P.build()
        _CACHE["prog"] = P
    P = _CACHE["prog"]
    shared = prep_shared(inputs, T)
    in_maps = []
    for c in range(8):
        d = dict(shared)
        d.update(prep_core(inputs, c, T))
        in_maps.append({k: d[k] for k in P.ins})
    res = run_bass_kernel_spmd(P.nc, in_maps, core_ids=list(range(8)))
    R = res.results
    return assemble(R, T)


def assemble(R, T=2048):
    f32 = np.float32
    yp = np.stack([R[b]["yT"][:, :T].T for b in range(4)]).astype(f32)
    ys = np.stack([R[c]["yT"][:, T:].T for c in range(8)]).astype(f32)
    outs = [yp, ys]
    if "kT_o0" not in R[0]:
        z = lambda *sh: np.zeros(sh, f32)
        for g in range(3):
            outs += [z(NLA, 4, min(NBUF[g], T), 2, 8, 128), z(NLA, 8, NBUF[g], 2, 8, 128)]
        outs += [z(NLB, 4, 8, 256, 128), z(NLB, 8, 8, 256, 128), z(NLB, 4, 8, 128), z(NLB, 8, 8, 128), z(NLB, 4, 8), z(NLB, 8, 8)]
        cv = lambda a: a.transpose(0, 3, 2, 1).reshape(NL, 2, 2 * FF)
        outs += [np.stack([cv(R[b]["convp_o"]) for b in range(4)], 1).astype(f32),
                 np.stack([cv(R[c]["convs_o"]) for c in range(8)], 1).astype(f32)]
        return tuple(outs)
    for g in range(3):
        dil, nb = DILS[g], NBUF[g]
        nkeep = min(nb, T)
        kvp = np.zeros((NLA, 4, nkeep, 2, 8, 128), f32)
        for b in range(4):
            kT = R[b]["kT_o%d" % g]
            kvp[:, b, :, 0] = kT.transpose(0, 3, 1, 2)
            v = R[b]["v_o%d" % g]
            kvp[:, b, :, 1] = v.transpose(0, 3, 2, 1, 4).reshape(NLA, nkeep, 8, 128)
        kvs = np.stack([R[c]["kvs_o%d" % g] for c in range(8)], 1).astype(f32)
        outs += [kvp, kvs]
    CT = [np.stack([R[b]["CT_o"][:, 0] for b in range(4)], 1), np.stack([R[c]["CT_o"][:, 1] for c in range(8)], 1)]
    outs += [np.ascontiguousarray(x.transpose(0, 1, 2, 4, 3)).astype(f32) for x in CT]
    outs += [np.stack([R[b]["n_o"][:, 0, :, :, 0] for b in range(4)], 1).astype(f32),
             np.stack([R[c]["n_o"][:, 1, :, :, 0] for c in range(8)], 1).astype(f32)]
    outs += [np.stack([R[b]["m_o"][:, 0, :, 0] for b in range(4)], 1).astype(f32),
             np.stack([R[c]["m_o"][:, 1, :, 0] for c in range(8)], 1).astype(f32)]
    cv = lambda a: a.transpose(0, 3, 2, 1).reshape(NL, 2, 2 * FF)
    outs += [np.stack([cv(R[b]["convp_o"]) for b in range(4)], 1).astype(f32),
             np.stack([cv(R[c]["convs_o"]) for c in range(8)], 1).astype(f32)]
    return tuple(outs)
```

```python
import numpy as np
import ml_dtypes
from contextlib import ExitStack
import concourse.bass as bass
import concourse.mybir as mybir
from concourse.bass_utils import run_bass_kernel_spmd

F32 = mybir.dt.float32
BF16 = mybir.dt.bfloat16
AF = mybir.ActivationFunctionType
ALU = mybir.AluOpType
AX = mybir.AxisListType

ENGS = ("pe", "act", "dve", "pool", "sp")
NSLOT = 12


class Sched:
    def __init__(self, nc):
        self.nc = nc
        self.ops = []

    def op(self, eng, fn, r=(), w=(), dma=False, drain=False):
        self.ops.append((eng, fn, tuple(r), tuple(w), dma, drain))

    def pe(self, fn, r=(), w=()):
        self.op("pe", fn, r, w)

    def act(self, fn, r=(), w=()):
        self.op("act", fn, r, w)

    def dve(self, fn, r=(), w=()):
        self.op("dve", fn, r, w)

    def pool(self, fn, r=(), w=()):
        self.op("pool", fn, r, w)

    def dma(self, eng, fn, r=(), w=()):
        self.op(eng, fn, r, w, True)

    def emit(self, stack):
        nc = self.nc
        ops = self.ops
        n = len(ops)
        pos = [0] * n
        streams = {e: [] for e in ENGS}
        for i, o in enumerate(ops):
            pos[i] = len(streams[o[0]])
            streams[o[0]].append(i)
        last_w = {}
        readers = {}
        waits = [[] for _ in range(n)]
        signal = [False] * n
        waited = {e: {} for e in ENGS}
        dwaited = {e: set() for e in ENGS}
        dslot = {}
        dval = {}
        dpre = {}
        dcount = {e: 0 for e in ENGS}
        duses = {e: [0] * NSLOT for e in ENGS}
        drainvals = {}
        for i, o in enumerate(ops):
            eng, fn, R, W, dma, drain = o
            drainvals[i] = list(duses[eng]) if drain else None
            deps = set()
            for b in R:
                if b in last_w:
                    deps.add(last_w[b])
            for b in W:
                if b in last_w:
                    deps.add(last_w[b])
                for r_ in readers.get(b, ()):
                    deps.add(r_)
            deps.discard(i)
            for j in sorted(deps):
                oj = ops[j]
                if oj[4]:
                    if j in dwaited[eng]:
                        continue
                    dwaited[eng].add(j)
                    waits[i].append(j)
                else:
                    k = oj[0]
                    if k == eng and eng == "pe":
                        continue
                    if waited[eng].get(k, -1) >= pos[j]:
                        continue
                    waited[eng][k] = pos[j]
                    waits[i].append(j)
                    signal[j] = True
            for b in R:
                readers.setdefault(b, []).append(i)
            for b in W:
                last_w[b] = i
                readers[b] = []
            if dma:
                s = dcount[eng] % NSLOT
                dcount[eng] += 1
                dslot[i] = (eng, s)
                dpre[i] = 16 * duses[eng][s]
                duses[eng][s] += 1
                dval[i] = 16 * duses[eng][s]
                if dpre[i] > 0:
                    pass
        rank = {}
        for e in ENGS:
            c = 0
            for i in streams[e]:
                if signal[i]:
                    c += 1
                    rank[i] = c
        esem = {e: stack.enter_context(nc.semaphore("es_" + e)) for e in ENGS}
        dsem = {}
        for e in ENGS:
            for s in range(min(NSLOT, dcount[e])):
                dsem[(e, s)] = stack.enter_context(nc.semaphore("ds_%s_%d" % (e, s)))
        self.n_wait = sum(len(w) for w in waits)
        block = stack.enter_context(nc.Block())

        def run_stream(engname, engobj):
            for i in streams[engname]:
                eng, fn, R, W, dma, drain = ops[i]
                if drain:
                    for s_ in range(NSLOT):
                        if drainvals[i][s_] > 0:
                            engobj.wait_ge(dsem[(engname, s_)], 16 * drainvals[i][s_])
                for j in waits[i]:
                    if ops[j][4]:
                        engobj.wait_ge(dsem[dslot[j]], dval[j])
                    else:
                        engobj.wait_ge(esem[ops[j][0]], rank[j])
                if dma:
                    if dpre[i] > 0:
                        engobj.wait_ge(dsem[dslot[i]], dpre[i])
                    ins = fn(engobj)
                    ins.then_inc(dsem[dslot[i]], 16)
                else:
                    ins = fn(engobj)
                    if signal[i]:
                        ins.then_inc(esem[eng], 1)
            for s in range(NSLOT):
                if duses[engname][s] > 0:
                    engobj.wait_ge(dsem[(engname, s)], 16 * duses[engname][s])

        @block.tensor
        def _(e):
            run_stream("pe", e)

        @block.scalar
        def _(e):
            run_stream("act", e)

        @block.vector
        def _(e):
            run_stream("dve", e)

        @block.gpsimd
        def _(e):
            run_stream("pool", e)

        @block.sync
        def _(e):
            run_stream("sp", e)


DM = 2048
KC = 16
FF = 5632
FC = 44
NL = 4
NLA = 2
NLB = 2
TS = 4
EPS = 1e-6
GRP = 8
NWB = 2
PAST_LEN = 16384
DILS = (1, 4, 16)
NBUF = (128, 512, 2048)
SCALE_A = 128 ** -0.5
OVF = 10240
OVB = 28672
NCM = 256 + 128 + 4 + 4 + 4 + 16


class Prog:
    def __init__(self, T=2048, plan=None, dbg=None):
        self.T = T
        self.TA = T + TS
        self.tiles = [(i * 512, 512) for i in range(T // 512)] + [(T, TS)]
        self.ntile = [(i * 128, 128) for i in range(T // 128)] + [(T, TS)]
        self.plan = plan if plan is not None else [("attn", 0), ("ffn", 0), ("mlstm", 0), ("ffn", 1),
                                                   ("attn", 1), ("ffn", 2), ("mlstm", 1), ("ffn", 3)]
        self.dbg = dbg or {}
        self.nc = bass.Bass("TRN2", target_bir_lowering=False)
        self.ins = {}
        self.outs = {}

    def din(self, name, shape, dt=F32):
        t = self.nc.dram_tensor(name, list(shape), dt, kind="ExternalInput").ap()
        self.ins[name] = t
        return t

    def dout(self, name, shape, dt=F32):
        t = self.nc.dram_tensor(name, list(shape), dt, kind="ExternalOutput").ap()
        self.outs[name] = t
        return t

    def sb(self, name, shape, dt):
        return self.st.enter_context(self.nc.sbuf_tensor(name, list(shape), dt))

    def ov_reset(self):
        self.barrier()
        self.rt_list = [(self.rt[i][:], ("rt", i)) for i in range(len(self.rt))]
        self.ovf_off = 0
        self.ovb_off = 0
        self.ov_gen += 1
        self.ov_cnt = 0

    def af(self, n):
        v = self.ovf[:, self.ovf_off:self.ovf_off + n]
        self.ovf_off += n
        assert self.ovf_off <= OVF, self.ovf_off
        self.ov_cnt += 1
        return v, ("o", self.ov_gen, self.ov_cnt)

    def ab(self, n):
        n = (n + 1) // 2 * 2
        v = self.ovb[:, self.ovb_off:self.ovb_off + n]
        self.ovb_off += n
        assert self.ovb_off <= OVB, self.ovb_off
        self.ov_cnt += 1
        return v, ("o", self.ov_gen, self.ov_cnt)

    def barrier(self):
        S = self.S
        self.bar_id += 1
        b = self.bar_id
        sc = self.bscr
        S.op("pe", lambda e: e.matmul(self.ps[7][0:1, 0:1], lhsT=self.ones1[0:1, 0:1], rhs=self.ones1[0:1, 0:1], start=True, stop=True),
             r=["ones1"], w=[("bar", b, "pe"), ("ps", 7)])
        S.op("act", lambda e: e.copy(out=sc[0:1, 0:1], in_=self.epsc[0:1, 0:1]), r=["epsc"], w=[("bar", b, "act"), ("bscr", 0)], drain=True)
        S.op("dve", lambda e: e.memset(sc[0:1, 1:2], 0.0), w=[("bar", b, "dve"), ("bscr", 1)])
        S.op("pool", lambda e: e.memset(sc[0:1, 2:3], 0.0), w=[("bar", b, "pool"), ("bscr", 2)], drain=True)
        S.op("sp", lambda e: e.dma_start(out=sc[0:1, 4:5], in_=self.epsc[0:1, 0:1]), r=["epsc"], w=[("bar", b, "sp"), ("bscr", 4)], dma=True, drain=True)
        allk = [("bar", b, e_) for e_ in ENGS]
        S.op("pe", lambda e: e.matmul(self.ps[7][0:1, 0:1], lhsT=self.ones1[0:1, 0:1], rhs=self.ones1[0:1, 0:1], start=True, stop=True),
             r=allk + ["ones1"], w=[("ps", 7)])
        S.op("act", lambda e: e.copy(out=sc[0:1, 0:1], in_=self.epsc[0:1, 0:1]), r=allk + ["epsc"], w=[("bscr", 0)])
        S.op("dve", lambda e: e.memset(sc[0:1, 1:2], 0.0), r=allk, w=[("bscr", 1)])
        S.op("pool", lambda e: e.memset(sc[0:1, 2:3], 0.0), r=allk, w=[("bscr", 2)])
        S.op("sp", lambda e: e.dma_start(out=sc[0:1, 5:6], in_=self.epsc[0:1, 0:1]), r=allk + ["epsc"], w=[("bscr", 5)], dma=True)

    def build(self):
        nc = self.nc
        T, TA = self.T, self.TA
        with ExitStack() as st:
            self.st = st
            self.S = Sched(nc)
            S = self.S
            self.bar_id = 0
            self.ov_gen = 0
            kinds = set(k for k, _ in self.plan)
            self.xT = self.din("xT", [DM, TA])
            self.yT = self.dout("yT", [DM, TA])
            self.rs = nc.dram_tensor("rs", [DM, TA], F32, kind="Internal").ap()
            self.res_dst = self.rs
            self.nrm = self.din("nrm", [128, 2 * NL, KC])
            self.cst = self.din("cst", [128, 4, 128])
            self.ident_d = self.din("ident", [128, 128])
            if "ffn" in kinds:
                self.w_up = self.din("w_up", [NL, DM, 2 * FF])
                self.w_down = self.din("w_down", [NL, FF, DM])
                self.convw = self.din("convw", [NL, 128, 2 * FC, 3])
                self.convb = self.din("convb", [NL, 128, 2 * FC])
                self.sconv = self.din("sconv", [NL, 128, 2 * FC, 2])
                self.convp_o = self.dout("convp_o", [NL, 128, 2 * FC, 2])
                self.convs_o = self.dout("convs_o", [NL, 128, 2 * FC, 2])
            if "attn" in kinds:
                self.w_qkv = self.din("w_qkv", [NLA, DM, 9216])
                self.w_o = self.din("w_o", [NLA, 1024, DM])
                self.qkg = self.din("qkg", [128, NLA, 2])
                self.rope = self.din("rope", [2, 128, TA])
                self.rmat_d = self.din("rmat", [128, 128])
                self.cmask_d = self.din("cmask", [128, NCM])
                self.cache = [self.din("cache%d" % g, [NLA, NBUF[g], 2, 8, 128]) for g in range(3)]
                self.kT_o = [self.dout("kT_o%d" % g, [NLA, 8, 128, min(NBUF[g], T)]) for g in range(3)]
                self.v_o = [self.dout("v_o%d" % g, [NLA, 8, DILS[g], 128, 128]) for g in range(3)]
                self.kvs_o = [self.dout("kvs_o%d" % g, [NLA, NBUF[g], 2, 8, 128]) for g in range(3)]
            if "mlstm" in kinds:
                self.w_in = self.din("w_in", [NLB, DM, 6160])
                self.w_out = self.din("w_out", [NLB, DM, DM])
                self.bg = self.din("bg", [32, NLB, 2])
                self.gh = self.din("gh", [128, NLB, 16])
                self.sel_d = self.din("sel", [32, 8, 128])
                self.C0T = self.din("C0T", [NLB, 8, 128, 256])
                self.n0 = self.din("n0", [NLB, 8, 128, 1])
                self.m0 = self.din("m0", [32, NLB])
                self.cmask2_d = self.din("cmask2", [128, 128])
                self.CT_o = self.dout("CT_o", [NLB, 2, 8, 128, 256])
                self.n_o = self.dout("n_o", [NLB, 2, 8, 128, 1])
                self.m_o = self.dout("m_o", [NLB, 2, 8, 1])
            self.xn = self.sb("xn", [128, KC, TA + 28], BF16)
            self.ovf = self.sb("ovf", [128, OVF], F32)
            self.ovb = self.sb("ovb", [128, OVB], BF16)
            self.wb = [self.sb("wb%d" % i, [128, 8192], BF16) for i in range(NWB)]
            self.wbi = 0
            self.rt = [self.sb("rt%d" % i, [128, 512], F32) for i in range(2)]
            self.rti = 0
            self.nrm_sb = self.sb("nrm_sb", [128, 2 * NL, KC], F32)
            self.ones_f = self.sb("ones_f", [128, 128], F32)
            self.cst_b = self.sb("cst_b", [128, 4, 128], BF16)
            self.ones_b = self.cst_b[:, 0, :]
            self.onesh = self.cst_b[:, 1, :]
            self.ones1 = self.cst_b[:, 2, :]
            self.identf = self.sb("identf", [128, 128], F32)
            self.identb = self.sb("identb", [128, 128], BF16)
            self.epsc = self.sb("epsc", [128, 1], F32)
            self.bscr = self.sb("bscr", [128, 8], F32)
            if "attn" in kinds:
                self.qkg_sb = self.sb("qkg_sb", [128, NLA, 2], F32)
                self.rmat = self.sb("rmat_s", [128, 128], BF16)
                self.cmask = self.sb("cmask_s", [128, NCM], F32)
            if "mlstm" in kinds:
                self.bg_sb = self.sb("bg_sb", [32, NLB, 2], F32)
                self.gh_sb = self.sb("gh_sb", [128, NLB, 16], F32)
                self.sel = self.sb("sel_s", [32, 8, 128], F32)
                self.m0_sb = self.sb("m0_sb", [32, NLB], F32)
                self.cmask2 = self.sb("cmask2_s", [128, 128], F32)
            self.ps = [st.enter_context(nc.psum_tensor("ps%d" % i, [128, 512], F32)) for i in range(8)]
            self.psi = 0
            S.dma("sp", lambda e: e.dma_start(out=self.ones_f[:], in_=self.cst[:, 2, :]), w=["cst_f"])
            S.dma("pool", lambda e: e.dma_start(out=self.cst_b[:], in_=self.cst), w=["ones_b", "ones1", "onesh"])
            S.dma("sp", lambda e: e.dma_start(out=self.nrm_sb[:], in_=self.nrm), w=["nrm_sb"])
            S.dma("sp", lambda e: e.dma_start(out=self.identf[:], in_=self.ident_d), w=["identf"])
            S.act(lambda e: e.copy(out=self.identb[:], in_=self.identf[:]), r=["identf"], w=["identb"])
            S.dve(lambda e: e.memset(self.epsc[:], EPS), w=["epsc"])
            S.dve(lambda e: e.memset(self.xn[:, :, TA:TA + 28], 0.0), w=["xnpad"])
            if "attn" in kinds:
                S.dma("sp", lambda e: e.dma_start(out=self.qkg_sb[:], in_=self.qkg), w=["qkg"])
                S.dma("pool", lambda e: e.dma_start(out=self.rmat[:], in_=self.rmat_d), w=["rmat"])
                S.dma("sp", lambda e: e.dma_start(out=self.cmask[:], in_=self.cmask_d), w=["cmask"])
            if "mlstm" in kinds:
                S.dma("sp", lambda e: e.dma_start(out=self.bg_sb[:], in_=self.bg), w=["bg"])
                S.dma("sp", lambda e: e.dma_start(out=self.gh_sb[:], in_=self.gh), w=["gh"])
                S.dma("sp", lambda e: e.dma_start(out=self.sel[:], in_=self.sel_d), w=["sel"])
                S.dma("sp", lambda e: e.dma_start(out=self.m0_sb[:], in_=self.m0), w=["m0"])
                S.dma("sp", lambda e: e.dma_start(out=self.cmask2[:], in_=self.cmask2_d), w=["cmask2"])
                S.dve(lambda e: e.tensor_scalar(out=self.bg_sb[:], in0=self.bg_sb[:], scalar1=1.0 / 15.0, scalar2=None, op0=ALU.mult),
                      r=["bg"], w=["bg"])
            self.res_src = self.xT
            for pi, (kind, l) in enumerate(self.plan):
                self.last_phase = (pi == len(self.plan) - 1)
                if kind == "ffn":
                    self.norm_phase(NL + l)
                    self.ffn_phase(l)
                elif kind == "attn":
                    self.norm_phase(2 * l)
                    self.attn_phase(l)
                elif kind == "mlstm":
                    self.norm_phase(2 * l + 1)
                    self.mlstm_phase(l)
            S.emit(st)
        return nc

    def next_wb(self):
        i = self.wbi % NWB
        self.wbi += 1
        return self.wb[i], [("wb", i, 0), ("wb", i, 1), ("wb", i, 2)]

    def next_ps(self, n=7):
        i = self.psi % n
        self.psi += 1
        return self.ps[i], ("ps", i)

    def res_view(self, src):
        return src.rearrange("(k p) t -> p k t", p=128)

    def reskeys(self, t0, n):
        return [("res", k, t0 // 512) for k in range(KC)]

    def xnkeys(self, ti):
        return [("xn", k, ti) for k in range(KC)]

    def allxn(self):
        return [("xn", k, ti) for k in range(KC) for ti in range(len(self.tiles))]

    def tix(self, t0):
        return t0 // 512

    def norm_phase(self, gi):
        S = self.S
        self.ov_reset()
        src = self.res_view(self.res_src)
        xts = [self.af(KC * 128) for _ in range(2)]
        sqs = [self.ab(KC * 128) for _ in range(2)]
        rstds = [self.af(128) for _ in range(2)]
        for i, (t0, n) in enumerate(self.ntile):
            xt, kx = xts[i % 2]
            sq, ks = sqs[i % 2]
            rstd, kr = rstds[i % 2]
            xt = xt.rearrange("p (k t) -> p k t", k=KC)
            sq = sq.rearrange("p (k t) -> p k t", k=KC)
            S.dma("sp", lambda e, xt=xt, t0=t0, n=n: e.dma_start(out=xt[:, :, :n], in_=src[:, :, t0:t0 + n]),
                  r=self.reskeys(t0, n), w=[kx])
            S.act(lambda e, xt=xt, sq=sq, n=n: e.activation(out=sq[:, :, :n], in_=xt[:, :, :n], func=AF.Square),
                  r=[kx], w=[ks])
            ps, kp = self.ps[6], ("ps", 6)
            for k in range(KC):
                S.pe(lambda e, ps=ps, sq=sq, k=k, n=n: e.matmul(ps[:, :n], lhsT=self.ones_b, rhs=sq[:, k, :n],
                                                                 start=(k == 0), stop=(k == KC - 1)),
                     r=[ks, "ones_b"], w=[kp])
            S.act(lambda e, ps=ps, rstd=rstd, n=n: e.activation(out=rstd[:, :n], in_=ps[:, :n], func=AF.Sqrt,
                                                                bias=self.epsc[:, 0:1], scale=1.0),
                  r=[kp, "epsc"], w=[kr])
            S.dve(lambda e, rstd=rstd, n=n: e.reciprocal(out=rstd[:, :n], in_=rstd[:, :n]), r=[kr], w=[kr])
            for k in range(KC):
                S.dve(lambda e, xt=xt, rstd=rstd, k=k, t0=t0, n=n: e.scalar_tensor_tensor(
                    out=self.xn[:, k, t0:t0 + n], in0=xt[:, k, :n], scalar=self.nrm_sb[:, gi, k:k + 1], in1=rstd[:, :n],
                    op0=ALU.mult, op1=ALU.mult), r=[kx, kr, "nrm_sb"], w=[("xn", k, t0 // 512)])

    def residual_add(self, ps, kp, dc, t0, n):
        S = self.S
        rt, kt = self.rt_list[self.rti % len(self.rt_list)]
        self.rti += 1
        src = self.res_src
        dst = self.res_dst
        key = ("res", dc, t0 // 512)
        S.dma("sp", lambda e: e.dma_start(out=rt[:, :n], in_=src[dc * 128:(dc + 1) * 128, t0:t0 + n]), r=[key], w=[kt])
        S.dve(lambda e: e.tensor_tensor(out=rt[:, :n], in0=ps[:, :n], in1=rt[:, :n], op=ALU.add), r=[kp, kt], w=[kt])
        S.dma("act", lambda e: e.dma_start(out=dst[dc * 128:(dc + 1) * 128, t0:t0 + n], in_=rt[:, :n]), r=[kt], w=[key])

    def proj_rmw(self, wsrc, c0, G, rhs_fn, final=False):
        S = self.S
        if final:
            self.res_dst = self.yT
        for dq in range(4):
            wt, kw = self.next_wb()
            wv = wt[:, 0:G * 512].rearrange("p (c d) -> p c d", c=G)
            S.dma("pool", lambda e, wv=wv, dq=dq: e.dma_start(out=wv, in_=wsrc[:, c0:c0 + G, dq * 512:(dq + 1) * 512]), w=kw)
            for d4 in range(4):
                dc = dq * 4 + d4
                for ti, (t0, n) in enumerate(self.tiles):
                    ps, kp = self.next_ps()
                    for c in range(G):
                        rhs, krhs = rhs_fn(c, t0, n, ti)
                        S.pe(lambda e, ps=ps, wv=wv, c=c, d4=d4, n=n, rhs=rhs: e.matmul(
                            ps[:, :n], lhsT=wv[:, c, d4 * 128:(d4 + 1) * 128], rhs=rhs,
                            start=(c == 0), stop=(c == G - 1)), r=[kw[0], krhs], w=[kp])
                    self.residual_add(ps, kp, dc, t0, n)
        self.res_src = self.rs

    def proj_rmw_gen(self, wsrc, c0, G, rhs_fn, wbufs, final=False):
        S = self.S
        if final:
            self.res_dst = self.yT
        for dq in range(4):
            wv, kwv = wbufs[self.pwi % len(wbufs)]
            self.pwi += 1
            wv3 = wv[:, 0:G * 512].rearrange("p (c d) -> p c d", c=G)
            S.dma("pool", lambda e, wv3=wv3, dq=dq: e.dma_start(out=wv3, in_=wsrc[:, c0:c0 + G, dq * 512:(dq + 1) * 512]), w=[kwv])
            for d4 in range(4):
                dc = dq * 4 + d4
                for ti, (t0, n) in enumerate(self.tiles):
                    ps, kp = self.next_ps()
                    for c in range(G):
                        rhs, krhs = rhs_fn(c, t0, n, ti)
                        S.pe(lambda e, ps=ps, wv3=wv3, c=c, d4=d4, n=n, rhs=rhs: e.matmul(
                            ps[:, :n], lhsT=wv3[:, c, d4 * 128:(d4 + 1) * 128], rhs=rhs,
                            start=(c == 0), stop=(c == G - 1)), r=[kwv, krhs], w=[kp])
                    self.residual_add(ps, kp, dc, t0, n)
                    yield
        self.res_src = self.rs

    def ffn_phase(self, l):
        S = self.S
        tiles = self.tiles
        TA = self.TA
        self.ov_reset()
        zt, _ = self.ab(GRP * TA)
        z = zt.rearrange("p (g t) -> p g t", g=GRP)
        Ub = [[self.af(516) for j in range(2)] for i in range(2)]
        ccb = [[self.af(512) for j in range(2)] for i in range(2)]
        sab = [self.af(512) for i in range(2)]
        self.rt_list = self.rt_list + [self.af(512) for _ in range(6)]
        cw_t, _ = self.af(2 * FC * 3)
        cb_t, _ = self.af(2 * FC)
        sc_t, _ = self.af(2 * FC * 2)
        cpo_t, _ = self.af(2 * FC * 2)
        cso_t, _ = self.af(2 * FC * 2)
        self.cw = cw_t.rearrange("p (c j) -> p c j", j=3)
        self.cb = cb_t
        self.sc = sc_t.rearrange("p (c j) -> p c j", j=2)
        self.cpo = cpo_t.rearrange("p (c j) -> p c j", j=2)
        self.cso = cso_t.rearrange("p (c j) -> p c j", j=2)
        S.dma("sp", lambda e: e.dma_start(out=self.cw, in_=self.convw[l]), w=["cw"])
        S.dma("sp", lambda e: e.dma_start(out=self.cb, in_=self.convb[l]), w=["cb"])
        S.dma("sp", lambda e: e.dma_start(out=self.sc, in_=self.sconv[l]), w=["sc"])
        wup = self.w_up[l].rearrange("(k p) c -> p k c", p=128)
        wdn = self.w_down[l].rearrange("(c p) d -> p c d", p=128)
        ucount = 0
        f0 = 0
        gen = self.ov_gen
        while f0 < FC:
            G = min(GRP, FC - f0)
            for pr in range(G // 2):
                fa = f0 + 2 * pr
                wt, kw = self.next_wb()
                wv = wt[:].rearrange("p (k c) -> p k c", k=KC)
                S.dma("pool", lambda e, wv=wv, fa=fa: e.dma_start(out=wv[:, :, 0:256], in_=wup[:, :, fa * 128:fa * 128 + 256]), w=kw)
                S.dma("pool", lambda e, wv=wv, fa=fa: e.dma_start(out=wv[:, :, 256:512], in_=wup[:, :, FF + fa * 128:FF + fa * 128 + 256]), w=[kw[1]])
                for fi in range(2):
                    f = fa + fi
                    zi = f - f0
                    prevU = None
                    for ti, (t0, n) in enumerate(tiles):
                        pa, ka = self.next_ps()
                        pb, kb = self.next_ps()
                        for (pp, kp, co, kwx) in ((pa, ka, fi * 128, kw[0]), (pb, kb, 256 + fi * 128, kw[1])):
                            for k in range(KC):
                                S.pe(lambda e, pp=pp, k=k, co=co, wv=wv, t0=t0, n=n: e.matmul(
                                    pp[:, :n], lhsT=wv[:, k, co:co + 128], rhs=self.xn[:, k, t0:t0 + n],
                                    start=(k == 0), stop=(k == KC - 1)), r=[kwx, ("xn", k, ti)], w=[kp])
                        ub = ucount % 2
                        ucount += 1
                        cs = []
                        for ab, (pp, kp, ch) in enumerate(((pa, ka, f), (pb, kb, FC + f))):
                            U, kU = Ub[ub][ab]
                            c, kc = ccb[ub][ab]
                            is_sample = (t0 == self.T)
                            if is_sample:
                                S.act(lambda e, U=U, ch=ch: e.copy(out=U[:, 0:2], in_=self.sc[:, ch, :]), r=["sc"], w=[kU])
                            elif ti == 0:
                                S.dve(lambda e, U=U: e.memset(U[:, 0:2], 0.0), w=[kU])
                            else:
                                pU, pk = prevU[ab]
                                S.act(lambda e, U=U, pU=pU: e.copy(out=U[:, 0:2], in_=pU[:, 512:514]), r=[pk], w=[kU])
                            S.act(lambda e, U=U, pp=pp, n=n: e.copy(out=U[:, 2:2 + n], in_=pp[:, :n]), r=[kp], w=[kU])
                            S.act(lambda e, c=c, pp=pp, n=n, ch=ch: e.activation(out=c[:, :n], in_=pp[:, :n], func=AF.Identity,
                                                                                 scale=self.cw[:, ch, 2:3], bias=self.cb[:, ch:ch + 1]),
                                  r=[kp, "cw", "cb"], w=[kc])
                            S.dve(lambda e, c=c, U=U, n=n, ch=ch: e.scalar_tensor_tensor(
                                out=c[:, :n], in0=U[:, 1:1 + n], scalar=self.cw[:, ch, 1:2], in1=c[:, :n], op0=ALU.mult, op1=ALU.add),
                                r=[kU, kc, "cw"], w=[kc])
                            S.dve(lambda e, c=c, U=U, n=n, ch=ch: e.scalar_tensor_tensor(
                                out=c[:, :n], in0=U[:, 0:n], scalar=self.cw[:, ch, 0:1], in1=c[:, :n], op0=ALU.mult, op1=ALU.add),
                                r=[kU, kc, "cw"], w=[kc])
                            if ti == len(tiles) - 2:
                                S.act(lambda e, U=U, ch=ch, n=n: e.copy(out=self.cpo[:, ch, :], in_=U[:, n:n + 2]), r=[kU], w=["cpo"])
                            if is_sample:
                                S.act(lambda e, U=U, ch=ch, n=n: e.copy(out=self.cso[:, ch, :], in_=U[:, n:n + 2]), r=[kU], w=["cso"])
                            cs.append((c, kc))
                        prevU = [Ub[ub][0], Ub[ub][1]]
                        sa, ksa = sab[ub]
                        S.act(lambda e, sa=sa, c=cs[0][0], n=n: e.activation(out=sa[:, :n], in_=c[:, :n], func=AF.Silu),
                              r=[cs[0][1]], w=[ksa])
                        S.dve(lambda e, sa=sa, c=cs[1][0], n=n, zi=zi, t0=t0: e.tensor_tensor(
                            out=z[:, zi, t0:t0 + n], in0=sa[:, :n], in1=c[:, :n], op=ALU.mult),
                            r=[ksa, cs[1][1]], w=[("z", gen, zi, ti)])
            self.proj_rmw(wdn, f0, G, lambda c, t0, n, ti: (z[:, c, t0:t0 + n], ("z", gen, c, ti)),
                          final=(self.last_phase and f0 + G >= FC))
            f0 += G
        S.dma("sp", lambda e: e.dma_start(out=self.convp_o[l], in_=self.cpo), r=["cpo"])
        S.dma("sp", lambda e: e.dma_start(out=self.convs_o[l], in_=self.cso), r=["cso"])

    def attn_phase(self, l):
        S = self.S
        T, TA = self.T, self.TA
        tiles = self.tiles
        self.ov_reset()
        gen = self.ov_gen
        oallt, _ = self.ab(8 * TA)
        oav = oallt.rearrange("p (h t) -> p h t", h=8)
        cosb, kcos = self.ab(TA)
        sinb, ksin = self.ab(TA)
        qr, kqr = self.ab(TA + 28)
        kr, kkr = self.ab(TA + 28)
        S.dve(lambda e: e.memset(qr[:, TA:TA + 28], 0.0), w=[(kqr, "pad")])
        S.dve(lambda e: e.memset(kr[:, TA:TA + 28], 0.0), w=[(kkr, "pad")])
        NUM, kN = self.af(TA)
        DEN, kD = self.af(TA)
        sqb = [self.ab(512) for _ in range(2)]
        xgb = [self.ab(512) for _ in range(2)]
        rsb = [self.af(512) for _ in range(2)]
        t1b = [self.af(512) for _ in range(2)]
        t2b = [self.af(512) for _ in range(2)]
        kof = [self.af(512) for _ in range(2)]
        vbb = [self.ab(128) for _ in range(3)]
        vfb = [self.af(128) for _ in range(2)]
        peb = [self.af(256) for _ in range(2)]
        ptb = [self.ab(256) for _ in range(3)]
        smf = [self.af(128) for _ in range(4)]
        smb = [self.ab(128) for _ in range(5)]
        kcf = [self.af(128) for _ in range(2)]
        vcf = [self.af(128) for _ in range(2)]
        S.dma("pool", lambda e: e.dma_start(out=cosb, in_=self.rope[0]), w=[kcos])
        S.dma("pool", lambda e: e.dma_start(out=sinb, in_=self.rope[1]), w=[ksin])
        wq = self.w_qkv[l].rearrange("(k p) c -> p k c", p=128)
        wo = self.w_o[l].rearrange("(c p) d -> p c d", p=128)
        cm = self.cmask
        maskA = cm[:, 0:256]
        maskc0 = cm[:, 384:388]
        masknew0 = cm[0:4, 388:392]
        masknewI = cm[0:4, 392:396]
        cnt = 0
        for h in range(self.dbg.get("heads", 8)):
            for g in self.dbg.get("groups", (0, 1, 2)):
                dil = DILS[g]
                nbuf = NBUF[g]
                nkeep = min(nbuf, T)
                cq = g * 3072 + h * 128
                wt, kw = self.next_wb()
                wv = wt[:, 0:KC * 384].rearrange("p (k c) -> p k c", k=KC)
                S.dma("pool", lambda e, wv=wv, cq=cq: e.dma_start(out=wv[:, :, 0:128], in_=wq[:, :, cq:cq + 128]), w=kw)
                S.dma("pool", lambda e, wv=wv, cq=cq: e.dma_start(out=wv[:, :, 128:256], in_=wq[:, :, cq + 1024:cq + 1152]), w=[kw[1]])
                S.dma("pool", lambda e, wv=wv, cq=cq: e.dma_start(out=wv[:, :, 256:384], in_=wq[:, :, cq + 2048:cq + 2176]), w=[kw[2]])
                for which, (dst, kdst, co) in enumerate(((qr, kqr, 0), (kr, kkr, 128))):
                    for ti, (t0, n) in enumerate(tiles):
                        ps, kp = self.next_ps()
                        for k in range(KC):
                            S.pe(lambda e, ps=ps, k=k, co=co, wv=wv, t0=t0, n=n: e.matmul(
                                ps[:, :n], lhsT=wv[:, k, co:co + 128], rhs=self.xn[:, k, t0:t0 + n],
                                start=(k == 0), stop=(k == KC - 1)), r=[kw[which], ("xn", k, ti)], w=[kp])
                        i2 = cnt % 2
                        cnt += 1
                        sq, ksq = sqb[i2]
                        xg, kxg = xgb[i2]
                        rs, krs = rsb[i2]
                        t1, kt1 = t1b[i2]
                        t2, kt2 = t2b[i2]
                        S.act(lambda e, sq=sq, ps=ps, n=n: e.activation(out=sq[:, :n], in_=ps[:, :n], func=AF.Square), r=[kp], w=[ksq])
                        S.act(lambda e, xg=xg, ps=ps, n=n, which=which: e.activation(out=xg[:, :n], in_=ps[:, :n], func=AF.Identity,
                                                                                      scale=self.qkg_sb[:, l, which:which + 1]),
                              r=[kp, "qkg"], w=[kxg])
                        ps2, kp2 = self.next_ps()
                        S.pe(lambda e, ps2=ps2, sq=sq, n=n: e.matmul(ps2[:, :n], lhsT=self.onesh, rhs=sq[:, :n], start=True, stop=True),
                             r=[ksq, "onesh"], w=[kp2])
                        ps3, kp3 = self.next_ps()
                        S.pe(lambda e, ps3=ps3, xg=xg, n=n: e.matmul(ps3[:, :n], lhsT=self.rmat[:], rhs=xg[:, :n], start=True, stop=True),
                             r=[kxg, "rmat"], w=[kp3])
                        S.act(lambda e, rs=rs, ps2=ps2, n=n: e.activation(out=rs[:, :n], in_=ps2[:, :n], func=AF.Sqrt,
                                                                          bias=self.epsc[:, 0:1], scale=1.0), r=[kp2, "epsc"], w=[krs])
                        S.dve(lambda e, rs=rs, n=n: e.reciprocal(out=rs[:, :n], in_=rs[:, :n]), r=[krs], w=[krs])
                        S.dve(lambda e, t1=t1, xg=xg, t0=t0, n=n: e.tensor_tensor(out=t1[:, :n], in0=xg[:, :n], in1=cosb[:, t0:t0 + n], op=ALU.mult),
                              r=[kxg, kcos], w=[kt1])
                        S.dve(lambda e, t2=t2, ps3=ps3, t0=t0, n=n: e.tensor_tensor(out=t2[:, :n], in0=ps3[:, :n], in1=sinb[:, t0:t0 + n], op=ALU.mult),
                              r=[kp3, ksin], w=[kt2])
                        S.pool(lambda e, t1=t1, t2=t2, n=n: e.tensor_tensor(out=t1[:, :n], in0=t1[:, :n], in1=t2[:, :n], op=ALU.add),
                               r=[kt1, kt2], w=[kt1])
                        S.dve(lambda e, dst=dst, t1=t1, rs=rs, t0=t0, n=n: e.tensor_tensor(out=dst[:, t0:t0 + n], in0=t1[:, :n], in1=rs[:, :n], op=ALU.mult),
                              r=[kt1, krs], w=[(kdst, ti)])
                krk = [(kkr, ti) for ti in range(len(tiles))]
                kqk = [(kqr, ti) for ti in range(len(tiles))]
                for pc in range(nkeep // 512 if nkeep >= 512 else 1):
                    w_ = min(512, nkeep)
                    c0 = T - nkeep + pc * w_
                    ko, kko = kof[pc % 2]
                    S.act(lambda e, ko=ko, c0=c0, w_=w_: e.copy(out=ko[:, :w_], in_=kr[:, c0:c0 + w_]), r=krk, w=[kko])
                    S.dma("sp", lambda e, ko=ko, pc=pc, w_=w_, g=g, h=h: e.dma_start(out=self.kT_o[g][l, h, :, pc * w_:(pc + 1) * w_], in_=ko[:, :w_]), r=[kko])
                if self.dbg.get("noattn"):
                    continue
                nb = (T // dil) // 128
                qv = qr[:, 0:T].rearrange("p (a b) -> p a b", b=dil)
                kv_ = kr[:, 0:T].rearrange("p (a b) -> p a b", b=dil)
                Nv = NUM[:, 0:T].rearrange("p (a b) -> p a b", b=dil)
                Dv = DEN[:, 0:T].rearrange("p (a b) -> p a b", b=dil)
                for r in range(dil):
                    prev = None
                    for m in range(nb):
                        psv, kpv = self.next_ps()
                        for k in range(KC if not self.dbg.get("nov") else 0):
                            xv = self.xn[:, k, 0:T].rearrange("p (a b) -> p a b", b=dil)
                            S.pe(lambda e, psv=psv, k=k, xv=xv, m=m, r=r, wv=wv: e.matmul(
                                psv[:, 0:128], lhsT=xv[:, 128 * m:128 * m + 128, r], rhs=wv[:, k, 256:384],
                                start=(k == 0), stop=(k == KC - 1)), r=[kw[2]] + [("xn", k, ti) for ti in range(len(tiles) - 1)], w=[kpv])
                        vb, kvb = vbb[cnt % 3]
                        if m == nb - 1:
                            vf, kvf = vfb[cnt % 2]
                            S.dve(lambda e, vf=vf, psv=psv: e.tensor_copy(out=vf, in_=psv[:, 0:128]), r=[kpv], w=[kvf])
                            S.act(lambda e, vb=vb, vf=vf: e.copy(out=vb, in_=vf), r=[kvf], w=[kvb])
                            S.dma("sp", lambda e, vf=vf, g=g, h=h, r=r: e.dma_start(out=self.v_o[g][l, h, r], in_=vf), r=[kvf])
                        else:
                            S.act(lambda e, vb=vb, psv=psv: e.copy(out=vb, in_=psv[:, 0:128]), r=[kpv], w=[kvb])
                        nq = 256 if m + 1 < nb else 128
                        pss, kps = self.next_ps()
                        S.pe(lambda e, pss=pss, m=m, r=r, nq=nq, kv_=kv_, qv=qv: e.matmul(
                            pss[:, 0:nq], lhsT=kv_[:, 128 * m:128 * m + 128, r], rhs=qv[:, 128 * m:128 * m + nq, r], start=True, stop=True),
                            r=krk + kqk, w=[kps])
                        pe_, kpe = peb[cnt % 2]
                        pt, kpt = ptb[cnt % 3]
                        S.act(lambda e, pe_=pe_, pss=pss, nq=nq: e.activation(out=pe_[:, :nq], in_=pss[:, :nq], func=AF.Exp, scale=SCALE_A), r=[kps], w=[kpe])
                        S.dve(lambda e, pt=pt, pe_=pe_, nq=nq: e.tensor_tensor(out=pt[:, :nq], in0=pe_[:, :nq], in1=maskA[:, :nq], op=ALU.mult),
                              r=[kpe, "cmask"], w=[kpt])
                        if self.dbg.get("nopv"):
                            prev = (vb, kvb, pt, kpt)
                            cnt += 1
                            continue
                        pso, kpo = self.next_ps()
                        for (lo, which) in ((0, "v"), (128, "1")):
                            if prev is not None:
                                pvb, pkvb, ppt, pkpt = prev
                                S.pe(lambda e, pso=pso, lo=lo, which=which, pvb=pvb, ppt=ppt: e.matmul(
                                    pso[:, lo:lo + 128], lhsT=(pvb if which == "v" else self.ones1), rhs=ppt[:, 128:256], start=True, stop=False),
                                    r=[pkvb, pkpt, "ones1"], w=[kpo])
                            S.pe(lambda e, pso=pso, lo=lo, which=which, vb=vb, pt=pt, first=(prev is None): e.matmul(
                                pso[:, lo:lo + 128], lhsT=(vb if which == "v" else self.ones1), rhs=pt[:, 0:128], start=first, stop=True),
                                r=[kvb, kpt, "ones1"], w=[kpo])
                        nvv = Nv[:, 128 * m:128 * m + 128, r]
                        dvv = Dv[:, 128 * m:128 * m + 128, r]
                        if g == 0:
                            S.act(lambda e, nvv=nvv, pso=pso: e.copy(out=nvv, in_=pso[:, 0:128]), r=[kpo], w=[kN])
                            S.act(lambda e, dvv=dvv, pso=pso: e.copy(out=dvv, in_=pso[:, 128:256]), r=[kpo], w=[kD])
                        else:
                            S.dve(lambda e, nvv=nvv, pso=pso: e.tensor_tensor(out=nvv, in0=nvv, in1=pso[:, 0:128], op=ALU.add), r=[kpo, kN], w=[kN])
                            S.dve(lambda e, dvv=dvv, pso=pso: e.tensor_tensor(out=dvv, in0=dvv, in1=pso[:, 128:256], op=ALU.add), r=[kpo, kD], w=[kD])
                        prev = (vb, kvb, pt, kpt)
                        cnt += 1
                if self.dbg.get("nosample"):
                    continue
                qs = qr[:, T:T + 4]
                ks_ = kr[:, T:T + 4]
                psv, kpv = self.next_ps()
                for k in range(KC):
                    S.pe(lambda e, psv=psv, k=k, wv=wv: e.matmul(psv[0:32, 0:128], lhsT=self.xn[:, k, T:T + 32], rhs=wv[:, k, 256:384],
                                                                  start=(k == 0), stop=(k == KC - 1)), r=[kw[2], ("xn", k, len(tiles) - 1), "xnpad"], w=[kpv])
                vnb, kvnb = smb[0]
                vnf, kvnf = smf[0]
                S.dve(lambda e, psv=psv: e.tensor_copy(out=vnf[0:32, :], in_=psv[0:32, 0:128]), r=[kpv], w=[kvnf])
                S.act(lambda e: e.copy(out=vnb[0:32, :], in_=vnf[0:32, :]), r=[kvnf], w=[kvnb])
                S.dma("sp", lambda e, g=g, h=h, nbuf=nbuf: e.dma_start(out=self.kvs_o[g][l, nbuf - 4:nbuf, 1, h, :], in_=vnf[0:4, :]), r=[kvnf])
                psk, kpk = self.next_ps()
                S.pe(lambda e, psk=psk: e.matmul(psk[0:32, 0:128], lhsT=kr[:, T:T + 32], rhs=self.identb[:], start=True, stop=True), r=krk + ["identb", (kkr, "pad")], w=[kpk])
                knf, kknf = smf[1]
                S.act(lambda e, psk=psk: e.copy(out=knf[0:4, :], in_=psk[0:4, 0:128]), r=[kpk], w=[kknf])
                S.dma("sp", lambda e, g=g, h=h, nbuf=nbuf: e.dma_start(out=self.kvs_o[g][l, nbuf - 4:nbuf, 0, h, :], in_=knf[0:4, :]), r=[kknf])
                if h == 0 and not self.dbg.get("noshift"):
                    S.dma("sp", lambda e, g=g, nbuf=nbuf: e.dma_start(out=self.kvs_o[g][l, 0:nbuf - 4], in_=self.cache[g][l, 4:nbuf]))
                if self.dbg.get("s_stage", 9) < 2:
                    continue
                psn, kpn = self.next_ps()
                S.pe(lambda e, psn=psn: e.matmul(psn[0:32, 0:4], lhsT=kr[:, T:T + 32], rhs=qs, start=True, stop=True), r=krk + kqk + [(kkr, "pad")], w=[kpn])
                pnf, kpnf = smf[2]
                pnb, kpnb = smb[1]
                S.act(lambda e, psn=psn: e.activation(out=pnf[0:32, 0:4], in_=psn[0:32, 0:4], func=AF.Exp, scale=SCALE_A), r=[kpn], w=[kpnf])
                mk = cm[0:32, 388:392] if g == 0 else cm[0:32, 392:396]
                S.dve(lambda e, mk=mk: e.tensor_tensor(out=pnb[0:32, 0:4], in0=pnf[0:32, 0:4], in1=mk, op=ALU.mult), r=[kpnf, "cmask"], w=[kpnb])
                pso, kpo = self.next_ps()
                S.pe(lambda e, pso=pso: e.matmul(pso[:, 0:4], lhsT=vnb[0:32, :], rhs=pnb[0:32, 0:4], start=True, stop=True), r=[kvnb, kpnb], w=[kpo])
                S.pe(lambda e, pso=pso: e.matmul(pso[:, 4:8], lhsT=self.ones1[0:32, :], rhs=pnb[0:32, 0:4], start=True, stop=True), r=[kpnb, "ones1"], w=[kpo])
                if g == 0:
                    S.act(lambda e, pso=pso: e.copy(out=NUM[:, T:T + 4], in_=pso[:, 0:4]), r=[kpo], w=[kN])
                    S.act(lambda e, pso=pso: e.copy(out=DEN[:, T:T + 4], in_=pso[:, 4:8]), r=[kpo], w=[kD])
                else:
                    S.dve(lambda e, pso=pso: e.tensor_tensor(out=NUM[:, T:T + 4], in0=NUM[:, T:T + 4], in1=pso[:, 0:4], op=ALU.add), r=[kpo, kN], w=[kN])
                    S.dve(lambda e, pso=pso: e.tensor_tensor(out=DEN[:, T:T + 4], in0=DEN[:, T:T + 4], in1=pso[:, 4:8], op=ALU.add), r=[kpo, kD], w=[kD])
                if self.dbg.get("s_stage", 9) < 3:
                    continue
                cg = self.cache[g][l]
                nblk = 1 if g == 0 else 4
                for bi in range(nblk):
                    rows = cg[0:128] if g == 0 else cg.rearrange("(u s) a h d -> s u a h d", s=dil)[bi]
                    kc, kkc = kcf[bi % 2]
                    vc, kvc = vcf[bi % 2]
                    S.dma("sp", lambda e, kc=kc, rows=rows, h=h: e.dma_start(out=kc, in_=rows[:, 0, h, :]), w=[kkc])
                    S.dma("sp", lambda e, vc=vc, rows=rows, h=h: e.dma_start(out=vc, in_=rows[:, 1, h, :]), w=[kvc])
                    pst, kpt_ = self.next_ps()
                    S.pe(lambda e, pst=pst, kc=kc: e.matmul(pst[:, 0:128], lhsT=kc, rhs=self.identf[:], start=True, stop=True), r=[kkc, "identf"], w=[kpt_])
                    kcT, kkcT = smb[2]
                    vcb, kvcb = smb[3]
                    S.act(lambda e, pst=pst: e.copy(out=kcT, in_=pst[:, 0:128]), r=[kpt_], w=[kkcT])
                    S.dve(lambda e, vc=vc: e.tensor_copy(out=vcb, in_=vc), r=[kvc], w=[kvcb])
                    q0, nqc = 0, 4
                    mkc = maskc0 if g == 0 else cm[:, 396 + 4 * bi:400 + 4 * bi]
                    pss, kps = self.next_ps()
                    S.pe(lambda e, pss=pss, q0=q0, nqc=nqc: e.matmul(pss[:, 0:nqc], lhsT=kcT, rhs=qr[:, T + q0:T + q0 + nqc], start=True, stop=True),
                         r=[kkcT] + kqk, w=[kps])
                    pcf, kpcf = smf[3]
                    pcb, kpcb = smb[4]
                    S.act(lambda e, pss=pss, nqc=nqc: e.activation(out=pcf[:, 0:nqc], in_=pss[:, 0:nqc], func=AF.Exp, scale=SCALE_A), r=[kps], w=[kpcf])
                    S.dve(lambda e, mkc=mkc: e.tensor_tensor(out=pcb[:, 0:4], in0=pcf[:, 0:4], in1=mkc, op=ALU.mult), r=[kpcf, "cmask"], w=[kpcb])
                    pso, kpo = self.next_ps()
                    S.pe(lambda e, pso=pso, nqc=nqc: e.matmul(pso[:, 0:nqc], lhsT=vcb, rhs=pcb[:, 0:nqc], start=True, stop=True), r=[kvcb, kpcb], w=[kpo])
                    S.pe(lambda e, pso=pso, nqc=nqc: e.matmul(pso[:, 4:4 + nqc], lhsT=self.ones1, rhs=pcb[:, 0:nqc], start=True, stop=True), r=[kpcb, "ones1"], w=[kpo])
                    S.dve(lambda e, pso=pso, q0=q0, nqc=nqc: e.tensor_tensor(out=NUM[:, T + q0:T + q0 + nqc], in0=NUM[:, T + q0:T + q0 + nqc], in1=pso[:, 0:nqc], op=ALU.add),
                          r=[kpo, kN], w=[kN])
                    S.dve(lambda e, pso=pso, q0=q0, nqc=nqc: e.tensor_tensor(out=DEN[:, T + q0:T + q0 + nqc], in0=DEN[:, T + q0:T + q0 + nqc], in1=pso[:, 4:4 + nqc], op=ALU.add),
                          r=[kpo, kD], w=[kD])
            S.dve(lambda e: e.reciprocal(out=DEN, in_=DEN), r=[kD], w=[kD])
            S.dve(lambda e, h=h: e.tensor_tensor(out=oav[:, h, :], in0=NUM, in1=DEN, op=ALU.mult), r=[kN, kD], w=[("oall", gen, h)])
        if self.dbg.get("noproj"):
            return
        self.barrier()
        self.ovf_off = 0
        self.rt_list = self.rt_list + [self.af(512) for _ in range(8)]
        self.proj_rmw(wo, 0, 8, lambda c, t0, n, ti: (oav[:, c, t0:t0 + n], ("oall", gen, c)), final=self.last_phase)

    def mlstm_phase(self, l):
        S = self.S
        T, TA = self.T, self.TA
        TP = TA + 28
        tiles = self.tiles
        NCH = T // 128
        self.ov_reset()
        gen = self.ov_gen
        win = self.w_in[l].rearrange("(k p) c -> p k c", p=128)
        wout = self.w_out[l].rearrange("(c p) d -> p c d", p=128)
        A, kA = self.af(TP)
        Bf, kBf = self.af(TP)
        Bc, kBc = self.af(TP)
        lb, klb = Bf, kBf
        Mloc, kMl = self.af(32)
        Mc, kMc = self.af(32)
        Mprev, kMp = self.af(32)
        negM, kNm = self.af(32)
        dec, kdec = self.af(32)
        mout, kmo = self.af(2)
        etok_t, ket = self.af(17 * 8)
        etok = etok_t.rearrange("p (c h) -> p c h", h=8)
        decbc_t, kdb = self.af(8 * 17)
        decbc = decbc_t.rearrange("p (h c) -> p h c", c=17)
        Sf, kSf = self.af(384)
        n0t, kn0 = self.af(1)
        rden = [self.af(128) for _ in range(2)]
        rsb, krsb = self.af(512)
        sgb, ksgb = self.af(512)
        tmpb, ktmp = self.af(512)
        qT, kqT = self.ab(TP)
        kT, kkT = self.ab(TP)
        hT_t, khT = self.ab(2 * TA)
        hT = hT_t.rearrange("p (j t) -> p j t", j=2)
        hn_bufs = []
        for _ in range(2):
            hn_t_, khn_ = self.ab(2 * TA)
            hn_bufs.append((hn_t_.rearrange("p (j t) -> p j t", j=2), khn_))
        pw_bufs = [self.ab(1024) for _ in range(2)]
        self.pwi = 0
        pending = None
        ktokb = [self.ab(128) for _ in range(2)]
        vxb = [self.ab(384) for _ in range(2)]
        Gmb = [self.ab(128) for _ in range(2)]
        Sb, kSb = self.ab(384)
        sqh_t, ksqh = self.ab(2 * 512)
        sqh = sqh_t.rearrange("p (j t) -> p j t", j=2)
        ones_f1 = self.ones_f[:]
        ones256 = self.cst_b[:, 3, :]
        for buf, kb in ((A, kA), (Bf, kBf), (Bc, kBc)):
            S.dve(lambda e, buf=buf: e.memset(buf[:, TA:TP], 0.0), w=[kb])
        S.dve(lambda e: e.memset(qT[:, TA:TP], 0.0), w=[kqT])
        S.dve(lambda e: e.memset(kT[:, TA:TP], 0.0), w=[kkT])
        wt, kw = self.next_wb()
        wg = wt[:, 0:KC * 64].rearrange("p (k c) -> p k c", k=KC)
        S.dve(lambda e: e.memset(wt[:, 0:KC * 64], 0.0), w=kw)
        S.dma("pool", lambda e: e.dma_start(out=wg[:, :, 0:8], in_=win[:, :, 6144:6152]), w=[kw[0]])
        S.dma("pool", lambda e: e.dma_start(out=wg[:, :, 32:40], in_=win[:, :, 6152:6160]), w=[kw[1]])
        for ti, (t0, n) in enumerate(tiles):
            for gi_, (dst, kd) in enumerate(((A, kA), (Bf, kBf))):
                ps, kp = self.next_ps()
                for k in range(KC):
                    S.pe(lambda e, ps=ps, k=k, gi_=gi_, t0=t0, n=n: e.matmul(
                        ps[0:32, :n], lhsT=wg[:, k, gi_ * 32:gi_ * 32 + 32], rhs=self.xn[:, k, t0:t0 + n],
                        start=(k == 0), stop=(k == KC - 1)), r=[kw[gi_], ("xn", k, ti)], w=[kp])
                S.act(lambda e, ps=ps, dst=dst, gi_=gi_, t0=t0, n=n: e.activation(
                    out=dst[0:32, t0:t0 + n], in_=ps[0:32, :n], func=AF.Tanh, scale=1.0 / 15.0, bias=self.bg_sb[0:32, l, gi_:gi_ + 1]),
                    r=[kp, "bg"], w=[kd])
        S.act(lambda e: e.activation(out=Bf[0:32, 0:TA], in_=Bf[0:32, 0:TA], func=AF.Exp, scale=-15.0), r=[kBf], w=[kBf])
        S.act(lambda e: e.activation(out=Bf[0:32, 0:TA], in_=Bf[0:32, 0:TA], func=AF.Ln, bias=self.ones_f[0:32, 0:1], scale=1.0), r=[kBf, "cst_f"], w=[kBf])
        S.dve(lambda e: e.tensor_scalar(out=A[0:32, 0:TA], in0=A[0:32, 0:TA], scalar1=15.0, scalar2=None, op0=ALU.mult), r=[kA], w=[kA])
        S.dve(lambda e: e.tensor_tensor_scan(out=Bc[0:32, 0:T], data0=Bf[0:32, 0:T], data1=Bf[0:32, 0:T], initial=0.0, op0=ALU.add, op1=ALU.bypass),
              r=[kBf], w=[kBc])
        S.dve(lambda e: e.tensor_tensor_scan(out=Bc[0:32, T:TA], data0=Bf[0:32, T:TA], data1=Bf[0:32, T:TA], initial=0.0, op0=ALU.add, op1=ALU.bypass),
              r=[kBf], w=[kBc])
        S.dve(lambda e: e.tensor_tensor(out=A[0:32, 0:TA], in0=A[0:32, 0:TA], in1=Bc[0:32, 0:TA], op=ALU.add), r=[kA, kBc], w=[kA])
        S.dve(lambda e: e.tensor_reduce(out=Mloc[0:32, 0:NCH], in_=A[0:32, 0:T].rearrange("p (c t) -> p c t", t=128), axis=AX.X, op=ALU.max),
              r=[kA], w=[kMl])
        S.dve(lambda e: e.tensor_reduce(out=Mloc[0:32, 16:17], in_=A[0:32, T:TA], axis=AX.X, op=ALU.max), r=[kA], w=[kMl])
        S.dve(lambda e: e.tensor_scalar(out=Mc[0:32, 0:1], in0=Mloc[0:32, 0:1], scalar1=0.0, scalar2=None, op0=ALU.max), r=[kMl], w=[kMc])
        for c in range(1, NCH):
            S.dve(lambda e, c=c: e.tensor_tensor(out=Mc[0:32, c:c + 1], in0=Mc[0:32, c - 1:c], in1=Mloc[0:32, c:c + 1], op=ALU.max), r=[kMl, kMc], w=[kMc])
        S.dve(lambda e: e.tensor_tensor(out=Mc[0:32, 16:17], in0=Mloc[0:32, 16:17], in1=self.m0_sb[0:32, l:l + 1], op=ALU.max), r=[kMl, "m0"], w=[kMc])
        S.dve(lambda e: e.memset(Mprev[0:32, 0:1], 0.0), w=[kMp])
        S.dve(lambda e: e.tensor_copy(out=Mprev[0:32, 1:NCH], in_=Mc[0:32, 0:NCH - 1]), r=[kMc], w=[kMp])
        S.dve(lambda e: e.tensor_copy(out=Mprev[0:32, 16:17], in_=self.m0_sb[0:32, l:l + 1]), r=["m0"], w=[kMp])
        S.dve(lambda e: e.tensor_tensor(out=dec[0:32, 0:17], in0=Mprev[0:32, 0:17], in1=Mc[0:32, 0:17], op=ALU.subtract), r=[kMp, kMc], w=[kdec])
        S.act(lambda e: e.activation(out=dec[0:32, 0:17], in_=dec[0:32, 0:17], func=AF.Exp), r=[kdec], w=[kdec])
        S.dve(lambda e: e.tensor_scalar(out=negM[0:32, 0:17], in0=Mc[0:32, 0:17], scalar1=-1.0, scalar2=None, op0=ALU.mult), r=[kMc], w=[kNm])
        S.dve(lambda e: e.tensor_tensor(out=mout[0:32, 0:1], in0=Mc[0:32, NCH - 1:NCH], in1=Bc[0:32, T - 1:T], op=ALU.subtract), r=[kMc, kBc], w=[kmo])
        S.dve(lambda e: e.tensor_tensor(out=mout[0:32, 1:2], in0=Mc[0:32, 16:17], in1=Bc[0:32, TA - 1:TA], op=ALU.subtract), r=[kMc, kBc], w=[kmo])
        S.dma("sp", lambda e: e.dma_start(out=self.m_o[l, 0], in_=mout[0:8, 0:1]), r=[kmo])
        S.dma("sp", lambda e: e.dma_start(out=self.m_o[l, 1], in_=mout[0:8, 1:2]), r=[kmo])
        if self.dbg.get("mstage", 9) < 2:
            return
        chunks = [(c * 128, 128, c) for c in range(NCH)] + [(T, TS, 16)]
        for (c0, L, c) in chunks:
            S.act(lambda e, c0=c0, L=L, c=c: e.activation(out=A[0:32, c0:c0 + L], in_=A[0:32, c0:c0 + L], func=AF.Exp, bias=negM[0:32, c:c + 1], scale=1.0),
                  r=[kA, kNm], w=[kA])
            S.act(lambda e, c0=c0, L=L, c=c: e.activation(out=Bc[0:32, c0:c0 + L], in_=Bc[0:32, c0:c0 + L], func=AF.Exp, bias=negM[0:32, c:c + 1], scale=1.0),
                  r=[kBc, kNm], w=[kBc])
        pse, kpe = self.next_ps()
        for (c0, L, c) in chunks:
            Lm = 128 if L == 128 else 32
            S.pe(lambda e, c0=c0, Lm=Lm, c=c: e.matmul(pse[0:Lm, c * 8:(c + 1) * 8], lhsT=A[0:32, c0:c0 + Lm], rhs=self.identf[0:32, 0:8], start=True, stop=True),
                 r=[kA, "identf"], w=[kpe])
        S.act(lambda e: e.copy(out=etok_t[:, 0:NCH * 8], in_=pse[:, 0:NCH * 8]), r=[kpe], w=[ket])
        S.act(lambda e: e.copy(out=etok_t[0:32, 128:136], in_=pse[0:32, 128:136]), r=[kpe], w=[ket])
        psd, kpd = self.next_ps()
        for h in range(8):
            S.pe(lambda e, h=h: e.matmul(psd[:, h * 17:(h + 1) * 17], lhsT=self.sel[0:32, h, :], rhs=dec[0:32, 0:17], start=True, stop=True),
                 r=[kdec, "sel"], w=[kpd])
        S.act(lambda e: e.copy(out=decbc_t[:, 0:136], in_=psd[:, 0:136]), r=[kpd], w=[kdb])
        if self.dbg.get("mstage", 9) < 3:
            return
        scale_k = 128 ** -0.5
        cnt = 0
        nheads = self.dbg.get("mheads", 8)
        for h in range(nheads):
            hn, khn = hn_bufs[h % 2]
            w1t, kw1 = self.next_wb()
            w1 = w1t[:].rearrange("p (k c) -> p k c", k=KC)
            S.dma("pool", lambda e, h=h: e.dma_start(out=w1[:, :, 0:128], in_=win[:, :, h * 128:h * 128 + 128]), w=kw1)
            S.dma("pool", lambda e, h=h: e.dma_start(out=w1[:, :, 128:256], in_=win[:, :, 1024 + h * 128:1024 + h * 128 + 128]), w=[kw1[1]])
            S.dma("pool", lambda e, h=h: e.dma_start(out=w1[:, :, 256:512], in_=win[:, :, 4096 + h * 256:4096 + h * 256 + 256]), w=[kw1[2]])
            w2t, kw2 = self.next_wb()
            w2 = w2t[:, 0:KC * 384].rearrange("p (k c) -> p k c", k=KC)
            S.dma("pool", lambda e, h=h: e.dma_start(out=w2[:, :, 0:128], in_=win[:, :, 1024 + h * 128:1024 + h * 128 + 128]), w=kw2)
            S.dma("pool", lambda e, h=h: e.dma_start(out=w2[:, :, 128:384], in_=win[:, :, 2048 + h * 256:2048 + h * 256 + 256]), w=[kw2[1]])
            for which, (dst, kd) in enumerate(((qT, kqT), (kT, kkT))):
                for ti, (t0, n) in enumerate(tiles):
                    ps, kp = self.next_ps()
                    for k in range(KC):
                        S.pe(lambda e, ps=ps, k=k, which=which, t0=t0, n=n, w1=w1: e.matmul(
                            ps[:, :n], lhsT=w1[:, k, which * 128:which * 128 + 128], rhs=self.xn[:, k, t0:t0 + n],
                            start=(k == 0), stop=(k == KC - 1)), r=[kw1[which], ("xn", k, ti)], w=[kp])
                    if which == 0:
                        S.act(lambda e, ps=ps, t0=t0, n=n: e.copy(out=qT[:, t0:t0 + n], in_=ps[:, :n]), r=[kp], w=[kqT])
                    else:
                        S.act(lambda e, ps=ps, t0=t0, n=n: e.mul(out=kT[:, t0:t0 + n], in_=ps[:, :n], mul=scale_k), r=[kp], w=[kkT])
            for ti, (t0, n) in enumerate(tiles):
                ps, kp = self.next_ps()
                S.pe(lambda e, ps=ps, h=h, t0=t0, n=n: e.matmul(ps[:, :n], lhsT=self.sel[0:32, h, :], rhs=Bc[0:32, t0:t0 + n], start=True, stop=True),
                     r=[kBc, "sel"], w=[kp])
                S.act(lambda e, ps=ps, t0=t0, n=n: e.copy(out=lb[:, t0:t0 + n], in_=ps[:, :n]), r=[kp], w=[klb])
            def stage1(c0, L, c, h=h, w2=w2, kw2=kw2):
                Lm = 128 if c != 16 else 32
                pkv, kpkv = self.next_ps()
                for k in range(KC):
                    S.pe(lambda e, pkv=pkv, k=k, c0=c0, Lm=Lm, w2=w2: e.matmul(
                        pkv[0:Lm, 0:384], lhsT=self.xn[:, k, c0:c0 + Lm], rhs=w2[:, k, 0:384], start=(k == 0), stop=(k == KC - 1)),
                        r=[kw2[0], kw2[1], ("xn", k, c0 // 512), "xnpad"], w=[kpkv])
                ktok, kkt = ktokb[c % 2]
                vx, kvx = vxb[c % 2]
                Gm, kGm = Gmb[c % 2]
                S.dve(lambda e, ktok=ktok, pkv=pkv, Lm=Lm: e.tensor_scalar(out=ktok[0:Lm, :], in0=pkv[0:Lm, 0:128], scalar1=scale_k, scalar2=None, op0=ALU.mult),
                      r=[kpkv], w=[kkt])
                S.dve(lambda e, vx=vx, pkv=pkv, Lm=Lm, c=c, h=h: e.tensor_scalar(out=vx[0:Lm, 0:256], in0=pkv[0:Lm, 128:384], scalar1=etok[0:Lm, c, h:h + 1],
                                                                               scalar2=None, op0=ALU.mult), r=[kpkv, ket], w=[kvx])
                S.dve(lambda e, vx=vx, Lm=Lm, c=c, h=h: e.tensor_scalar(out=vx[0:Lm, 256:384], in0=ones_f1[0:Lm, :], scalar1=etok[0:Lm, c, h:h + 1],
                                                                      scalar2=None, op0=ALU.mult), r=[ket, "cst_f"], w=[kvx])
                pg, kpg = self.next_ps()
                S.pe(lambda e, pg=pg, c0=c0, Lm=Lm, L=L: e.matmul(pg[0:Lm, 0:L], lhsT=kT[:, c0:c0 + Lm], rhs=qT[:, c0:c0 + L], start=True, stop=True),
                     r=[kkT, kqT], w=[kpg])
                S.dve(lambda e, Gm=Gm, pg=pg, Lm=Lm, L=L: e.tensor_tensor(out=Gm[0:Lm, 0:L], in0=pg[0:Lm, 0:L], in1=self.cmask2[0:Lm, 0:L], op=ALU.mult),
                      r=[kpg, "cmask2"], w=[kGm])

            def stage2(c0, L, c, h=h):
                sample = (c == 16)
                Lm = 128 if not sample else 32
                has_state = sample or c > 0
                ktok, kkt = ktokb[c % 2]
                vx, kvx = vxb[c % 2]
                Gm, kGm = Gmb[c % 2]
                rd, krd = rden[c % 2]
                if sample:
                    S.dma("sp", lambda e, h=h: e.dma_start(out=self.CT_o[l, 0, h], in_=Sf[:, 0:256]), r=[kSf])
                    S.dma("sp", lambda e, h=h: e.dma_start(out=self.n_o[l, 0, h], in_=Sf[:, 256:257]), r=[kSf])
                    S.dma("sp", lambda e, h=h: e.dma_start(out=Sf[:, 0:256], in_=self.C0T[l, h]), w=[kSf])
                    S.dma("sp", lambda e, h=h: e.dma_start(out=n0t[:, 0:1], in_=self.n0[l, h]), w=[kn0])
                    S.dve(lambda e: e.tensor_scalar(out=Sf[:, 256:384], in0=ones_f1, scalar1=n0t[:, 0:1], scalar2=None, op0=ALU.mult),
                          r=[kn0, "cst_f", kSf], w=[kSf])
                    S.act(lambda e, h=h: e.activation(out=Sb, in_=Sf, func=AF.Identity, scale=decbc[:, h, 16:17]), r=[kSf, kdb], w=[kSb])
                pn, kpn = self.next_ps()
                for j in range(3):
                    S.pe(lambda e, pn=pn, j=j, vx=vx, Gm=Gm, Lm=Lm, L=L, hs=has_state: e.matmul(
                        pn[:, j * 128:j * 128 + L], lhsT=vx[0:Lm, j * 128:(j + 1) * 128], rhs=Gm[0:Lm, 0:L], start=True, stop=(not hs)),
                        r=[kvx, kGm], w=[kpn])
                    if has_state:
                        S.pe(lambda e, pn=pn, j=j, c0=c0, L=L: e.matmul(
                            pn[:, j * 128:j * 128 + L], lhsT=Sb[:, j * 128:(j + 1) * 128], rhs=qT[:, c0:c0 + L], start=False, stop=True),
                            r=[kSb, kqT], w=[kpn])
                S.act(lambda e, rd=rd, pn=pn, L=L: e.activation(out=rd[:, 0:L], in_=pn[:, 256:256 + L], func=AF.Abs), r=[kpn], w=[krd])
                S.dve(lambda e, rd=rd, c0=c0, L=L: e.tensor_tensor(out=rd[:, 0:L], in0=rd[:, 0:L], in1=lb[:, c0:c0 + L], op=ALU.max),
                      r=[krd, klb], w=[krd])
                S.dve(lambda e, rd=rd, L=L: e.reciprocal(out=rd[:, 0:L], in_=rd[:, 0:L]), r=[krd], w=[krd])
                for j in range(2):
                    S.dve(lambda e, rd=rd, pn=pn, j=j, c0=c0, L=L: e.tensor_tensor(out=hT[:, j, c0:c0 + L], in0=pn[:, j * 128:j * 128 + L], in1=rd[:, 0:L], op=ALU.mult),
                          r=[kpn, krd], w=[(khT, c0 // 512)])
                pS, kpS = self.next_ps()
                S.pe(lambda e, pS=pS, ktok=ktok, vx=vx, Lm=Lm: e.matmul(pS[:, 0:384], lhsT=ktok[0:Lm, :], rhs=vx[0:Lm, 0:384], start=True, stop=True),
                     r=[kkt, kvx], w=[kpS])
                if not has_state:
                    S.act(lambda e, pS=pS: e.copy(out=Sf, in_=pS[:, 0:384]), r=[kpS], w=[kSf])
                else:
                    S.dve(lambda e, pS=pS, h=h, c=c: e.scalar_tensor_tensor(out=Sf, in0=Sf, scalar=decbc[:, h, c:c + 1], in1=pS[:, 0:384],
                                                                            op0=ALU.mult, op1=ALU.add), r=[kpS, kSf, kdb], w=[kSf])
                if c < NCH - 1:
                    S.act(lambda e, h=h, c=c: e.activation(out=Sb, in_=Sf, func=AF.Identity, scale=decbc[:, h, c + 1:c + 2]), r=[kSf, kdb], w=[kSb])
                if sample:
                    S.dma("sp", lambda e, h=h: e.dma_start(out=self.CT_o[l, 1, h], in_=Sf[:, 0:256]), r=[kSf])
                    S.dma("sp", lambda e, h=h: e.dma_start(out=self.n_o[l, 1, h], in_=Sf[:, 256:257]), r=[kSf])

            stage1(*chunks[0])
            for ci in range(len(chunks)):
                if ci + 1 < len(chunks):
                    stage1(*chunks[ci + 1])
                stage2(*chunks[ci])
                if pending is not None:
                    for _ in range(5):
                        next(pending, None)
            if pending is not None:
                for _ in pending:
                    pass
                pending = None
            for ti, (t0, n) in enumerate(tiles if not self.dbg.get("nonorm") else []):
                S.act(lambda e, t0=t0, n=n: e.activation(out=sqh[:, :, :n], in_=hT[:, :, t0:t0 + n], func=AF.Square), r=[(khT, ti)], w=[ksqh])
                pss, kps = self.next_ps()
                for j in range(2):
                    S.pe(lambda e, pss=pss, j=j, n=n: e.matmul(pss[:, :n], lhsT=ones256, rhs=sqh[:, j, :n], start=(j == 0), stop=(j == 1)),
                         r=[ksqh, "ones_b"], w=[kps])
                S.act(lambda e, pss=pss, n=n: e.activation(out=rsb[:, :n], in_=pss[:, :n], func=AF.Sqrt, bias=self.epsc[:, 0:1], scale=1.0),
                      r=[kps, "epsc"], w=[krsb])
                S.dve(lambda e, n=n: e.reciprocal(out=rsb[:, :n], in_=rsb[:, :n]), r=[krsb], w=[krsb])
                for j in range(2):
                    pog, kpog = self.next_ps()
                    for k in range(KC):
                        S.pe(lambda e, pog=pog, k=k, j=j, t0=t0, n=n, w1=w1: e.matmul(
                            pog[:, :n], lhsT=w1[:, k, 256 + j * 128:384 + j * 128], rhs=self.xn[:, k, t0:t0 + n],
                            start=(k == 0), stop=(k == KC - 1)), r=[kw1[2], ("xn", k, ti)], w=[kpog])
                    S.act(lambda e, pog=pog, n=n: e.activation(out=sgb[:, :n], in_=pog[:, :n], func=AF.Sigmoid), r=[kpog], w=[ksgb])
                    S.dve(lambda e, j=j, t0=t0, n=n, h=h: e.scalar_tensor_tensor(out=tmpb[:, :n], in0=hT[:, j, t0:t0 + n], scalar=self.gh_sb[:, l, 2 * h + j:2 * h + j + 1],
                                                                                in1=rsb[:, :n], op0=ALU.mult, op1=ALU.mult), r=[(khT, ti), krsb, "gh"], w=[ktmp])
                    S.dve(lambda e, j=j, t0=t0, n=n, hn=hn: e.tensor_tensor(out=hn[:, j, t0:t0 + n], in0=tmpb[:, :n], in1=sgb[:, :n], op=ALU.mult),
                          r=[ktmp, ksgb], w=[(khn, ti)])
            if self.dbg.get("noproj"):
                continue
            last = (h == nheads - 1)
            pending = self.proj_rmw_gen(wout, 2 * h, 2, lambda c, t0, n, ti, hn=hn, khn=khn: (hn[:, c, t0:t0 + n], (khn, ti)), pw_bufs,
                                        final=(self.last_phase and last))
            if last:
                for _ in pending:
                    pass
                pending = None


def _consts(T):
    TA = T + TS
    half = 64
    inv_freq = (10000.0 ** (-np.arange(half, dtype=np.float32) / half)).astype(np.float32)
    pos = np.concatenate([np.arange(T), PAST_LEN + np.arange(TS)]).astype(np.float32)
    ang = (pos[None, :] * np.tile(inv_freq, 2)[:, None]).astype(np.float32)
    rope = np.stack([np.cos(ang), np.sin(ang)]).astype(np.float32)
    rmat = np.zeros((128, 128), np.float32)
    for p in range(64):
        rmat[p + 64, p] = -1.0
        rmat[p, p + 64] = 1.0
    cm = np.zeros((128, NCM), np.float32)
    i = np.arange(128)[:, None]
    j = np.arange(256)[None, :]
    cm[:, 0:256] = ((j - i >= 0) & (j - i <= 128)).astype(np.float32)
    t = np.arange(4)[None, :]
    cm[:, 384:388] = (i >= t).astype(np.float32)
    tp = np.arange(4)[:, None]
    cm[0:4, 388:392] = (tp <= t).astype(np.float32)
    cm[0:4, 392:396] = (tp == t).astype(np.float32)
    for bi in range(4):
        cm[:, 396 + 4 * bi + bi] = 1.0
    cst = np.stack([np.full((128, 128), 1.0 / 2048), np.full((128, 128), 1.0 / 128), np.ones((128, 128)), np.full((128, 128), 1.0 / 256)], 1).astype(np.float32)
    sel = np.zeros((32, 8, 128), np.float32)
    for h in range(8):
        sel[h, h, :] = 1.0
    cm2 = (np.arange(128)[:, None] <= np.arange(128)[None, :]).astype(np.float32)
    return dict(rope=rope, rmat=rmat, cmask=cm, cst=cst, ident=np.eye(128, dtype=np.float32), sel=sel, cmask2=cm2)


def prep_shared(inp, T=2048):
    f = lambda a: np.ascontiguousarray(np.asarray(a, dtype=np.float32))
    d = _consts(T)
    nm, nf = f(inp["norm_mix"]), f(inp["norm_ffn"])
    d["nrm"] = f(np.concatenate([nm, nf], 0).reshape(8, 16, 128).transpose(2, 0, 1))
    d["w_up"] = f(inp["ffn_w_up"])
    d["w_down"] = f(inp["ffn_w_down"])
    d["convw"] = f(f(inp["ffn_conv_w"]).reshape(4, 3, 88, 128).transpose(0, 3, 2, 1))
    d["convb"] = f(f(inp["ffn_conv_b"]).reshape(4, 88, 128).transpose(0, 2, 1))
    d["w_qkv"] = f(inp["attn_w_qkv"])
    d["w_o"] = f(inp["attn_w_o"])
    d["qkg"] = f(np.stack([f(inp["attn_q_norm"]), f(inp["attn_k_norm"])], -1).transpose(1, 0, 2))
    d["w_in"] = f(inp["mlstm_w_in"])
    d["w_out"] = f(inp["mlstm_w_out"])
    bgs = f(inp["mlstm_b_gates"])
    bgp = np.zeros((32, 2, 2), np.float32)
    bgp[:8] = np.stack([bgs[:, :8], bgs[:, 8:]], -1).transpose(1, 0, 2)
    d["bg"] = bgp
    d["gh"] = f(f(inp["mlstm_norm_h"]).reshape(2, 16, 128).transpose(2, 0, 1))
    return d


def prep_core(inp, c, T=2048):
    f = lambda a: np.ascontiguousarray(np.asarray(a, dtype=np.float32))
    d = {}
    xp = np.asarray(inp["x_prompt"])[c % 4, :T]
    xs = np.asarray(inp["x_sample"])[c]
    d["xT"] = f(np.concatenate([xp.T, xs.T], 1))
    d["sconv"] = f(f(inp["state_ffn_conv"])[:, c].reshape(4, 2, 88, 128).transpose(0, 3, 2, 1))
    for g, nm in enumerate(("cache_kv_w128", "cache_kv_w512", "cache_kv_w2048")):
        d["cache%d" % g] = f(np.asarray(inp[nm])[:, c])
    d["C0T"] = f(np.asarray(inp["state_mlstm_C"])[:, c].transpose(0, 1, 3, 2))
    d["n0"] = f(np.asarray(inp["state_mlstm_n"])[:, c][..., None])
    m0p = np.zeros((32, 2), np.float32)
    m0p[:8] = np.asarray(inp["state_mlstm_m"])[:, c].T
    d["m0"] = m0p
    return d


_CACHE = {}


def kernel(**inputs):
    T = 2048
    if "prog" not in _CACHE:
        P = Prog(T=T)
        P.build()
        _CACHE["prog"] = P
    P = _CACHE["prog"]
    shared = prep_shared(inputs, T)
    in_maps = []
    for c in range(8):
        d = dict(shared)
        d.update(prep_core(inputs, c, T))
        in_maps.append({k: d[k] for k in P.ins})
    res = run_bass_kernel_spmd(P.nc, in_maps, core_ids=list(range(8)))
    R = res.results
    return assemble(R, T)


def assemble(R, T=2048):
    f32 = np.float32
    yp = np.stack([R[b]["yT"][:, :T].T for b in range(4)]).astype(f32)
    ys = np.stack([R[c]["yT"][:, T:].T for c in range(8)]).astype(f32)
    outs = [yp, ys]
    if "kT_o0" not in R[0]:
        z = lambda *sh: np.zeros(sh, f32)
        for g in range(3):
            outs += [z(NLA, 4, min(NBUF[g], T), 2, 8, 128), z(NLA, 8, NBUF[g], 2, 8, 128)]
        outs += [z(NLB, 4, 8, 256, 128), z(NLB, 8, 8, 256, 128), z(NLB, 4, 8, 128), z(NLB, 8, 8, 128), z(NLB, 4, 8), z(NLB, 8, 8)]
        cv = lambda a: a.transpose(0, 3, 2, 1).reshape(NL, 2, 2 * FF)
        outs += [np.stack([cv(R[b]["convp_o"]) for b in range(4)], 1).astype(f32),
                 np.stack([cv(R[c]["convs_o"]) for c in range(8)], 1).astype(f32)]
        return tuple(outs)
    for g in range(3):
        dil, nb = DILS[g], NBUF[g]
        nkeep = min(nb, T)
        kvp = np.zeros((NLA, 4, nkeep, 2, 8, 128), f32)
        for b in range(4):
            kT = R[b]["kT_o%d" % g]
            kvp[:, b, :, 0] = kT.transpose(0, 3, 1, 2)
            v = R[b]["v_o%d" % g]
            kvp[:, b, :, 1] = v.transpose(0, 3, 2, 1, 4).reshape(NLA, nkeep, 8, 128)
        kvs = np.stack([R[c]["kvs_o%d" % g] for c in range(8)], 1).astype(f32)
        outs += [kvp, kvs]
    CT = [np.stack([R[b]["CT_o"][:, 0] for b in range(4)], 1), np.stack([R[c]["CT_o"][:, 1] for c in range(8)], 1)]
    outs += [np.ascontiguousarray(x.transpose(0, 1, 2, 4, 3)).astype(f32) for x in CT]
    outs += [np.stack([R[b]["n_o"][:, 0, :, :, 0] for b in range(4)], 1).astype(f32),
             np.stack([R[c]["n_o"][:, 1, :, :, 0] for c in range(8)], 1).astype(f32)]
    outs += [np.stack([R[b]["m_o"][:, 0, :, 0] for b in range(4)], 1).astype(f32),
             np.stack([R[c]["m_o"][:, 1, :, 0] for c in range(8)], 1).astype(f32)]
    cv = lambda a: a.transpose(0, 3, 2, 1).reshape(NL, 2, 2 * FF)
    outs += [np.stack([cv(R[b]["convp_o"]) for b in range(4)], 1).astype(f32),
             np.stack([cv(R[c]["convs_o"]) for c in range(8)], 1).astype(f32)]
    return tuple(outs)
```
